# Optimizing a Trainium2 kernel written in Bass

```python
import math
import jax, jax.numpy as jnp
from jax import lax
import numpy as np

D_MODEL = 2048
BATCH = 4
SEQ = 2048
DEPTH = 1
DEC_BATCH = 128
DEC_SEQ = 1
PAST_LEN = 16384
PAGE_SIZE = 128

D_MIX = D_MODEL
D_SSM = D_MIX // 2
D_POOL = D_MIX - D_SSM
SSM_GROUP = 16
N_SSM_GROUPS = D_SSM // SSM_GROUP
SSM_STATE = 64
POOL_WINDOWS = (2, 4, 8, 16)
N_POOL_GROUPS = len(POOL_WINDOWS)
POOL_GROUP = D_POOL // N_POOL_GROUPS
POOL_HIST = max(POOL_WINDOWS) - 1
D_FF = 5632
FFN_RES = 0.5
N_SUBLAYERS = 3
N_MOD = 3
EPS = 1e-6
DT_MIN = 1e-3
DT_MAX = 1e-1

kernel_name = "s5_pool_hybrid_adaln_decode_step"


def rmsnorm(x, g):
    xf = x.astype(jnp.float32)
    y = xf * lax.rsqrt(jnp.mean(xf * xf, axis=-1, keepdims=True) + EPS)
    return (y * g.astype(jnp.float32)).astype(x.dtype)


def modulate(h, shift, scale):
    return h * (1 + scale) + shift


def swiglu(h, w_gate, w_up, w_down):
    return (jax.nn.silu(h @ w_gate) * (h @ w_up)) @ w_down


def s5_discretise(lam_re, lam_im, log_dt, b_re, b_im):
    lr = jnp.minimum(lam_re.astype(jnp.float32), -1e-4)
    li = lam_im.astype(jnp.float32)
    dt = jnp.exp(log_dt.astype(jnp.float32))[:, None]
    mag = jnp.exp(lr * dt)
    ang = li * dt
    a_re = mag * jnp.cos(ang)
    a_im = mag * jnp.sin(ang)
    den = lr * lr + li * li
    num_re = a_re - 1.0
    f_re = (num_re * lr + a_im * li) / den
    f_im = (a_im * lr - num_re * li) / den
    br = b_re.astype(jnp.float32)
    bi = b_im.astype(jnp.float32)
    bbar_re = f_re[..., None] * br - f_im[..., None] * bi
    bbar_im = f_re[..., None] * bi + f_im[..., None] * br
    return a_re, a_im, bbar_re, bbar_im


def s5_combine(e1, e2):
    a1r, a1i, b1r, b1i = e1
    a2r, a2i, b2r, b2i = e2
    ar = a2r * a1r - a2i * a1i
    ai = a2r * a1i + a2i * a1r
    br = a2r * b1r - a2i * b1i + b2r
    bi = a2r * b1i + a2i * b1r + b2i
    return (ar, ai, br, bi)


def s5_mixer(u, s0_re, s0_im, lam_re, lam_im, log_dt, b_re, b_im, c_re, c_im, d_skip, glu_w, glu_b):
    a_re, a_im, bbar_re, bbar_im = s5_discretise(lam_re, lam_im, log_dt, b_re, b_im)
    bu_re = jnp.einsum('nlgh,gph->nlgp', u, bbar_re)
    bu_im = jnp.einsum('nlgh,gph->nlgp', u, bbar_im)
    ar = jnp.broadcast_to(a_re, bu_re.shape)
    ai = jnp.broadcast_to(a_im, bu_re.shape)
    pa_re, pa_im, sb_re, sb_im = lax.associative_scan(s5_combine, (ar, ai, bu_re, bu_im), axis=1)
    x0r = s0_re.astype(jnp.float32)[:, None]
    x0i = s0_im.astype(jnp.float32)[:, None]
    s_re = pa_re * x0r - pa_im * x0i + sb_re
    s_im = pa_re * x0i + pa_im * x0r + sb_im
    y = (jnp.einsum('nlgp,ghp->nlgh', s_re, c_re.astype(jnp.float32))
         - jnp.einsum('nlgp,ghp->nlgh', s_im, c_im.astype(jnp.float32))
         + d_skip.astype(jnp.float32) * u)
    gy = jax.nn.gelu(y)
    out = gy * jax.nn.sigmoid(jnp.einsum('nlgh,ghk->nlgk', gy, glu_w.astype(jnp.float32))
                              + glu_b.astype(jnp.float32))
    return out, s_re[:, -1], s_im[:, -1]


def pool_mixer(v_buf, start, pool_w, pool_b, pool_scale):
    n, lb, _ = v_buf.shape
    pos = start - POOL_HIST + jnp.arange(lb)
    vf = v_buf.astype(jnp.float32).reshape(n, lb, N_POOL_GROUPS, POOL_GROUP)
    cs = jnp.cumsum(vf, axis=1)
    outs = []
    for gi, w in enumerate(POOL_WINDOWS):
        csg = cs[:, :, gi]
        prev = jnp.pad(csg, ((0, 0), (w, 0), (0, 0)))[:, :lb]
        cnt = jnp.clip(pos + 1, 1, w).astype(jnp.float32)[None, :, None]
        outs.append((csg - prev) / cnt - vf[:, :, gi])
    z = jnp.stack(outs, axis=2)[:, POOL_HIST:]
    z = jnp.einsum('nlgc,gcd->nlgd', z, pool_w.astype(jnp.float32)) + pool_b.astype(jnp.float32)
    return z.reshape(n, lb - POOL_HIST, D_POOL) * pool_scale.astype(jnp.float32)


def decoder_layer(x, c, ssm_re0, ssm_im0, pool_past, start, ada_w, ada_b,
                  ffn1_norm, ffn1_w_gate, ffn1_w_up, ffn1_w_down, mix_norm, w_in,
                  lam_re, lam_im, log_dt, b_re, b_im, c_re, c_im, d_skip, glu_w, glu_b,
                  pool_w, pool_b, pool_scale, w_out,
                  ffn2_norm, ffn2_w_gate, ffn2_w_up, ffn2_w_down):
    n, l, _ = x.shape
    mod = (jax.nn.silu(c) @ ada_w + ada_b).reshape(n, N_SUBLAYERS, N_MOD, 1, D_MODEL)
    h = modulate(rmsnorm(x, ffn1_norm), mod[:, 0, 0], mod[:, 0, 1])
    x = x + FFN_RES * mod[:, 0, 2] * swiglu(h, ffn1_w_gate, ffn1_w_up, ffn1_w_down)
    h = modulate(rmsnorm(x, mix_norm), mod[:, 1, 0], mod[:, 1, 1])
    proj = h @ w_in
    u = proj[..., :D_SSM].astype(jnp.float32).reshape(n, l, N_SSM_GROUPS, SSM_GROUP)
    v = proj[..., D_SSM:]
    y_ssm, new_re, new_im = s5_mixer(u, ssm_re0, ssm_im0, lam_re, lam_im, log_dt,
                                     b_re, b_im, c_re, c_im, d_skip, glu_w, glu_b)
    pool_buf = jnp.concatenate([pool_past.astype(v.dtype), v], axis=1)
    y_pool = pool_mixer(pool_buf, start, pool_w, pool_b, pool_scale)
    mixed = jnp.concatenate([y_ssm.reshape(n, l, D_SSM), y_pool], axis=-1).astype(x.dtype)
    x = x + mod[:, 1, 2] * (mixed @ w_out)
    h = modulate(rmsnorm(x, ffn2_norm), mod[:, 2, 0], mod[:, 2, 1])
    x = x + FFN_RES * mod[:, 2, 2] * swiglu(h, ffn2_w_gate, ffn2_w_up, ffn2_w_down)
    return x, new_re, new_im, pool_buf[:, -POOL_HIST:]


def setup_inputs(seed: int = 0) -> dict:
    key = jax.random.key(seed)
    ks = iter(jax.random.split(key, 48))
    f32 = jnp.float32

    def nrm(shape, s):
        return jax.random.normal(next(ks), shape, f32) * s

    L, G, P, H = DEPTH, N_SSM_GROUPS, SSM_STATE, SSM_GROUP
    inp = {}
    inp['x_prompt'] = nrm((BATCH, SEQ, D_MODEL), 1.0)
    inp['x_sample'] = nrm((DEC_BATCH, DEC_SEQ, D_MODEL), 1.0)
    inp['state_ssm_re'] = nrm((L, DEC_BATCH, G, P), 0.3)
    inp['state_ssm_im'] = nrm((L, DEC_BATCH, G, P), 0.3)
    inp['state_pool'] = nrm((L, DEC_BATCH, POOL_HIST, D_POOL), 1.0)
    inp['c_prompt'] = nrm((BATCH, D_MODEL), 1.0)
    inp['c_sample'] = nrm((DEC_BATCH, D_MODEL), 1.0)
    inp['ada_w'] = nrm((L, D_MODEL, N_SUBLAYERS * N_MOD * D_MODEL), 0.5 * D_MODEL ** -0.5)
    inp['ada_b'] = nrm((L, N_SUBLAYERS * N_MOD * D_MODEL), 0.02)
    inp['ffn1_norm'] = 1.0 + nrm((L, D_MODEL), 0.02)
    inp['ffn1_w_gate'] = nrm((L, D_MODEL, D_FF), D_MODEL ** -0.5)
    inp['ffn1_w_up'] = nrm((L, D_MODEL, D_FF), D_MODEL ** -0.5)
    inp['ffn1_w_down'] = nrm((L, D_FF, D_MODEL), D_FF ** -0.5)
    inp['mix_norm'] = 1.0 + nrm((L, D_MODEL), 0.02)
    inp['w_in'] = nrm((L, D_MODEL, D_MIX), D_MODEL ** -0.5)
    inp['ssm_lambda_re'] = -0.5 + nrm((L, G, P), 0.01)
    inp['ssm_lambda_im'] = jnp.pi * jnp.arange(P, dtype=f32) + nrm((L, G, P), 0.01)
    inp['ssm_log_dt'] = jax.random.uniform(next(ks), (L, G), f32,
                                           minval=math.log(DT_MIN), maxval=math.log(DT_MAX))
    inp['ssm_b_re'] = nrm((L, G, P, H), (2 * H) ** -0.5)
    inp['ssm_b_im'] = nrm((L, G, P, H), (2 * H) ** -0.5)
    inp['ssm_c_re'] = nrm((L, G, H, P), (2 * P) ** -0.5)
    inp['ssm_c_im'] = nrm((L, G, H, P), (2 * P) ** -0.5)
    inp['ssm_d'] = nrm((L, G, H), 0.5)
    inp['ssm_glu_w'] = nrm((L, G, H, H), H ** -0.5)
    inp['ssm_glu_b'] = nrm((L, G, H), 0.02)
    inp['pool_w'] = nrm((L, N_POOL_GROUPS, POOL_GROUP, POOL_GROUP), POOL_GROUP ** -0.5)
    inp['pool_b'] = nrm((L, N_POOL_GROUPS, POOL_GROUP), 0.02)
    inp['pool_scale'] = 1.0 + nrm((L, D_POOL), 0.02)
    inp['w_out'] = nrm((L, D_MIX, D_MODEL), D_MIX ** -0.5)
    inp['ffn2_norm'] = 1.0 + nrm((L, D_MODEL), 0.02)
    inp['ffn2_w_gate'] = nrm((L, D_MODEL, D_FF), D_MODEL ** -0.5)
    inp['ffn2_w_up'] = nrm((L, D_MODEL, D_FF), D_MODEL ** -0.5)
    inp['ffn2_w_down'] = nrm((L, D_FF, D_MODEL), D_FF ** -0.5)
    inp['final_norm'] = 1.0 + nrm((D_MODEL,), 0.02)
    return inp


def reference(x_prompt, x_sample, state_ssm_re, state_ssm_im, state_pool, c_prompt, c_sample,
              ada_w, ada_b, ffn1_norm, ffn1_w_gate, ffn1_w_up, ffn1_w_down, mix_norm, w_in,
              ssm_lambda_re, ssm_lambda_im, ssm_log_dt, ssm_b_re, ssm_b_im, ssm_c_re, ssm_c_im,
              ssm_d, ssm_glu_w, ssm_glu_b, pool_w, pool_b, pool_scale, w_out,
              ffn2_norm, ffn2_w_gate, ffn2_w_up, ffn2_w_down, final_norm):

    def trunk(x, c, ssm_re0, ssm_im0, pool_past, start):
        new_re, new_im, new_pool = [], [], []
        for l in range(DEPTH):
            x, r, i, p = decoder_layer(
                x, c, ssm_re0[l], ssm_im0[l], pool_past[l], start, ada_w[l], ada_b[l],
                ffn1_norm[l], ffn1_w_gate[l], ffn1_w_up[l], ffn1_w_down[l], mix_norm[l], w_in[l],
                ssm_lambda_re[l], ssm_lambda_im[l], ssm_log_dt[l], ssm_b_re[l], ssm_b_im[l],
                ssm_c_re[l], ssm_c_im[l], ssm_d[l], ssm_glu_w[l], ssm_glu_b[l],
                pool_w[l], pool_b[l], pool_scale[l], w_out[l],
                ffn2_norm[l], ffn2_w_gate[l], ffn2_w_up[l], ffn2_w_down[l])
            new_re.append(r)
            new_im.append(i)
            new_pool.append(p)
        return (rmsnorm(x, final_norm), jnp.stack(new_re), jnp.stack(new_im), jnp.stack(new_pool))

    nb = x_prompt.shape[0]
    zero_ssm = jnp.zeros((DEPTH, nb, N_SSM_GROUPS, SSM_STATE), jnp.float32)
    zero_pool = jnp.zeros((DEPTH, nb, POOL_HIST, D_POOL), x_prompt.dtype)
    y_prompt, ssm_re_p, ssm_im_p, pool_p = trunk(x_prompt, c_prompt, zero_ssm, zero_ssm, zero_pool, 0)
    y_sample, ssm_re_s, ssm_im_s, pool_s = trunk(x_sample, c_sample, state_ssm_re, state_ssm_im,
                                                 state_pool, PAST_LEN)
    return (y_prompt, y_sample, ssm_re_p, ssm_im_p, pool_p, ssm_re_s, ssm_im_s, pool_s)
```

```python
import contextlib
import numpy as np
import concourse.bass as bass
import concourse.mybir as mybir
from concourse.bass_utils import run_bass_kernel_spmd

F32 = mybir.dt.float32
BF16 = mybir.dt.bfloat16
I32 = mybir.dt.int32
ACT = mybir.ActivationFunctionType
ALU = mybir.AluOpType

D = 2048
KC = 16
FF = 5632
FC = 44
NPT = 1024
NS = 16
NT = NPT + NS
TILES = [(0, 512), (512, 512), (1024, 16)]
NPART = 4
FPP = FC // NPART
EPS = 1e-6
NCORES = 8


class Sched:
    ENGS = ("pe", "act", "dve", "pool", "sp")

    def __init__(self, nc, es, n_dma_sems=12):
        self.nc = nc
        self.es = es
        self.ncc = 0
        self.eng = {"pe": nc.tensor, "act": nc.scalar, "dve": nc.vector,
                    "pool": nc.gpsimd, "sp": nc.sync}
        self.sem = {e: es.enter_context(nc.semaphore("s_" + e)) for e in self.ENGS}
        self.cnt = {e: 0 for e in self.ENGS}
        self.dsem = {}
        for q in ("sp", "pool", "act"):
            n = n_dma_sems if q != "act" else 4
            self.dsem[q] = [[es.enter_context(nc.semaphore("d_%s%d" % (q, i))), 0]
                            for i in range(n)]
        self.dnext = {q: 0 for q in self.dsem}
        self.waited = {e: {} for e in self.ENGS}
        self.last_w = {}
        self.readers = {}
        self.all_dma = []
        self.last_ev = {e: None for e in self.ENGS}

    def _deps(self, r, w):
        deps = []
        for k in r:
            ev = self.last_w.get(k)
            if ev is not None:
                deps.append(ev)
        for k in w:
            ev = self.last_w.get(k)
            if ev is not None:
                deps.append(ev)
            deps.extend(self.readers.get(k, ()))
        return deps

    def _emit_waits(self, e, deps):
        best = {}
        for ev in deps:
            if ev is None:
                continue
            sem, val, src, sid = ev
            if src == "pe" and e == "pe":
                continue
            if sid not in best or best[sid][1] < val:
                best[sid] = ev
        for sid, (sem, val, src, _) in best.items():
            if src is not None and val > self.cnt[src]:
                raise RuntimeError("wait on future event %s %d>%d" % (src, val, self.cnt[src]))
            if self.waited[e].get(sid, 0) >= val:
                continue
            self.eng[e].wait_ge(sem, val)
            self.waited[e][sid] = val

    def _record(self, ev, r, w):
        for k in r:
            self.readers.setdefault(k, []).append(ev)
        for k in w:
            self.last_w[k] = ev
            self.readers[k] = []

    def op(self, e, fn, r=(), w=(), sig=True, extra=(), skip_same=False):
        deps = self._deps(r, w) + list(extra)
        if skip_same:
            deps = [d for d in deps if d is not None and d[2] != e]
        self._emit_waits(e, deps)
        ins = fn()
        if sig:
            self.cnt[e] += 1
            ins.then_inc(self.sem[e], 1)
            ev = (self.sem[e], self.cnt[e], e, "c_" + e)
        else:
            ev = (self.sem[e], self.cnt[e] + 1, e, "c_" + e)
        self.last_ev[e] = ev
        self._record(ev, r, w)
        return ev

    def dma(self, q, fn, r=(), w=(), extra=(), inc=16):
        deps = self._deps(r, w) + list(extra)
        ring = self.dsem[q]
        i = self.dnext[q]
        self.dnext[q] = (i + 1) % len(ring)
        slot = ring[i]
        sid = "d_%s%d" % (q, i)
        if slot[1] > 0:
            deps.append((slot[0], slot[1], None, sid))
        self._emit_waits(q, deps)
        ins = fn()
        slot[1] += inc
        ins.then_inc(slot[0], inc)
        ev = (slot[0], slot[1], None, sid)
        self.all_dma.append(ev)
        self._record(ev, r, w)
        return ev

    def cc(self, fn, r=(), w=()):
        self.barrier_keep()
        deps = self._deps(r, w)
        self._emit_waits("pool", deps)
        sem = self.es.enter_context(self.nc.semaphore("cc%d" % self.ncc))
        sid = "cc%d" % self.ncc
        self.ncc += 1
        ins = fn()
        ins.then_inc(sem)
        ev = (sem, 1, None, sid)
        self.all_dma.append(ev)
        self._record(ev, r, w)
        self.barrier_keep()
        return ev

    def barrier_keep(self):
        evs = [self.last_ev[e] for e in self.ENGS if self.last_ev[e] is not None]
        evs += self.all_dma
        for e in self.ENGS:
            self._emit_waits(e, [ev for ev in evs if not (ev[2] == e == "pe")])

    def barrier(self):
        evs = [self.last_ev[e] for e in self.ENGS if self.last_ev[e] is not None]
        evs += self.all_dma
        for e in self.ENGS:
            self._emit_waits(e, [ev for ev in evs if not (ev[2] == e == "pe")])
        self.all_dma = []
        self.last_w = {}
        self.readers = {}

    def finish(self):
        self._emit_waits("sp", self.all_dma)


WINS = (2, 4, 8, 16)
TWO_PI = 6.283185307179586


def build_program(debug=None, with_s5=True, s5main=True):
    nc = bass.Bass("TRN2", target_bir_lowering=False)

    def din(name, shape, dt=F32):
        return nc.dram_tensor(name, list(shape), dt, kind="ExternalInput").ap()

    def dout(name, shape, dt=F32):
        return nc.dram_tensor(name, list(shape), dt, kind="ExternalOutput").ap()

    def dint(name, shape, dt=F32):
        return nc.dram_tensor(name, list(shape), dt).ap()

    xp = din("xp", [NPT, D])
    xs = din("xs", [NS, D])
    call = din("call", [132, D])
    adaA = din("adaA", [D, 512])
    adaB = din("adaB", [D, 1792])
    ada_b = din("ada_b", [9 * D // 128, 128])
    sel1 = din("sel1", [128, 17])
    sel2 = din("sel2", [4, 17])
    flags = din("flags", [128, 2])
    norms = din("norms", [4 * KC, 128])
    w_g = [din("w_g%d" % i, [D, FF]) for i in range(2)]
    w_u = [din("w_u%d" % i, [D, FF]) for i in range(2)]
    w_d = [din("w_d%d" % i, [FF, D]) for i in range(2)]
    w_in = din("w_in", [D, D])
    w_out = din("w_out", [D, D])
    pool_w = din("pool_w", [4, 256, 256])
    pvec = din("pvec", [16, 128])
    spool = din("spool", [NS, 15, 1024])
    lamA_d = din("lamA", [128, 3, 32])
    BA_d = din("BA", [128, 2, 32, 16])
    CA_d = din("CA", [128, 2, 32, 16])
    Dcol_d = din("Dcol", [128, 64])
    gluw = din("gluw", [64, 16, 16])
    glub = din("glub", [8, 128])
    s0A_d = din("s0A", [128, 2, 32, NS])

    yp = dout("yp", [NPT, D])
    ys = dout("ys", [NS, D])
    nssm_p = dout("nssm_p", [128, 2, 32])
    npool_p = dout("npool_p", [16, 1024])
    nssm_s = dout("nssm_s", [128, 2, 32, NS])
    npool_s = dout("npool_s", [NS, 15, 1024])

    ag1_in = dint("ag1_in", [132, 512]); ag1_out = dint("ag1_out", [8 * 132, 512])
    ag2_in = dint("ag2_in", [132, 1792]); ag2_out = dint("ag2_out", [8 * 132, 1792])
    agv_in = dint("agv_in", [16, 1024]); agv_out = dint("agv_out", [32, 1024])
    ags_in = dint("ags_in", [128, 64]); ags_out = dint("ags_out", [256, 64])
    st_w1 = dint("st_w1", [2, 128, 64 * 64], BF16)
    st_toep = dint("st_toep", [128, 64 * 128], BF16)
    st_w2 = dint("st_w2", [2, 128, 32 * 128], BF16)

    PAIRS = [[0, 1], [2, 3], [4, 5], [6, 7]]
    ALL8 = [list(range(8))]

    es = contextlib.ExitStack()
    with es:
        S = Sched(nc, es)
        uid = [0]

        def sb(st, name, shape, dt=F32):
            uid[0] += 1
            return st.enter_context(nc.sbuf_tensor("%s_%d" % (name, uid[0]), list(shape), dt))

        ps = [es.enter_context(nc.psum_tensor("ps%d" % i, [128, 512], F32)) for i in range(8)]

        xT = sb(es, "xT", [128, KC, NT])
        ident = sb(es, "ident", [128, 128])
        identb = sb(es, "identb", [128, 128], BF16)
        onesb = sb(es, "onesb", [128, 128], BF16)
        iot = sb(es, "iot", [128, 128], I32)
        adab = sb(es, "adab", [128, 9 * KC])
        normw = sb(es, "normw", [128, 4 * KC])
        flg = sb(es, "flg", [128, 2])
        gsP = sb(es, "gsP", [128, 3, KC])
        shP = sb(es, "shP", [128, 3, KC])
        gtP = sb(es, "gtP", [128, 3, KC])
        gsS = sb(es, "gsS", [128, 3, KC, NS])
        shS = sb(es, "shS", [128, 3, KC, NS])
        gtS = sb(es, "gtS", [128, 3, KC, NS])
        epsb = sb(es, "epsb", [128, 1])
        pvT = sb(es, "pvT", [128, 16])
        Fw = sb(es, "Fw", [128, 4, 8, 2])
        a_re = sb(es, "a_re", [128, 32]); a_im = sb(es, "a_im", [128, 32])
        al_re = sb(es, "al_re", [128, 32]); al_im = sb(es, "al_im", [128, 32])
        glubT = sb(es, "glubT", [128, 8])

        S.op("pool", lambda: nc.gpsimd.iota(iot[:], pattern=[[1, 128]], base=0, channel_multiplier=-1), w=["iot"])
        S.op("dve", lambda: nc.vector.tensor_single_scalar(out=ident[:], in_=iot[:], scalar=0, op=ALU.is_equal),
             r=["iot"], w=["ident"])
        S.op("dve", lambda: nc.vector.tensor_copy(out=identb[:], in_=ident[:]), r=["ident"], w=["identb"])
        S.op("dve", lambda: nc.vector.memset(onesb[:], 1.0), w=["onesb"])
        S.op("dve", lambda: nc.vector.memset(epsb[:], EPS), w=["epsb"])
        S.dma("sp", lambda: nc.sync.dma_start(out=flg[:, :], in_=flags), w=["flg"])

        psrr = [0]

        def misc_bank():
            b = 6 + (psrr[0] % 2)
            psrr[0] += 1
            return b

        def evac(i, dst, src, r, w):
            if i % 2 == 0:
                S.op("dve", lambda: nc.vector.tensor_copy(out=dst, in_=src), r=r, w=w)
            else:
                S.op("act", lambda: nc.scalar.copy(out=dst, in_=src), r=r, w=w)

        def rows_to_featmajor(st, src_rows, nrows, dst, dkey, tag):
            stg = sb(st, "stg" + tag, [128, 128])
            S.dma("sp", lambda: nc.sync.dma_start(out=stg[:nrows, :], in_=src_rows), w=["stg" + tag])
            b = misc_bank()
            S.op("pe", lambda: nc.tensor.transpose(out=ps[b][:, 0:nrows], in_=stg[:nrows, :],
                                                   identity=ident[:nrows, :nrows]),
                 r=["stg" + tag, "ident"], w=[("ps", b)])
            S.op("dve", lambda: nc.vector.tensor_copy(out=dst, in_=ps[b][:, 0:nrows]), r=[("ps", b)], w=[dkey])

        modscope = contextlib.ExitStack()
        es.enter_context(modscope)
        modT = sb(modscope, "modT", [128, 9 * KC, 17])

        def derive(sub, kinds):
            base = sub * 3 * KC
            sc = modT[:, base + KC:base + 2 * KC, :]
            sh = modT[:, base:base + KC, :]
            gt = modT[:, base + 2 * KC:base + 3 * KC, :]
            nw = normw[:, sub * KC:(sub + 1) * KC]
            if "ss" in kinds:
                S.op("dve", lambda: nc.vector.scalar_tensor_tensor(
                    out=gsS[:, sub, :, :], in0=sc[:, :, 1:17], scalar=1.0,
                    in1=nw.unsqueeze(2).to_broadcast([128, KC, NS]), op0=ALU.add, op1=ALU.mult),
                    r=["modT", "normw"], w=["gsS"])
                S.op("dve", lambda: nc.vector.scalar_tensor_tensor(
                    out=gsP[:, sub, :], in0=sc[:, :, 0], scalar=1.0, in1=nw, op0=ALU.add, op1=ALU.mult),
                    r=["modT", "normw"], w=["gsP"])
                S.op("dve", lambda: nc.vector.tensor_copy(out=shS[:, sub, :, :], in_=sh[:, :, 1:17]),
                     r=["modT"], w=["shS"])
                S.op("dve", lambda: nc.vector.tensor_copy(out=shP[:, sub, :], in_=sh[:, :, 0]),
                     r=["modT"], w=["shP"])
            if "g" in kinds:
                gm = 1.0 if sub == 1 else 0.5
                S.op("dve", lambda: nc.vector.tensor_scalar(
                    out=gtS[:, sub, :, :], in0=gt[:, :, 1:17], scalar1=gm, scalar2=None, op0=ALU.mult),
                    r=["modT"], w=["gtS"])
                S.op("dve", lambda: nc.vector.tensor_scalar(
                    out=gtP[:, sub, :], in0=gt[:, :, 0], scalar1=gm, scalar2=None, op0=ALU.mult),
                    r=["modT"], w=["gtP"])

        ph0 = contextlib.ExitStack()
        es.enter_context(ph0)
        cT = sb(ph0, "cT", [128, KC, 132], BF16)
        sel1s = sb(ph0, "sel1s", [128, 17]); sel2s = sb(ph0, "sel2s", [4, 17])
        S.dma("sp", lambda: nc.sync.dma_start(out=sel1s[:, :], in_=sel1), w=["sel"])
        S.dma("sp", lambda: nc.sync.dma_start(out=sel2s[:, :], in_=sel2), w=["sel"])

        def ada_issue(st, wsrc, W, tag):
            nblk = (W + 511) // 512
            wv = wsrc.rearrange("(kc p) n -> p kc n", p=128)
            tiles = []
            for bi in range(nblk):
                n = min(512, W - bi * 512)
                t = sb(st, "adw" + tag, [128, KC, n], BF16)
                S.dma("pool", lambda: nc.gpsimd.dma_start(
                    out=t[:, :, :], in_=wv[:, :, bi * 512:bi * 512 + n]), w=[("adw" + tag, bi)])
                tiles.append((t, n, bi))
            return tiles

        def ada_compute(st, tiles, W, agin, agout, mc0, tag):
            o1 = sb(st, "ado1" + tag, [128, W])
            o2 = sb(st, "ado2" + tag, [4, W])
            for (t, n, bi) in tiles:
                b = misc_bank()
                for kc in range(KC):
                    S.op("pe", lambda: nc.tensor.matmul(
                        ps[b][:, 0:n], lhsT=cT[:, kc, 0:128], rhs=t[:, kc, :], start=(kc == 0), stop=(kc == KC - 1)),
                        r=[("adw" + tag, bi), "cT"], w=[("ps", b)], sig=(kc == KC - 1))
                evac(bi, o1[:, bi * 512:bi * 512 + n], ps[b][:, 0:n], [("ps", b)], ["ado1" + tag])
                b = misc_bank()
                for kc in range(KC):
                    S.op("pe", lambda: nc.tensor.matmul(
                        ps[b][0:4, 0:n], lhsT=cT[:, kc, 128:132], rhs=t[:, kc, :], start=(kc == 0), stop=(kc == KC - 1)),
                        r=[("adw" + tag, bi), "cT"], w=[("ps", b)], sig=(kc == KC - 1))
                evac(bi + 1, o2[:, bi * 512:bi * 512 + n], ps[b][0:4, 0:n], [("ps", b)], ["ado2" + tag])
            S.dma("sp", lambda: nc.sync.dma_start(out=agin[0:128, :], in_=o1[:, :]), r=["ado1" + tag], w=["agin" + tag])
            S.dma("sp", lambda: nc.sync.dma_start(out=agin[128:132, :], in_=o2[:, :]), r=["ado2" + tag], w=["agin" + tag])
            S.cc(lambda: nc.gpsimd.collective_compute("AllGather", ALU.bypass, replica_groups=ALL8,
                                                      ins=[agin.opt()], outs=[agout.opt()]),
                 r=["agin" + tag], w=["agout" + tag])
            nch = W // 128
            nbuf = 2 if W <= 512 else 1
            g1 = [sb(st, "adg1" + tag, [128, W]) for _ in range(nbuf)]
            g2 = [sb(st, "adg2" + tag, [4, W]) for _ in range(nbuf)]
            for r_ in range(8):
                G1 = g1[r_ % nbuf]; G2 = g2[r_ % nbuf]
                k1 = ("adg1" + tag, r_ % nbuf); k2 = ("adg2" + tag, r_ % nbuf)
                S.dma("sp", lambda: nc.sync.dma_start(out=G1[:, :], in_=agout[r_ * 132:r_ * 132 + 128, :]),
                      r=["agout" + tag], w=[k1])
                S.dma("sp", lambda: nc.sync.dma_start(out=G2[:, :], in_=agout[r_ * 132 + 128:r_ * 132 + 132, :]),
                      r=["agout" + tag], w=[k2])
                for j0 in range(0, nch, 16):
                    nj = min(16, nch - j0)
                    b = misc_bank()
                    for j in range(j0, j0 + nj):
                        o = ps[b][:, (j - j0) * 32:(j - j0) * 32 + 17]
                        S.op("pe", lambda: nc.tensor.matmul(
                            o, lhsT=G1[:, j * 128:(j + 1) * 128], rhs=sel1s[:, :], start=True, stop=False),
                            r=[k1, "sel"], w=[("ps", b)], sig=False)
                        S.op("pe", lambda: nc.tensor.matmul(
                            o, lhsT=G2[:, j * 128:(j + 1) * 128], rhs=sel2s[:, :], start=False, stop=True),
                            r=[k2, "sel"], w=[("ps", b)], sig=(j == j0 + nj - 1))
                    mc = mc0 + r_ * nch + j0
                    src = ps[b][:, 0:nj * 32].rearrange("p (s n) -> p s n", n=32)[:, :, 0:17]
                    S.op("dve", lambda: nc.vector.tensor_tensor(
                        out=modT[:, mc:mc + nj, :], in0=src,
                        in1=adab[:, mc:mc + nj].unsqueeze(2).to_broadcast([128, nj, 17]), op=ALU.add),
                        r=[("ps", b), "adab"], w=["modT"])

        rows_to_featmajor(ph0, ada_b[0:128, :], 128, adab[:, 0:128], "adab", "ab0")
        rows_to_featmajor(ph0, ada_b[128:144, :], 16, adab[:, 128:144], "adab", "ab1")
        rows_to_featmajor(ph0, norms, 64, normw[:, :], "normw", "nw")
        rows_to_featmajor(ph0, pvec, 16, pvT[:, :], "pvT", "pv")
        rows_to_featmajor(ph0, glub, 8, glubT[:, :], "glubT", "gb")

        with contextlib.ExitStack() as scA:
            tilesA = ada_issue(scA, adaA, 512, "A")
            cst = sb(scA, "cst", [128, D]); cst2 = sb(scA, "cst2", [4, D])
            S.dma("sp", lambda: nc.sync.dma_start(out=cst[:, :], in_=call[0:128, :]), w=["cst"])
            S.dma("sp", lambda: nc.sync.dma_start(out=cst2[:, :], in_=call[128:132, :]), w=["cst2"])
            S.op("act", lambda: nc.scalar.activation(out=cst[:, :], in_=cst[:, :], func=ACT.Silu), r=["cst"], w=["cst"])
            S.op("act", lambda: nc.scalar.activation(out=cst2[:, :], in_=cst2[:, :], func=ACT.Silu), r=["cst2"], w=["cst2"])
            for j0 in range(0, KC, 4):
                b = misc_bank()
                for j in range(j0, j0 + 4):
                    S.op("pe", lambda: nc.tensor.transpose(
                        out=ps[b][:, (j - j0) * 128:(j - j0 + 1) * 128], in_=cst[:, j * 128:(j + 1) * 128], identity=ident[:, :]),
                        r=["cst", "ident"], w=[("ps", b)], sig=(j == j0 + 3))
                evac(j0 // 4, cT[:, j0:j0 + 4, 0:128], ps[b][:, :].rearrange("p (j n) -> p j n", j=4), [("ps", b)], ["cT"])
                b = misc_bank()
                for j in range(j0, j0 + 4):
                    S.op("pe", lambda: nc.tensor.transpose(
                        out=ps[b][:, (j - j0) * 128:(j - j0) * 128 + 4], in_=cst2[:4, j * 128:(j + 1) * 128],
                        identity=ident[:4, :4]),
                        r=["cst2", "ident"], w=[("ps", b)], sig=(j == j0 + 3))
                evac(j0 // 4 + 1, cT[:, j0:j0 + 4, 128:132], ps[b][:, :].rearrange("p (j n) -> p j n", j=4)[:, :, 0:4],
                     [("ps", b)], ["cT"])
            ada_compute(scA, tilesA, 512, ag1_in, ag1_out, 0, "A")
            derive(0, ("ss",))
            S.barrier()

        with contextlib.ExitStack() as scB:
            tilesB = ada_issue(scB, adaB, 1792, "B")
            xpk = xp.rearrange("(c k) d -> k c d", k=8)
            scX = contextlib.ExitStack()
            scB.enter_context(scX)
            stgx = [sb(scX, "stgx", [128, D]) for _ in range(2)]
            for k in range(9):
                sx = stgx[k % 2]
                key = ("stgx", k % 2)
                nrows = 128 if k < 8 else NS
                src = xpk[k] if k < 8 else xs
                col0 = k * 128
                S.dma("sp", lambda: nc.sync.dma_start(out=sx[:nrows, :], in_=src), w=[key])
                for j0 in range(0, KC, 4):
                    b = misc_bank()
                    for j in range(j0, j0 + 4):
                        S.op("pe", lambda: nc.tensor.transpose(
                            out=ps[b][:, (j - j0) * 128:(j - j0) * 128 + nrows],
                            in_=sx[:nrows, j * 128:(j + 1) * 128], identity=ident[:nrows, :nrows]),
                            r=[key, "ident"], w=[("ps", b)], sig=(j == j0 + 3))
                    evac(j0 // 4, xT[:, j0:j0 + 4, col0:col0 + nrows],
                         ps[b][:, :].rearrange("p (j n) -> p j n", j=4)[:, :, 0:nrows],
                         [("ps", b)], [("xT", j) for j in range(j0, j0 + 4)])
            S.barrier()
            scX.close()
            tp1i = sb(scB, "tp1i", [128, 16], I32); tp1 = sb(scB, "tp1", [128, 16])
            S.op("pool", lambda: nc.gpsimd.iota(tp1i[:], pattern=[[1, 8], [8, 2]], base=1, channel_multiplier=0), w=["tp1i"])
            S.op("dve", lambda: nc.vector.tensor_copy(out=tp1[:], in_=tp1i[:]), r=["tp1i"], w=["tp1"])
            for wi, w_ in enumerate(WINS):
                fw = Fw[:, wi, :, :].rearrange("p k c -> p (k c)")
                S.op("dve", lambda: nc.vector.tensor_scalar(out=fw, in0=tp1[:], scalar1=float(w_), scalar2=None,
                                                            op0=ALU.min), r=["tp1"], w=[("Fw", wi)])
                S.op("dve", lambda: nc.vector.reciprocal(out=fw, in_=fw), r=[("Fw", wi)], w=[("Fw", wi)])
                S.op("dve", lambda: nc.vector.tensor_scalar(out=fw, in0=fw, scalar1=float(w_), scalar2=-1.0,
                                                            op0=ALU.mult, op1=ALU.add), r=[("Fw", wi)], w=[("Fw", wi)])
                S.op("dve", lambda: nc.vector.tensor_scalar(out=fw, in0=fw, scalar1=flg[:, 1:2], scalar2=1.0,
                                                            op0=ALU.mult, op1=ALU.add), r=[("Fw", wi), "flg"], w=[("Fw", wi)])
            ada_compute(scB, tilesB, 1792, ag2_in, ag2_out, 32, "B")
            derive(0, ("g",))
            derive(1, ("ss", "g"))
            derive(2, ("ss", "g"))
            S.barrier()

        if with_s5:
            with contextlib.ExitStack() as scS:
                S5Setup(nc, S, sb, ps, misc_bank, evac, locals()).run(scS)
                S.barrier()
        ph0.close()
        modscope.close()

        def rms_stats(scr, rstd):
            for kc in range(KC):
                S.op("act", lambda kc=kc: nc.scalar.activation(out=scr[:, kc, :], in_=xT[:, kc, :], func=ACT.Square),
                     r=[("xT", kc)], w=[("hT", kc)])
            for ti, (c0, n) in enumerate(TILES):
                b = misc_bank()
                for kc in range(KC):
                    S.op("pe", lambda kc=kc, b=b, c0=c0, n=n: nc.tensor.matmul(
                        ps[b][:, 0:n], lhsT=onesb[:, :], rhs=scr[:, kc, c0:c0 + n],
                        start=(kc == 0), stop=(kc == KC - 1)),
                        r=[("hT", kc), "onesb"], w=[("ps", b)], sig=(kc == KC - 1))
                S.op("act", lambda b=b, c0=c0, n=n: nc.scalar.activation(
                    out=rstd[:, c0:c0 + n], in_=ps[b][:, 0:n], func=ACT.Sqrt, bias=epsb[:, 0:1], scale=1.0 / D),
                    r=[("ps", b), "epsb"], w=[("rstd", ti)])
                S.op("dve", lambda c0=c0, n=n: nc.vector.reciprocal(out=rstd[:, c0:c0 + n], in_=rstd[:, c0:c0 + n]),
                     r=[("rstd", ti)], w=[("rstd", ti)])

        def norm_mod(sub, hT):
            with contextlib.ExitStack() as st:
                rstd = sb(st, "rstd", [128, NT])
                tsc = sb(st, "tsc", [128, 2, NT])
                rms_stats(hT, rstd)
                for kc in range(KC):
                    t = tsc[:, kc % 2, :]
                    tk = ("tsc", kc % 2)
                    S.op("dve", lambda kc=kc, t=t: nc.vector.tensor_tensor(out=t, in0=xT[:, kc, :], in1=rstd[:, :], op=ALU.mult),
                         r=[("xT", kc)] + [("rstd", i) for i in range(3)], w=[tk])
                    S.op("act", lambda kc=kc, t=t: nc.scalar.activation(
                        out=hT[:, kc, 0:NPT], in_=t[:, 0:NPT], func=ACT.Identity,
                        bias=shP[:, sub, kc:kc + 1], scale=gsP[:, sub, kc:kc + 1]),
                        r=[tk, "gsP", "shP"], w=[("hT", kc)])
                    S.op("dve", lambda kc=kc, t=t: nc.vector.tensor_tensor(
                        out=t[:, NPT:NT], in0=t[:, NPT:NT], in1=gsS[:, sub, kc, :], op=ALU.mult),
                        r=[tk, "gsS"], w=[tk])
                    S.op("dve", lambda kc=kc, t=t: nc.vector.tensor_tensor(
                        out=hT[:, kc, NPT:NT], in0=t[:, NPT:NT], in1=shS[:, sub, kc, :], op=ALU.add),
                        r=[tk, "shS"], w=[("hT", kc)])
                S.barrier()

        def resid_evac(bo, m, ti, c0, n, sub, tmpS):
            if ti < 2:
                S.op("dve", lambda: nc.vector.scalar_tensor_tensor(
                    out=xT[:, m, c0:c0 + n], in0=ps[bo][:, 0:n], scalar=gtP[:, sub, m:m + 1],
                    in1=xT[:, m, c0:c0 + n], op0=ALU.mult, op1=ALU.add),
                    r=[("ps", bo), "gtP", ("xT", m)], w=[("xT", m)])
            else:
                S.op("dve", lambda: nc.vector.tensor_tensor(
                    out=tmpS[:, :], in0=ps[bo][:, 0:n], in1=gtS[:, sub, m, :], op=ALU.mult),
                    r=[("ps", bo), "gtS"], w=["tmpS"])
                S.op("dve", lambda: nc.vector.tensor_tensor(
                    out=xT[:, m, c0:c0 + n], in0=xT[:, m, c0:c0 + n], in1=tmpS[:, :], op=ALU.add),
                    r=["tmpS", ("xT", m)], w=[("xT", m)])

        def ffn(fi, sub, hook=None):
            with contextlib.ExitStack() as ph:
                hT = sb(ph, "hT", [128, KC, NT], BF16)
                aT = sb(ph, "aT", [128, FPP, NT], BF16)
                NR = 3
                wgr = [sb(ph, "wgr", [128, KC, 128], BF16) for i in range(NR)]
                wur = [sb(ph, "wur", [128, KC, 128], BF16) for i in range(NR)]
                wdr = [sb(ph, "wdr", [128, FPP, 128], BF16) for i in range(NR)]
                sgb = [sb(ph, "sgb", [128, 512]) for i in range(2)]
                tmpS = sb(ph, "tmpS", [128, NS])
                wgv = w_g[fi].rearrange("(kc p) n -> p kc n", p=128)
                wuv = w_u[fi].rearrange("(kc p) n -> p kc n", p=128)
                wdv = w_d[fi].rearrange("(fc p) n -> p fc n", p=128)
                loads = []
                for q in range(NPART):
                    for fl in range(FPP):
                        loads.append(("gu", q * FPP + fl))
                    for m in range(KC):
                        loads.append(("d", q, m))
                state = {"next": 0}

                def issue_load():
                    i = state["next"]
                    if i >= len(loads):
                        return
                    state["next"] += 1
                    L = loads[i]
                    if L[0] == "gu":
                        f = L[1]
                        sl = f % NR
                        S.dma("pool", lambda: nc.gpsimd.dma_start(
                            out=wgr[sl][:, :, :], in_=wgv[:, :, f * 128:(f + 1) * 128]), w=[("wg", sl)])
                        S.dma("pool", lambda: nc.gpsimd.dma_start(
                            out=wur[sl][:, :, :], in_=wuv[:, :, f * 128:(f + 1) * 128]), w=[("wu", sl)])
                    else:
                        _, q, m = L
                        sl = (q * KC + m) % NR
                        S.dma("pool", lambda: nc.gpsimd.dma_start(
                            out=wdr[sl][:, :, :], in_=wdv[:, q * FPP:(q + 1) * FPP, m * 128:(m + 1) * 128]),
                            w=[("wd", sl)])

                issue_load()
                issue_load()
                norm_mod(sub, hT)
                gi = 0
                ei = 0
                for q in range(NPART):
                    for fl in range(FPP):
                        f = q * FPP + fl
                        sl = f % NR
                        issue_load()
                        if hook is not None and f == 3:
                            hook(ph)
                        for ti, (c0, n) in enumerate(TILES):
                            bg = gi % 2
                            bu = 2 + gi % 2
                            gi += 1
                            for kc in range(KC):
                                S.op("pe", lambda kc=kc: nc.tensor.matmul(
                                    ps[bg][:, 0:n], lhsT=wgr[sl][:, kc, :], rhs=hT[:, kc, c0:c0 + n],
                                    start=(kc == 0), stop=(kc == KC - 1)),
                                    r=[("wg", sl), ("hT", kc)], w=[("ps", bg)], sig=(kc == KC - 1))
                            for kc in range(KC):
                                S.op("pe", lambda kc=kc: nc.tensor.matmul(
                                    ps[bu][:, 0:n], lhsT=wur[sl][:, kc, :], rhs=hT[:, kc, c0:c0 + n],
                                    start=(kc == 0), stop=(kc == KC - 1)),
                                    r=[("wu", sl), ("hT", kc)], w=[("ps", bu)], sig=(kc == KC - 1))
                            sg = sgb[ei % 2]
                            sk = ("sgb", ei % 2)
                            ei += 1
                            S.op("act", lambda: nc.scalar.activation(out=sg[:, 0:n], in_=ps[bg][:, 0:n], func=ACT.Silu),
                                 r=[("ps", bg)], w=[sk])
                            S.op("dve", lambda: nc.vector.tensor_tensor(
                                out=aT[:, fl, c0:c0 + n], in0=sg[:, 0:n], in1=ps[bu][:, 0:n], op=ALU.mult),
                                r=[sk, ("ps", bu)], w=[("aT", fl, ti)])
                    for m in range(KC):
                        sl = (q * KC + m) % NR
                        issue_load()
                        for ti, (c0, n) in enumerate(TILES):
                            bo = 4 + gi % 2
                            gi += 1
                            for fl in range(FPP):
                                S.op("pe", lambda fl=fl: nc.tensor.matmul(
                                    ps[bo][:, 0:n], lhsT=wdr[sl][:, fl, :], rhs=aT[:, fl, c0:c0 + n],
                                    start=(fl == 0), stop=(fl == FPP - 1)),
                                    r=[("wd", sl), ("aT", fl, ti)], w=[("ps", bo)], sig=(fl == FPP - 1))
                            resid_evac(bo, m, ti, c0, n, sub, tmpS)
            S.barrier()

        ffn(0, 0)

        mixer = Mixer(nc, S, sb, ps, misc_bank, evac, locals())
        mixer.run(with_s5 and s5main)

        ffn(1, 2)

        with contextlib.ExitStack() as ph:
            scr = sb(ph, "scrF", [128, KC, NT], BF16)
            rstd = sb(ph, "rstdF", [128, NT])
            ost = [sb(ph, "ost", [128, D]) for i in range(2)]
            rms_stats(scr, rstd)
            for kc in range(KC):
                S.op("dve", lambda kc=kc: nc.vector.scalar_tensor_tensor(
                    out=xT[:, kc, :], in0=xT[:, kc, :], scalar=normw[:, 3 * KC + kc:3 * KC + kc + 1],
                    in1=rstd[:, :], op0=ALU.mult, op1=ALU.mult),
                    r=[("xT", kc), "normw"] + [("rstd", i) for i in range(3)], w=[("xT", kc)])
            ypk = yp.rearrange("(c k) d -> k c d", k=8)
            for k in range(9):
                o = ost[k % 2]
                ok = ("ost", k % 2)
                nrows = 128 if k < 8 else NS
                col0 = k * 128
                for j0 in range(0, KC, 4):
                    b = misc_bank()
                    for j in range(j0, j0 + 4):
                        S.op("pe", lambda j=j, b=b, j0=j0: nc.tensor.transpose(
                            out=ps[b][:nrows, (j - j0) * 128:(j - j0 + 1) * 128],
                            in_=xT[:, j, col0:col0 + nrows], identity=ident[:, :]),
                            r=[("xT", j), "ident"], w=[("ps", b)], sig=(j == j0 + 3))
                    evac(j0 // 4, o[:nrows, j0 * 128:(j0 + 4) * 128], ps[b][:nrows, :], [("ps", b)], [ok])
                dstd = ypk[k] if k < 8 else ys
                S.dma("sp", lambda o=o, dstd=dstd: nc.sync.dma_start(out=dstd, in_=o[:nrows, :]), r=[ok])
            S.finish()
    return nc


class Mixer:
    def __init__(self, nc, S, sb, ps, misc_bank, evac, env):
        self.nc, self.S, self.sb, self.ps, self.misc_bank, self.evac, self.e = nc, S, sb, ps, misc_bank, evac, env

    def run(self, with_s5):
        nc, S, sb, ps, misc_bank, evac, e = self.nc, self.S, self.sb, self.ps, self.misc_bank, self.evac, self.e
        xT, ident, identb, flg, pvT, Fw = e["xT"], e["ident"], e["identb"], e["flg"], e["pvT"], e["Fw"]
        w_in, w_out, pool_w, spool = e["w_in"], e["w_out"], e["pool_w"], e["spool"]
        agv_in, agv_out, npool_p, npool_s = e["agv_in"], e["agv_out"], e["npool_p"], e["npool_s"]
        gtP, gtS = e["gtP"], e["gtS"]
        with contextlib.ExitStack() as mp:
            mixedP = sb(mp, "mixedP", [128, 8, NT], BF16)
            us = sb(mp, "us", [NS, 1024])
            X = sb(mp, "X", [128, 64, 128], BF16)
            with contextlib.ExitStack() as m1:
                hT = sb(m1, "hTm", [128, KC, NT], BF16)
                e["norm_mod"](1, hT)
                wr = [sb(m1, "wr", [128, KC, 128], BF16) for _ in range(2)]
                hrT = sb(m1, "hrT", [128, 8, 16])
                pw = sb(m1, "pw", [128, 4, 2, 256], BF16)
                zsT = sb(m1, "zsT", [128, 8, NS], BF16)
                hH = sb(m1, "hH", [128, KC, 16], BF16)
                sp_ = contextlib.ExitStack()
                m1.enter_context(sp_)
                vs = sb(sp_, "vs", [NS, 1024])
                vh = sb(sp_, "vh", [16, 1024])
                hrecv = sb(sp_, "hrecv", [16, 1024])
                wiv = w_in.rearrange("(kc p) n -> p kc n", p=128)
                S.dma("pool", lambda: nc.gpsimd.dma_start(
                    out=pw[:, :, :, :], in_=pool_w.rearrange("g (k p) n -> p g k n", p=128)), w=["pw"])
                lcount = [0]

                def wload(blk):
                    sl = lcount[0] % 2
                    lcount[0] += 1
                    S.dma("pool", lambda: nc.gpsimd.dma_start(
                        out=wr[sl][:, :, :], in_=wiv[:, :, blk * 128:(blk + 1) * 128]), w=[("wr", sl)])
                    return sl

                for k8 in range(0, KC, 8):
                    S.op("dve", lambda: nc.vector.tensor_copy(
                        out=hH[:, k8:k8 + 8, :].rearrange("p a (k c) -> p a k c", c=2),
                        in_=hT[:, k8:k8 + 8, 0:NPT].rearrange("p a (k c) -> p a k c", k=8)[:, :, :, 126:128]),
                        r=[("hT", kc) for kc in range(k8, k8 + 8)], w=["hH"])
                sl = wload(8)
                for j in range(8):
                    nsl = wload(8 + j + 1) if j < 7 else None
                    for kc in range(KC):
                        S.op("pe", lambda: nc.tensor.matmul(
                            ps[j // 4][0:NS, (j % 4) * 128:(j % 4 + 1) * 128], lhsT=hT[:, kc, NPT:NT], rhs=wr[sl][:, kc, :],
                            start=(kc == 0), stop=(kc == KC - 1)),
                            r=[("wr", sl), ("hT", kc)], w=[("ps", j // 4)], sig=(kc == KC - 1))
                    for kc in range(KC):
                        S.op("pe", lambda: nc.tensor.matmul(
                            ps[2 + j // 4][0:16, (j % 4) * 128:(j % 4 + 1) * 128], lhsT=hH[:, kc, :],
                            rhs=wr[sl][:, kc, :], start=(kc == 0), stop=(kc == KC - 1)),
                            r=[("wr", sl), "hH"], w=[("ps", 2 + j // 4)], sig=(kc == KC - 1))
                    sl = nsl
                for hb in range(2):
                    evac(hb, vs[:, hb * 512:(hb + 1) * 512], ps[hb][0:NS, :], [("ps", hb)], ["vs"])
                    evac(hb + 1, vh[:, hb * 512:(hb + 1) * 512], ps[2 + hb][0:16, :], [("ps", 2 + hb)], ["vh"])
                S.dma("sp", lambda: nc.sync.dma_start(out=agv_in, in_=vh[:, :]), r=["vh"], w=["agv_in"])
                S.cc(lambda: nc.gpsimd.collective_compute("AllGather", ALU.bypass, replica_groups=e["PAIRS"],
                                                          ins=[agv_in.opt()], outs=[agv_out.opt()]),
                     r=["agv_in"], w=["agv_out"])
                S.dma("sp", lambda: nc.sync.dma_start(
                    out=npool_p.rearrange("(cp k) d -> k cp d", k=8), in_=agv_in.rearrange("(k cp) d -> k cp d", cp=2)),
                    r=["agv_in"])
                S.dma("sp", lambda: nc.sync.dma_start(out=hrecv[:, :], in_=agv_out[0:16, :]), r=["agv_out"], w=["hrecv"])
                S.dma("sp", lambda: nc.sync.dma_start(out=npool_s[:, 0:14, :], in_=spool[:, 1:15, :]))
                S.dma("sp", lambda: nc.sync.dma_start(out=npool_s[:, 14, :], in_=vs[:, :]), r=["vs"])
                if True:
                    sbuf_ = [sb(sp_, "spb", [NS, 16, 128]) for _ in range(2)]
                    zs = sb(sp_, "zs", [NS, 1024])
                    for ch in range(8):
                        bf = sbuf_[ch % 2]
                        bk = ("spb", ch % 2)
                        w_ = WINS[ch // 2]
                        S.dma("sp", lambda: nc.sync.dma_start(out=bf[:, 0:15, :], in_=spool[:, :, ch * 128:(ch + 1) * 128]), w=[bk])
                        S.op("dve", lambda: nc.vector.tensor_copy(out=bf[:, 15, :], in_=vs[:, ch * 128:(ch + 1) * 128]),
                             r=["vs"], w=[bk])
                        S.op("dve", lambda: nc.vector.tensor_reduce(
                            out=zs[:, ch * 128:(ch + 1) * 128], in_=bf[:, 16 - w_:16, :].rearrange("p r c -> p c r"),
                            axis=mybir.AxisListType.X, op=ALU.add), r=[bk], w=[("zs", ch)])
                        S.op("dve", lambda: nc.vector.scalar_tensor_tensor(
                            out=zs[:, ch * 128:(ch + 1) * 128], in0=zs[:, ch * 128:(ch + 1) * 128], scalar=1.0 / w_,
                            in1=vs[:, ch * 128:(ch + 1) * 128], op0=ALU.mult, op1=ALU.subtract),
                            r=[("zs", ch), "vs"], w=[("zs", ch)])
                    for j0 in range(0, 8, 4):
                        b = misc_bank()
                        for j in range(j0, j0 + 4):
                            S.op("pe", lambda: nc.tensor.transpose(
                                out=ps[b][:, (j - j0) * 128:(j - j0) * 128 + NS], in_=zs[:NS, j * 128:(j + 1) * 128],
                                identity=ident[:NS, :NS]),
                                r=[("zs", j), "ident"], w=[("ps", b)], sig=(j == j0 + 3))
                        evac(0, zsT[:, j0:j0 + 4, :], ps[b][:, :].rearrange("p (j n) -> p j n", j=4)[:, :, 0:NS],
                             [("ps", b)], ["zsT"])
                    for j0 in range(0, 8, 4):
                        b = misc_bank()
                        for j in range(j0, j0 + 4):
                            S.op("pe", lambda: nc.tensor.transpose(
                                out=ps[b][:, (j - j0) * 128:(j - j0) * 128 + 16], in_=hrecv[:16, j * 128:(j + 1) * 128],
                                identity=ident[:16, :16]),
                                r=["hrecv", "ident"], w=[("ps", b)], sig=(j == j0 + 3))
                        S.op("dve", lambda: nc.vector.tensor_scalar(
                            out=hrT[:, j0:j0 + 4, :], in0=ps[b][:, :].rearrange("p (j n) -> p j n", j=4)[:, :, 0:16],
                            scalar1=flg[:, 0:1], scalar2=None, op0=ALU.mult), r=[("ps", b), "flg"], w=["hrT"])
                    S.barrier()
                    sp_.close()
                with contextlib.ExitStack() as m2:
                    U_tok = sb(m2, "U_tok", [128, 64, 8, 16], BF16)
                    sl = wload(0)
                    for blk in range(8):
                        nsl = wload(blk + 1) if blk < 7 else wload(8)
                        for half in range(2):
                            b = half
                            for i in range(4):
                                k = 7 - (half * 4 + i)
                                for kc in range(KC):
                                    S.op("pe", lambda: nc.tensor.matmul(
                                        ps[b][:, i * 128:(i + 1) * 128], lhsT=hT[:, kc, k * 128:(k + 1) * 128],
                                        rhs=wr[sl][:, kc, :], start=(kc == 0), stop=(kc == KC - 1)),
                                        r=[("wr", sl), ("hT", kc)], w=[("ps", b)], sig=(kc == KC - 1 and i == 3))
                            evac(half, U_tok[:, blk * 8:(blk + 1) * 8, half * 4:half * 4 + 4, :].rearrange("p g k h -> p k g h"),
                                 ps[b][:, :].rearrange("p (i g h) -> p i g h", i=4, g=8), [("ps", b)], [("U_tok", blk)])
                        b = 2 + blk % 2
                        for kc in range(KC):
                            S.op("pe", lambda: nc.tensor.matmul(
                                ps[b][0:NS, 0:128], lhsT=hT[:, kc, NPT:NT], rhs=wr[sl][:, kc, :],
                                start=(kc == 0), stop=(kc == KC - 1)),
                                r=[("wr", sl), ("hT", kc)], w=[("ps", b)], sig=(kc == KC - 1))
                        evac(blk, us[:, blk * 128:(blk + 1) * 128], ps[b][0:NS, 0:128], [("ps", b)], ["us"])
                        for g0 in range(0, 8, 4):
                            b = misc_bank()
                            for gg in range(g0, g0 + 4):
                                g = blk * 8 + gg
                                S.op("pe", lambda: nc.tensor.matmul(
                                    ps[b][:, (gg - g0) * 128:(gg - g0 + 1) * 128], lhsT=U_tok[:, g, :, :].rearrange("p k h -> p (k h)"),
                                    rhs=identb[:, :], start=True, stop=True),
                                    r=[("U_tok", blk), "identb"], w=[("ps", b)], sig=(gg == g0 + 3))
                            evac(g0 // 4, X[:, blk * 8 + g0:blk * 8 + g0 + 4, :],
                                 ps[b][:, :].rearrange("p (g n) -> p g n", g=4), [("ps", b)], [("X", blk)])
                        sl = nsl
                    vT = [sb(m2, "vT", [128, 8, 130]) for _ in range(2)]
                    Ab = [sb(m2, "Ab", [128, 8, 130]) for _ in range(2)]
                    Bb = [sb(m2, "Bb", [128, 8, 130]) for _ in range(2)]
                    zTg = sb(m2, "zTg", [128, 2, NT], BF16)
                    for t_ in Ab + Bb + vT:
                        S.op("pool", lambda: nc.gpsimd.memset(t_[:, :, :], 0.0), w=["poolinit"])
                    S.barrier()

                    def shift_add(dst, src, s, kk):
                        S.op("dve", lambda: nc.vector.tensor_tensor(out=dst[:, s:8, :], in0=src[:, s:8, :], in1=src[:, 0:8 - s, :],
                                                                   op=ALU.add), r=[kk], w=[kk])
                        S.op("dve", lambda: nc.vector.tensor_tensor(out=dst[:, 0:s, 1:130], in0=src[:, 0:s, 1:130],
                                                                   in1=src[:, 8 - s:8, 0:129], op=ALU.add), r=[kk], w=[kk])

                    for g in range(4):
                        w_ = WINS[g]
                        for c2 in range(2):
                            ch = 2 * g + c2
                            nsl = wload(8 + ch + 1) if ch < 7 else None
                            V, A_, B_ = vT[c2], Ab[c2], Bb[c2]
                            kk = ("pool", c2)
                            for ti in range(2):
                                b = ti
                                c0 = ti * 512
                                for kc in range(KC):
                                    S.op("pe", lambda: nc.tensor.matmul(
                                        ps[b][:, :], lhsT=wr[sl][:, kc, :], rhs=hT[:, kc, c0:c0 + 512],
                                        start=(kc == 0), stop=(kc == KC - 1)),
                                        r=[("wr", sl), ("hT", kc)], w=[("ps", b)], sig=(kc == KC - 1))
                                evac(ti, V[:, ti * 4:ti * 4 + 4, 2:130], ps[b][:, :].rearrange("p (k c) -> p k c", k=4),
                                     [("ps", b)], [kk])
                            S.op("dve", lambda: nc.vector.tensor_copy(out=V[:, :, 0:2], in_=hrT[:, ch, :].rearrange("p (k c) -> p k c", c=2)),
                                 r=["hrT", kk], w=[kk])
                            shift_add(A_, V, 1, kk)
                            P_ = A_
                            if w_ >= 4:
                                shift_add(B_, A_, 2, kk)
                                P_ = B_
                            if w_ >= 8:
                                shift_add(A_, B_, 4, kk)
                                P_ = A_
                            if w_ >= 16:
                                S.op("dve", lambda: nc.vector.tensor_tensor(out=B_[:, :, 1:130], in0=A_[:, :, 1:130],
                                                                           in1=A_[:, :, 0:129], op=ALU.add), r=[kk], w=[kk])
                                P_ = B_
                            S.op("dve", lambda: nc.vector.tensor_tensor(out=P_[:, :, 2:4], in0=P_[:, :, 2:4], in1=Fw[:, g, :, :],
                                                                       op=ALU.mult), r=[kk, ("Fw", g)], w=[kk])
                            S.op("dve", lambda: nc.vector.scalar_tensor_tensor(
                                out=zTg[:, c2, 0:NPT].rearrange("p (k c) -> p k c", k=8), in0=P_[:, :, 2:130], scalar=1.0 / w_,
                                in1=V[:, :, 2:130], op0=ALU.mult, op1=ALU.subtract), r=[kk], w=[("zTg", c2)])
                            S.op("dve", lambda: nc.vector.tensor_copy(out=zTg[:, c2, NPT:NT], in_=zsT[:, ch, :]),
                                 r=["zsT"], w=[("zTg", c2)])
                            sl = nsl
                        for m2_ in range(2):
                            chn = 2 * g + m2_
                            for ti, (c0, n) in enumerate(TILES):
                                b = 4 + (ti % 2)
                                for k2 in range(2):
                                    S.op("pe", lambda: nc.tensor.matmul(
                                        ps[b][:, 0:n], lhsT=pw[:, g, k2, m2_ * 128:(m2_ + 1) * 128], rhs=zTg[:, k2, c0:c0 + n],
                                        start=(k2 == 0), stop=(k2 == 1)),
                                        r=["pw", ("zTg", k2)], w=[("ps", b)], sig=(k2 == 1))
                                S.op("dve", lambda: nc.vector.tensor_scalar(
                                    out=mixedP[:, chn, c0:c0 + n], in0=ps[b][:, 0:n], scalar1=pvT[:, chn:chn + 1],
                                    scalar2=pvT[:, 8 + chn:8 + chn + 1], op0=ALU.add, op1=ALU.mult),
                                    r=[("ps", b), "pvT"], w=[("mixedP", chn)])
                    S.barrier()
                S.barrier()
            mixedS = sb(mp, "mixedS", [128, 8, NT], BF16)
            if with_s5:
                S5Main(nc, S, sb, ps, misc_bank, evac, e, X, us, mixedS).run(mp)
            else:
                S.op("dve", lambda: nc.vector.memset(mixedS[:, :, :], 0.0), w=["mixedS"])
            S.barrier()
            with contextlib.ExitStack() as m3:
                wo = [sb(m3, "wo", [128, KC, 128], BF16) for _ in range(3)]
                tmpS = sb(m3, "tmpSo", [128, NS])
                wov = w_out.rearrange("(kc p) n -> p kc n", p=128)

                def oload(m):
                    S.dma("pool", lambda: nc.gpsimd.dma_start(out=wo[m % 3][:, :, :], in_=wov[:, :, m * 128:(m + 1) * 128]),
                          w=[("wo", m % 3)])
                oload(0)
                oload(1)
                gi = 0
                for m in range(KC):
                    if m + 2 < KC:
                        oload(m + 2)
                    for ti, (c0, n) in enumerate(TILES):
                        bo = 4 + gi % 2
                        gi += 1
                        for kc in range(KC):
                            src = mixedS[:, kc, c0:c0 + n] if kc < 8 else mixedP[:, kc - 8, c0:c0 + n]
                            S.op("pe", lambda: nc.tensor.matmul(
                                ps[bo][:, 0:n], lhsT=wo[m % 3][:, kc, :], rhs=src, start=(kc == 0), stop=(kc == KC - 1)),
                                r=[("wo", m % 3), "mixedS", ("mixedP", kc - 8)], w=[("ps", bo)], sig=(kc == KC - 1))
                        e["resid_evac"](bo, m, ti, c0, n, 1, tmpS)
                S.barrier()

S5STAGE = [4]


class S5Setup:
    def __init__(self, nc, S, sb, ps, misc_bank, evac, env):
        self.nc, self.S, self.sb, self.ps, self.misc_bank, self.evac, self.e = nc, S, sb, ps, misc_bank, evac, env

    def run(self, st):
        nc, S, sb, ps, e = self.nc, self.S, self.sb, self.ps, self.e
        V = nc.vector
        ident, identb = e["ident"], e["identb"]
        a_re, a_im, al_re, al_im = e["a_re"], e["a_im"], e["al_re"], e["al_im"]
        st_w1, st_toep, st_w2 = e["st_w1"], e["st_toep"], e["st_w2"]
        K = "s5s"

        def dv(fn, r=(), w=()):
            S.op("dve", fn, r=[K] + list(r), w=[K] + list(w))

        def ac(fn, r=(), w=()):
            S.op("act", fn, r=[K] + list(r), w=[K] + list(w))

        lamA = sb(st, "lamA", [128, 3, 32])
        BA = sb(st, "BA", [128, 2, 32, 16])
        CA = sb(st, "CA", [128, 2, 32, 16])
        Dcol = sb(st, "Dcol", [128, 64])
        S.dma("sp", lambda: nc.sync.dma_start(out=lamA[:, :, :], in_=e["lamA_d"]), w=[K])
        S.dma("sp", lambda: nc.sync.dma_start(out=BA[:, :, :, :], in_=e["BA_d"]), w=[K])
        S.dma("sp", lambda: nc.sync.dma_start(out=CA[:, :, :, :], in_=e["CA_d"]), w=[K])
        S.dma("sp", lambda: nc.sync.dma_start(out=Dcol[:, :], in_=e["Dcol_d"]), w=[K])
        SM = sb(st, "SM", [128, 20, 32])
        QI = sb(st, "QI", [128, 32], I32)
        (lr, dt, lrd, ang, mag, rr, qf, fr, m1, sn, cs, den, nr, f_re, f_im, t1, t2, fc) = [SM[:, i, :] for i in range(18)]
        li = lamA[:, 1, :]
        dv(lambda: V.tensor_scalar(out=lr, in0=lamA[:, 0, :], scalar1=-1e-4, scalar2=None, op0=ALU.min))
        ac(lambda: nc.scalar.activation(out=dt, in_=lamA[:, 2, :], func=ACT.Exp))
        dv(lambda: V.tensor_tensor(out=lrd, in0=lr, in1=dt, op=ALU.mult))
        dv(lambda: V.tensor_tensor(out=ang, in0=li, in1=dt, op=ALU.mult))
        ac(lambda: nc.scalar.activation(out=mag, in_=lrd, func=ACT.Exp))
        dv(lambda: V.tensor_scalar(out=rr, in0=ang, scalar1=1.0 / TWO_PI, scalar2=None, op0=ALU.mult))
        dv(lambda: V.tensor_copy(out=QI[:, :], in_=rr))
        dv(lambda: V.tensor_copy(out=qf, in_=QI[:, :]))
        dv(lambda: V.tensor_tensor(out=fr, in0=rr, in1=qf, op=ALU.subtract))
        dv(lambda: V.tensor_single_scalar(out=m1, in_=fr, scalar=0.5, op=ALU.is_gt))
        dv(lambda: V.tensor_tensor(out=fr, in0=fr, in1=m1, op=ALU.subtract))
        dv(lambda: V.tensor_single_scalar(out=m1, in_=fr, scalar=-0.5, op=ALU.is_lt))
        dv(lambda: V.tensor_tensor(out=fr, in0=fr, in1=m1, op=ALU.add))
        ac(lambda: nc.scalar.activation(out=sn, in_=fr, func=ACT.Sin, scale=TWO_PI))
        dv(lambda: V.tensor_scalar(out=fc, in0=fr, scalar1=0.25, scalar2=None, op0=ALU.add))
        dv(lambda: V.tensor_single_scalar(out=m1, in_=fc, scalar=0.5, op=ALU.is_gt))
        dv(lambda: V.tensor_tensor(out=fc, in0=fc, in1=m1, op=ALU.subtract))
        ac(lambda: nc.scalar.activation(out=cs, in_=fc, func=ACT.Sin, scale=TWO_PI))
        dv(lambda: V.tensor_tensor(out=a_re[:, :], in0=mag, in1=cs, op=ALU.mult))
        dv(lambda: V.tensor_tensor(out=a_im[:, :], in0=mag, in1=sn, op=ALU.mult))
        dv(lambda: V.tensor_tensor(out=t1, in0=lr, in1=lr, op=ALU.mult))
        dv(lambda: V.tensor_tensor(out=t2, in0=li, in1=li, op=ALU.mult))
        dv(lambda: V.tensor_tensor(out=den, in0=t1, in1=t2, op=ALU.add))
        dv(lambda: V.reciprocal(out=den, in_=den))
        dv(lambda: V.tensor_scalar(out=nr, in0=a_re[:, :], scalar1=-1.0, scalar2=None, op0=ALU.add))
        dv(lambda: V.tensor_tensor(out=t1, in0=nr, in1=lr, op=ALU.mult))
        dv(lambda: V.tensor_tensor(out=t2, in0=a_im[:, :], in1=li, op=ALU.mult))
        dv(lambda: V.tensor_tensor(out=t1, in0=t1, in1=t2, op=ALU.add))
        dv(lambda: V.tensor_tensor(out=f_re, in0=t1, in1=den, op=ALU.mult))
        dv(lambda: V.tensor_tensor(out=t1, in0=a_im[:, :], in1=lr, op=ALU.mult))
        dv(lambda: V.tensor_tensor(out=t2, in0=nr, in1=li, op=ALU.mult))
        dv(lambda: V.tensor_tensor(out=t1, in0=t1, in1=t2, op=ALU.subtract))
        dv(lambda: V.tensor_tensor(out=f_im, in0=t1, in1=den, op=ALU.mult))
        Pw = sb(st, "Pw", [128, 2, 9, 32])
        dv(lambda: V.memset(Pw[:, 0, 0, :], 1.0))
        dv(lambda: V.memset(Pw[:, 1, 0, :], 0.0))
        for m in range(1, 9):
            dv(lambda: V.tensor_tensor(out=t1, in0=Pw[:, 0, m - 1, :], in1=a_re[:, :], op=ALU.mult))
            dv(lambda: V.tensor_tensor(out=t2, in0=Pw[:, 1, m - 1, :], in1=a_im[:, :], op=ALU.mult))
            dv(lambda: V.tensor_tensor(out=Pw[:, 0, m, :], in0=t1, in1=t2, op=ALU.subtract))
            dv(lambda: V.tensor_tensor(out=t1, in0=Pw[:, 0, m - 1, :], in1=a_im[:, :], op=ALU.mult))
            dv(lambda: V.tensor_tensor(out=t2, in0=Pw[:, 1, m - 1, :], in1=a_re[:, :], op=ALU.mult))
            dv(lambda: V.tensor_tensor(out=Pw[:, 1, m, :], in0=t1, in1=t2, op=ALU.add))
        dv(lambda: V.tensor_copy(out=al_re[:, :], in_=Pw[:, 0, 8, :]))
        dv(lambda: V.tensor_copy(out=al_im[:, :], in_=Pw[:, 1, 8, :]))

        def bc(ap32):
            return ap32.unsqueeze(2).to_broadcast([128, 32, 16])

        T1 = sb(st, "T1", [128, 32, 16]); T2 = sb(st, "T2", [128, 32, 16])
        Bb = sb(st, "Bb", [128, 2, 32, 16])
        dv(lambda: V.tensor_tensor(out=T1[:], in0=BA[:, 0], in1=bc(f_re), op=ALU.mult))
        dv(lambda: V.tensor_tensor(out=T2[:], in0=BA[:, 1], in1=bc(f_im), op=ALU.mult))
        dv(lambda: V.tensor_tensor(out=Bb[:, 0], in0=T1[:], in1=T2[:], op=ALU.subtract))
        dv(lambda: V.tensor_tensor(out=T1[:], in0=BA[:, 1], in1=bc(f_re), op=ALU.mult))
        dv(lambda: V.tensor_tensor(out=T2[:], in0=BA[:, 0], in1=bc(f_im), op=ALU.mult))
        dv(lambda: V.tensor_tensor(out=Bb[:, 1], in0=T1[:], in1=T2[:], op=ALU.add))
        CPr = sb(st, "CPr", [128, 2, 32, 16, 16], BF16)
        Bz = sb(st, "Bz", [128, 2, 32, 15, 16], BF16)
        W1A = sb(st, "W1A", [128, 2, 32, 8, 16], BF16)
        S.op("pool", lambda: nc.gpsimd.memset(CPr[:].rearrange("p a b c d -> p (a b c d)"), 0.0), w=[K])
        S.op("pool", lambda: nc.gpsimd.memset(Bz[:].rearrange("p a b c d -> p (a b c d)"), 0.0), w=[K])
        for m in range(9):
            s_ = 8 - m
            dv(lambda: V.tensor_tensor(out=T1[:], in0=CA[:, 0], in1=bc(Pw[:, 0, m, :]), op=ALU.mult))
            dv(lambda: V.tensor_tensor(out=T2[:], in0=CA[:, 1], in1=bc(Pw[:, 1, m, :]), op=ALU.mult))
            dv(lambda: V.tensor_tensor(out=CPr[:, 0, :, s_, :], in0=T1[:], in1=T2[:], op=ALU.subtract))
            dv(lambda: V.tensor_tensor(out=T1[:], in0=CA[:, 0], in1=bc(Pw[:, 1, m, :]), op=ALU.mult))
            dv(lambda: V.tensor_tensor(out=T2[:], in0=CA[:, 1], in1=bc(Pw[:, 0, m, :]), op=ALU.mult))
            dv(lambda: V.scalar_tensor_tensor(out=CPr[:, 1, :, s_, :], in0=T1[:], scalar=-1.0, in1=T2[:],
                                              op0=ALU.mult, op1=ALU.subtract))
        for ri in range(2):
            dv(lambda: V.tensor_copy(out=Bz[:, ri, :, 7, :], in_=Bb[:, ri]))
        for k_ in range(8):
            dv(lambda: V.tensor_tensor(out=T1[:], in0=Bb[:, 0], in1=bc(Pw[:, 0, k_, :]), op=ALU.mult))
            dv(lambda: V.tensor_tensor(out=T2[:], in0=Bb[:, 1], in1=bc(Pw[:, 1, k_, :]), op=ALU.mult))
            dv(lambda: V.tensor_tensor(out=W1A[:, 0, :, k_, :], in0=T1[:], in1=T2[:], op=ALU.subtract))
            dv(lambda: V.tensor_tensor(out=T1[:], in0=Bb[:, 0], in1=bc(Pw[:, 1, k_, :]), op=ALU.mult))
            dv(lambda: V.tensor_tensor(out=T2[:], in0=Bb[:, 1], in1=bc(Pw[:, 0, k_, :]), op=ALU.mult))
            dv(lambda: V.tensor_tensor(out=W1A[:, 1, :, k_, :], in0=T1[:], in1=T2[:], op=ALU.add))
        for ri in range(2):
            S.dma("sp", lambda: nc.sync.dma_start(
                out=st_w2[ri].rearrange("p (r s h) -> p r s h", r=32, s=8), in_=CPr[:, ri, :, 0:8, :]), r=[K], w=["st_w2"])
        TS = [sb(st, "TS", [128, 4, 128], BF16) for _ in range(2)]
        toepv = st_toep.rearrange("p (r two c) -> p r two c", two=2, c=128)
        for gq in range(16):
            gl, q4 = gq // 8, gq % 8
            b = gq % 4
            ts_ = TS[gq % 2]
            tk = ("TS", gq % 2)
            R = slice(gl * 64, gl * 64 + 64)
            for gg in range(4):
                pr = q4 * 4 + gg
                n_ = 0
                for v in range(8):
                    for ri in range(2):
                        S.op("pe", lambda: nc.tensor.matmul(
                            ps[b][:, gg * 128:(gg + 1) * 128],
                            lhsT=Bz[R, ri, pr, 7 - v:15 - v, :].rearrange("p b h -> p (b h)"),
                            rhs=CPr[R, ri, pr, 8 - v:16 - v, :].rearrange("p s h -> p (s h)"),
                            start=(n_ == 0), stop=(n_ == 15)),
                            r=[K], w=[("ps", b)], sig=(n_ == 15 and gg == 3))
                        n_ += 1
            for gg in range(4):
                g = 2 * (q4 * 4 + gg) + gl
                S.op("dve", lambda: V.scalar_tensor_tensor(
                    out=ts_[:, gg, :], in0=ident[:, :], scalar=Dcol[:, g:g + 1], in1=ps[b][:, gg * 128:(gg + 1) * 128],
                    op0=ALU.mult, op1=ALU.add), r=[("ps", b), K, "ident"], w=[tk])
            S.dma("sp", lambda: nc.sync.dma_start(out=toepv[:, q4 * 4:(q4 + 1) * 4, gl, :], in_=ts_[:, :, :]),
                  r=[tk], w=["st_toep"])
        WS = [[sb(st, "WS", [128, 512], BF16) for _ in range(2)] for _ in range(2)]
        for gq in range(8):
            for ri in range(2):
                b = 4 + ri
                ws = WS[gq % 2][ri]
                wk = ("WS", gq % 2, ri)
                for gg in range(8):
                    g = gq * 8 + gg
                    pr, gl = g // 2, g % 2
                    S.op("pe", lambda: nc.tensor.matmul(
                        ps[b][:, gg * 64:(gg + 1) * 64], lhsT=W1A[:, ri, pr, :, :].rearrange("p k h -> p (k h)"),
                        rhs=identb[:, gl * 64:gl * 64 + 64], start=True, stop=True),
                        r=[K, "identb"], w=[("ps", b)], sig=(gg == 7))
                self.evac(ri, ws[:, :], ps[b][:, :], [("ps", b)], [wk])
                S.dma("sp", lambda: nc.sync.dma_start(out=st_w1[ri][:, gq * 512:(gq + 1) * 512], in_=ws[:, :]),
                      r=[wk], w=["st_w1"])


class S5Main:
    def __init__(self, nc, S, sb, ps, misc_bank, evac, env, X, us, mixedS):
        self.nc, self.S, self.sb, self.ps, self.misc_bank, self.evac, self.e = nc, S, sb, ps, misc_bank, evac, env
        self.X, self.us, self.mixedS = X, us, mixedS

    def run(self, mp):
        nc, S, sb, ps, e = self.nc, self.S, self.sb, self.ps, self.e
        misc_bank, evac = self.misc_bank, self.evac
        X, us, mixedS = self.X, self.us, self.mixedS
        V = nc.vector
        ident, identb, flg, glubT = e["ident"], e["identb"], e["flg"], e["glubT"]
        a_re, a_im, al_re, al_im = e["a_re"], e["a_im"], e["al_re"], e["al_im"]
        st_w1, st_toep, st_w2 = e["st_w1"], e["st_toep"], e["st_w2"]
        ags_in, ags_out = e["ags_in"], e["ags_out"]
        with contextlib.ExitStack() as s1:
            ZbB = sb(s1, "ZbB", [128, 2, 32, 128], BF16)
            GWb = sb(s1, "GWb", [128, 8, 128], BF16)
            with contextlib.ExitStack() as sg:
                GWf = sb(sg, "GWf", [128, 8, 128])
                S.op("pool", lambda: nc.gpsimd.memset(GWf[:].rearrange("p a b -> p (a b)"), 0.0), w=["GWf"])
                gwv = e["gluw"].rearrange("(c g) h k -> g h c k", g=8)
                for g8 in range(8):
                    S.dma("sp", lambda: nc.sync.dma_start(out=GWf[g8 * 16:(g8 + 1) * 16, :, g8 * 16:(g8 + 1) * 16], in_=gwv[g8]),
                          r=["GWf"], w=["GWf"])
                S.op("dve", lambda: V.tensor_copy(out=GWb[:], in_=GWf[:]), r=["GWf"], w=["GWb"])
                S.barrier()
            with contextlib.ExitStack() as sc:
                ZZ = sb(sc, "ZZ", [128, 129, 2, 32])
                w1r = [sb(sc, "w1r", [128, 2, 8, 64], BF16) for _ in range(2)]
                TB = sb(sc, "TB", [128, 32, 2, 32])
                tq = sb(sc, "tq", [128, 32, 32])
                al2 = sb(sc, "al2", [128, 7, 2, 32])
                AA = sb(sc, "AA", [128, 2, 32]); AB = sb(sc, "AB", [128, 2, 32])
                Tt = sb(sc, "Tt", [128, 2, 32]); Ut = sb(sc, "Ut", [128, 2, 32])
                Rin = sb(sc, "Rin", [128, 2, 32])
                TA = ZbB[:].rearrange("p a b c -> p (a b c)").bitcast(F32).rearrange("p (i r q) -> p i r q", i=64, r=2)

                def w1load(gb):
                    for ri in range(2):
                        S.dma("sp", lambda: nc.sync.dma_start(
                            out=w1r[gb % 2][:, ri, :, :],
                            in_=st_w1[ri][:, gb * 512:(gb + 1) * 512].rearrange("p (g c) -> p g c", c=64)),
                            r=["st_w1"], w=[("w1r", gb % 2)])
                w1load(0)
                for gb in range(8):
                    if gb + 1 < 8:
                        w1load(gb + 1)
                    sl = gb % 2
                    for ri in range(2):
                        b = ri * 2 + gb % 2
                        for gg in range(8):
                            g = gb * 8 + gg
                            pr, gl = g // 2, g % 2
                            pl = pr % 4
                            S.op("pe", lambda: nc.tensor.matmul(
                                ps[b][gl * 64:(gl + 1) * 64, pl * 128:(pl + 1) * 128], lhsT=w1r[sl][:, ri, gg, :],
                                rhs=X[:, g, :], start=True, stop=True),
                                r=[("w1r", sl), "X"], w=[("ps", b)], sig=(gg == 7))
                        evac(ri, ZZ[:, 1:129, ri, gb * 4:(gb + 1) * 4], ps[b][:, :].rearrange("p (a c) -> p c a", a=4),
                             [("ps", b)], ["ZZ"])
                KZ = "ZZ"

                def dv(fn, r=(), w=()):
                    S.op("dve", fn, r=[KZ] + list(r), w=[KZ] + list(w))
                dv(lambda: V.tensor_copy(out=al2[:, 0, 0, :], in_=al_re[:, :]))
                dv(lambda: V.tensor_copy(out=al2[:, 0, 1, :], in_=al_im[:, :]))
                for l in range(1, 7):
                    pr_, pi_ = al2[:, l - 1, 0, :], al2[:, l - 1, 1, :]
                    dv(lambda: V.tensor_tensor(out=Tt[:, 0, :], in0=pr_, in1=pr_, op=ALU.mult))
                    dv(lambda: V.tensor_tensor(out=Tt[:, 1, :], in0=pi_, in1=pi_, op=ALU.mult))
                    dv(lambda: V.tensor_tensor(out=al2[:, l, 0, :], in0=Tt[:, 0, :], in1=Tt[:, 1, :], op=ALU.subtract))
                    dv(lambda: V.tensor_tensor(out=Tt[:, 0, :], in0=pr_, in1=pi_, op=ALU.mult))
                    dv(lambda: V.tensor_scalar(out=al2[:, l, 1, :], in0=Tt[:, 0, :], scalar1=2.0, scalar2=None, op0=ALU.mult))
                dv(lambda: V.tensor_copy(out=AA[:, 0, :], in_=al_re[:, :]))
                dv(lambda: V.tensor_copy(out=AA[:, 1, :], in_=al_re[:, :]))
                dv(lambda: V.tensor_copy(out=AB[:, 0, :], in_=al_im[:, :]))
                dv(lambda: V.tensor_copy(out=AB[:, 1, :], in_=al_im[:, :]))

                def level(src, n_in, dst, l, d0=0):
                    n = n_in // 2
                    sv = src.rearrange("p (i two) r q -> p i two r q", two=2)
                    Er, Ei, Or, Oi = sv[:, :, 0, 0, :], sv[:, :, 0, 1, :], sv[:, :, 1, 0, :], sv[:, :, 1, 1, :]
                    Dr, Di = dst[:, d0:d0 + n, 0, :], dst[:, d0:d0 + n, 1, :]
                    ar = al2[:, l, 0, :].unsqueeze(1).to_broadcast([128, n, 32])
                    ai = al2[:, l, 1, :].unsqueeze(1).to_broadcast([128, n, 32])
                    tt = tq[:, 0:n, :]
                    dv(lambda: V.tensor_tensor(out=Dr, in0=Er, in1=ar, op=ALU.mult), r=["ZbB"], w=["ZbB"])
                    dv(lambda: V.tensor_tensor(out=tt, in0=Ei, in1=ai, op=ALU.mult), r=["ZbB"], w=["ZbB"])
                    dv(lambda: V.tensor_tensor(out=Dr, in0=Dr, in1=tt, op=ALU.subtract), r=["ZbB"], w=["ZbB"])
                    dv(lambda: V.tensor_tensor(out=Dr, in0=Dr, in1=Or, op=ALU.add), r=["ZbB"], w=["ZbB"])
                    dv(lambda: V.tensor_tensor(out=Di, in0=Er, in1=ai, op=ALU.mult), r=["ZbB"], w=["ZbB"])
                    dv(lambda: V.tensor_tensor(out=tt, in0=Ei, in1=ar, op=ALU.mult), r=["ZbB"], w=["ZbB"])
                    dv(lambda: V.tensor_tensor(out=Di, in0=Di, in1=tt, op=ALU.add), r=["ZbB"], w=["ZbB"])
                    dv(lambda: V.tensor_tensor(out=Di, in0=Di, in1=Oi, op=ALU.add), r=["ZbB"], w=["ZbB"])
                level(ZZ[:, 1:65, :, :], 64, TA, 0, 0)
                level(ZZ[:, 65:129, :, :], 64, TA, 0, 32)
                level(TA[:, 0:64], 64, TB, 1)
                level(TB[:, 0:32], 32, TA, 2)
                level(TA[:, 0:16], 16, TB, 3)
                level(TB[:, 0:8], 8, TA, 4)
                level(TA[:, 0:4], 4, TB, 5)
                level(TB[:, 0:2], 2, TA, 6)
                S.dma("sp", lambda: nc.sync.dma_start(out=ags_in, in_=TA[:, 0, :, :].rearrange("p r q -> p (r q)")),
                      r=[KZ, "ZbB"], w=["ags_in"])
                S.cc(lambda: nc.gpsimd.collective_compute("AllGather", ALU.bypass, replica_groups=e["PAIRS"],
                                                          ins=[ags_in.opt()], outs=[ags_out.opt()]),
                     r=["ags_in"], w=["ags_out"])
                S.dma("sp", lambda: nc.sync.dma_start(out=Rin[:, :, :].rearrange("p r q -> p (r q)"), in_=ags_out[0:128, :]),
                      r=["ags_out"], w=["Rin"])
                dv(lambda: V.tensor_scalar(out=ZZ[:, 0, :, :], in0=Rin[:, :, :], scalar1=flg[:, 0:1], scalar2=None,
                                           op0=ALU.mult), r=["Rin", "flg"])
                q4 = sb(sc, "q4", [128, 4, 32])

                def dvn(fn, first=False):
                    S.op("dve", fn, r=[KZ], w=[KZ], skip_same=not first)
                ar_, ai_ = AA[:, 0, :], AB[:, 0, :]
                for c in range(1, 129):
                    zr, zi = ZZ[:, c - 1, 0, :], ZZ[:, c - 1, 1, :]
                    dvn(lambda: V.tensor_tensor(out=q4[:, 0, :], in0=zr, in1=ar_, op=ALU.mult), first=(c == 1))
                    dvn(lambda: V.tensor_tensor(out=q4[:, 1, :], in0=zi, in1=ai_, op=ALU.mult))
                    dvn(lambda: V.tensor_tensor(out=q4[:, 2, :], in0=zr, in1=ai_, op=ALU.mult))
                    dvn(lambda: V.tensor_tensor(out=q4[:, 3, :], in0=zi, in1=ar_, op=ALU.mult))
                    dvn(lambda: V.tensor_tensor(out=ZZ[:, c, 0, :], in0=ZZ[:, c, 0, :], in1=q4[:, 0, :], op=ALU.add))
                    dvn(lambda: V.tensor_tensor(out=ZZ[:, c, 1, :], in0=ZZ[:, c, 1, :], in1=q4[:, 2, :], op=ALU.add))
                    dvn(lambda: V.tensor_tensor(out=ZZ[:, c, 0, :], in0=ZZ[:, c, 0, :], in1=q4[:, 1, :], op=ALU.subtract))
                    dvn(lambda: V.tensor_tensor(out=ZZ[:, c, 1, :], in0=ZZ[:, c, 1, :], in1=q4[:, 3, :], op=ALU.add))
                dv(lambda: V.tensor_copy(out=Tt[:], in_=ZZ[:, 128, :, :]))
                S.dma("sp", lambda: nc.sync.dma_start(out=e["nssm_p"], in_=ZZ[:, 128, :, :]), r=[KZ])
                S.op("dve", lambda: V.tensor_copy(out=ZbB[:, 0, :, :], in_=ZZ[:, 0:128, 0, :].rearrange("p c q -> p q c")),
                     r=[KZ, "ZbB"], w=["ZbB"])
                S.op("act", lambda: nc.scalar.copy(out=ZbB[:, 1, :, :], in_=ZZ[:, 0:128, 1, :].rearrange("p c q -> p q c")),
                     r=[KZ, "ZbB"], w=["ZbB"])
                S.barrier()
            if S5STAGE[0] < 2:
                S.op("dve", lambda: V.memset(mixedS[:, :, :], 0.0), w=["mixedS"])
                S.barrier()
                return
            with contextlib.ExitStack() as ss:
                w1s = sb(ss, "w1s", [16, 2, 64, 64], BF16)
                toeps = sb(ss, "toeps", [128, 64, 16], BF16)
                w2s = sb(ss, "w2s", [128, 2, 64, 16], BF16)
                SN = sb(ss, "SN", [128, 2, 32, NS])
                q1 = sb(ss, "q1", [128, 32, NS]); q2 = sb(ss, "q2", [128, 32, NS])
                gys = sb(ss, "gys", [16, 1024], BF16)
                S0A = sb(ss, "S0A", [128, 2, 32, NS])
                S0b = sb(ss, "S0b", [128, 2, 32, NS], BF16)
                usTb = sb(ss, "usTb", [128, 64, NS], BF16)
                S.dma("sp", lambda: nc.sync.dma_start(out=S0A[:, :, :, :], in_=e["s0A_d"]), w=["S0A"])
                S.op("dve", lambda: V.tensor_copy(out=S0b[:], in_=S0A[:]), r=["S0A"], w=["S0b"])
                S.op("pool", lambda: nc.gpsimd.memset(usTb[:].rearrange("p a b -> p (a b)"), 0.0), w=["usTb"])
                for hb in range(2):
                    b = misc_bank()
                    for gg in range(32):
                        g = hb * 32 + gg
                        S.op("pe", lambda: nc.tensor.transpose(
                            out=ps[b][0:16, gg * 16:(gg + 1) * 16], in_=us[:NS, g * 16:(g + 1) * 16], identity=ident[:NS, :NS]),
                            r=["us", "ident"], w=[("ps", b)], sig=(gg == 31))
                    S.op("dve", lambda: V.tensor_copy(out=usTb[0:16, hb * 32:(hb + 1) * 32, :],
                                                      in_=ps[b][0:16, :].rearrange("p (g t) -> p g t", t=NS)),
                         r=[("ps", b), "usTb"], w=["usTb"])
                S.op("pool", lambda: nc.gpsimd.memset(toeps[:].rearrange("p a b -> p (a b)"), 0.0), w=["toeps"])
                S.op("pool", lambda: nc.gpsimd.memset(w2s[:].rearrange("p a b c -> p (a b c)"), 0.0), w=["w2s"])
                for ri in range(2):
                    S.dma("sp", lambda: nc.sync.dma_start(
                        out=w1s[:, ri, :, :], in_=st_w1[ri][0:16, :].rearrange("p (g c) -> p g c", c=64)), r=["st_w1"], w=["w1s"])
                    for gl in range(2):
                        S.dma("sp", lambda: nc.sync.dma_start(
                            out=w2s[gl * 64:(gl + 1) * 64, ri, :, :].rearrange("p (r two) h -> p r two h", two=2)[:, :, gl, :],
                            in_=st_w2[ri][gl * 64:(gl + 1) * 64, :].rearrange("p (r c) -> p r c", c=128)[:, :, 112:128]),
                            r=["st_w2", "w2s"], w=["w2s"])
                S.dma("sp", lambda: nc.sync.dma_start(
                    out=toeps[0:16, :, :], in_=st_toep[0:16, :].rearrange("p (g c) -> p g c", c=128)[:, :, 0:16]),
                    r=["st_toep", "toeps"], w=["toeps"])

                def bt(ap32):
                    return ap32.unsqueeze(2).to_broadcast([128, 32, NS])
                for ri in range(2):
                    for g in range(64):
                        pr, gl = g // 2, g % 2
                        S.op("pe", lambda: nc.tensor.matmul(
                            ps[ri][gl * 64:(gl + 1) * 64, pr * 16:(pr + 1) * 16], lhsT=w1s[0:16, ri, g, :], rhs=usTb[0:16, g, :],
                            start=True, stop=True), r=["w1s", "usTb"], w=[("ps", ri)], sig=(g == 63))
                KS = "SN"

                def dv2(fn, r=(), w=()):
                    S.op("dve", fn, r=[KS] + list(r), w=[KS] + list(w))
                dv2(lambda: V.tensor_tensor(out=q1[:], in0=S0A[:, 0], in1=bt(a_re[:, :]), op=ALU.mult), r=["S0A"])
                dv2(lambda: V.tensor_tensor(out=q2[:], in0=S0A[:, 1], in1=bt(a_im[:, :]), op=ALU.mult))
                dv2(lambda: V.tensor_tensor(out=SN[:, 0], in0=q1[:], in1=q2[:], op=ALU.subtract))
                dv2(lambda: V.tensor_tensor(out=SN[:, 0], in0=SN[:, 0], in1=ps[0][:, :].rearrange("p (r t) -> p r t", t=NS),
                                            op=ALU.add), r=[("ps", 0)])
                dv2(lambda: V.tensor_tensor(out=q1[:], in0=S0A[:, 0], in1=bt(a_im[:, :]), op=ALU.mult))
                dv2(lambda: V.tensor_tensor(out=q2[:], in0=S0A[:, 1], in1=bt(a_re[:, :]), op=ALU.mult))
                dv2(lambda: V.tensor_tensor(out=SN[:, 1], in0=q1[:], in1=q2[:], op=ALU.add))
                dv2(lambda: V.tensor_tensor(out=SN[:, 1], in0=SN[:, 1], in1=ps[1][:, :].rearrange("p (r t) -> p r t", t=NS),
                                            op=ALU.add), r=[("ps", 1)])
                S.dma("sp", lambda: nc.sync.dma_start(out=e["nssm_s"], in_=SN[:, :, :, :]), r=[KS])
                for g in range(64):
                    pr, gl = g // 2, g % 2
                    b = 2 + g // 32
                    o = ps[b][0:16, (g % 32) * 16:(g % 32 + 1) * 16]
                    S.op("pe", lambda: nc.tensor.matmul(o, lhsT=usTb[:, g, :], rhs=toeps[:, g, :], start=True, stop=False),
                         r=["usTb", "toeps"], w=[("ps", b)], sig=False)
                    S.op("pe", lambda: nc.tensor.matmul(o, lhsT=S0b[:, 0, pr, :], rhs=w2s[:, 0, g, :], start=False, stop=False),
                         r=["S0b", "w2s"], w=[("ps", b)], sig=False)
                    S.op("pe", lambda: nc.tensor.matmul(o, lhsT=S0b[:, 1, pr, :], rhs=w2s[:, 1, g, :], start=False, stop=True),
                         r=["S0b", "w2s"], w=[("ps", b)], sig=(g % 32 == 31))
                for hb in range(2):
                    S.op("act", lambda: nc.scalar.activation(out=gys[:, hb * 512:(hb + 1) * 512], in_=ps[2 + hb][0:16, :],
                                                             func=ACT.Gelu_apprx_tanh), r=[("ps", 2 + hb)], w=["gys"])
                b = misc_bank()
                for ch in range(8):
                    S.op("pe", lambda: nc.tensor.matmul(ps[b][:, ch * 16:(ch + 1) * 16], lhsT=gys[0:16, ch * 128:(ch + 1) * 128],
                                                        rhs=identb[0:16, 0:16], start=True, stop=True),
                         r=["gys", "identb"], w=[("ps", b)], sig=(ch == 7))
                S.op("dve", lambda: V.tensor_copy(out=mixedS[:, :, NPT:NT], in_=ps[b][:, 0:128].rearrange("p (c t) -> p c t", t=NS)),
                     r=[("ps", b)], w=["mixedS"])
                S.barrier()
            if S5STAGE[0] < 3:
                S.op("dve", lambda: V.memset(mixedS[:, :, 0:NPT], 0.0), w=["mixedS"])
                S.barrier()
                return
            with contextlib.ExitStack() as so:
                toepr = [sb(so, "toepr", [128, 8, 128], BF16) for _ in range(2)]
                w2r = [sb(so, "w2r", [128, 2, 8, 128], BF16) for _ in range(2)]
                gy8 = [sb(so, "gy8", [128, 8, 128], BF16) for _ in range(2)]
                sgt = [sb(so, "sgt", [128, 512], BF16) for _ in range(2)]
                for t_ in w2r:
                    S.op("dve", lambda: V.memset(t_[:].rearrange("p a b c -> p (a b c)"), 0.0), w=["w2init"])
                S.barrier()

                def oload(ch8):
                    sl = ch8 % 2
                    S.dma("sp", lambda: nc.sync.dma_start(
                        out=toepr[sl][:, :, :], in_=st_toep[:, ch8 * 1024:(ch8 + 1) * 1024].rearrange("p (g c) -> p g c", c=128)),
                        r=["st_toep"], w=[("toepr", sl)])
                    for ri in range(2):
                        for gl in range(2):
                            S.dma("sp", lambda: nc.sync.dma_start(
                                out=w2r[sl][gl * 64:(gl + 1) * 64, ri, :, :].rearrange("p (r two) c -> p r two c", two=2)[:, :, gl, :],
                                in_=st_w2[ri][gl * 64:(gl + 1) * 64, ch8 * 512:(ch8 + 1) * 512].rearrange("p (r c) -> p r c", c=128)),
                                r=["st_w2"], w=[("w2r", sl)])
                oload(0)
                si = 0
                for ch8 in range(8):
                    if ch8 + 1 < 8:
                        oload(ch8 + 1)
                    sl = ch8 % 2
                    gy = gy8[ch8 % 2]
                    gk = ("gy8", ch8 % 2)
                    for half in range(2):
                        b = half
                        for gg in range(4):
                            g8 = half * 4 + gg
                            g = ch8 * 8 + g8
                            pr = g // 2
                            o = ps[b][:, gg * 128:(gg + 1) * 128]
                            S.op("pe", lambda: nc.tensor.matmul(o, lhsT=X[:, g, :], rhs=toepr[sl][:, g8, :], start=True, stop=False),
                                 r=["X", ("toepr", sl)], w=[("ps", b)], sig=False)
                            S.op("pe", lambda: nc.tensor.matmul(o, lhsT=ZbB[:, 0, pr, :], rhs=w2r[sl][:, 0, g8, :], start=False, stop=False),
                                 r=["ZbB", ("w2r", sl)], w=[("ps", b)], sig=False)
                            S.op("pe", lambda: nc.tensor.matmul(o, lhsT=ZbB[:, 1, pr, :], rhs=w2r[sl][:, 1, g8, :], start=False, stop=True),
                                 r=["ZbB", ("w2r", sl)], w=[("ps", b)], sig=(gg == 3))
                        S.op("act", lambda: nc.scalar.activation(
                            out=gy[:, :, half * 64:(half + 1) * 64].rearrange("p j (g h) -> p g j h", g=4),
                            in_=ps[b][:, :].rearrange("p (g j h) -> p g j h", g=4, j=8), func=ACT.Gelu_apprx_tanh),
                            r=[("ps", b)], w=[gk])
                    for hf in range(2):
                        b = 2 + hf
                        for i in range(4):
                            jj = (3 - i) if hf == 1 else (7 - i)
                            S.op("pe", lambda: nc.tensor.matmul(ps[b][:, i * 128:(i + 1) * 128], lhsT=gy[:, jj, :], rhs=identb[:, :],
                                                                start=True, stop=True),
                                 r=[gk, "identb"], w=[("ps", b)], sig=(i == 3))
                        c0 = 512 if hf == 1 else 0
                        evac(hf, mixedS[:, ch8, c0:c0 + 512], ps[b][:, :], [("ps", b)], ["mixedS"])
                    for ti, (c0, n) in enumerate(TILES):
                        b = 4 + si % 2
                        sg = sgt[si % 2]
                        sk = ("sgt", si % 2)
                        si += 1
                        S.op("pe", lambda: nc.tensor.matmul(ps[b][:, 0:n], lhsT=GWb[:, ch8, :], rhs=mixedS[:, ch8, c0:c0 + n],
                                                            start=True, stop=True), r=["GWb", "mixedS"], w=[("ps", b)])
                        S.op("act", lambda: nc.scalar.activation(out=sg[:, 0:n], in_=ps[b][:, 0:n], func=ACT.Sigmoid,
                                                                 bias=glubT[:, ch8:ch8 + 1], scale=1.0),
                             r=[("ps", b), "glubT"], w=[sk])
                        S.op("dve", lambda: V.tensor_tensor(out=mixedS[:, ch8, c0:c0 + n], in0=mixedS[:, ch8, c0:c0 + n],
                                                            in1=sg[:, 0:n], op=ALU.mult), r=[sk, "mixedS"], w=["mixedS"])
                    if S5STAGE[0] == 4:
                        S.barrier()
                S.barrier()


_CACHE = {}


def _get_program(with_s5=True):
    if with_s5 not in _CACHE:
        _CACHE[with_s5] = build_program(with_s5=with_s5)
    return _CACHE[with_s5]


def make_in_maps(inp):
    f = lambda a: np.ascontiguousarray(a, dtype=np.float32)
    norms = np.concatenate([inp["ffn1_norm"][0], inp["mix_norm"][0], inp["ffn2_norm"][0],
                            inp["final_norm"]]).reshape(4 * KC, 128)
    ada_w = inp["ada_w"][0]
    def layA(x):
        sh = x.shape[2:]
        x = x.reshape((32, 2, 64) + sh)
        x = np.moveaxis(x, 0, 2)
        return x.reshape((128, 32) + sh)
    lamA = np.stack([layA(inp["ssm_lambda_re"][0]), layA(inp["ssm_lambda_im"][0]),
                     layA(np.repeat(inp["ssm_log_dt"][0][:, None], 64, axis=1))], axis=1)
    BA = np.stack([layA(inp["ssm_b_re"][0]), layA(inp["ssm_b_im"][0])], axis=1)
    CA = np.stack([layA(np.swapaxes(inp["ssm_c_re"][0], 1, 2)), layA(np.swapaxes(inp["ssm_c_im"][0], 1, 2))], axis=1)
    Dcol = np.tile(inp["ssm_d"][0].T, (8, 1))
    shared = {
        "call": f(np.concatenate([inp["c_sample"], inp["c_prompt"]], axis=0)),
        "ada_b": f(inp["ada_b"][0].reshape(9 * D // 128, 128)),
        "norms": f(norms),
        "w_g0": f(inp["ffn1_w_gate"][0]), "w_u0": f(inp["ffn1_w_up"][0]), "w_d0": f(inp["ffn1_w_down"][0]),
        "w_g1": f(inp["ffn2_w_gate"][0]), "w_u1": f(inp["ffn2_w_up"][0]), "w_d1": f(inp["ffn2_w_down"][0]),
        "w_in": f(inp["w_in"][0]), "w_out": f(inp["w_out"][0]),
        "pool_w": f(inp["pool_w"][0]),
        "pvec": f(np.concatenate([inp["pool_b"][0].reshape(8, 128), inp["pool_scale"][0].reshape(8, 128)], axis=0)),
        "lamA": f(lamA), "BA": f(BA), "CA": f(CA), "Dcol": f(Dcol),
        "gluw": f(inp["ssm_glu_w"][0]),
        "glub": f(inp["ssm_glu_b"][0].reshape(8, 128)),
    }
    maps = []
    for c in range(NCORES):
        b, h = c // 2, c % 2
        m = dict(shared)
        m["xp"] = f(inp["x_prompt"][b, h * NPT:(h + 1) * NPT])
        m["xs"] = f(inp["x_sample"][c * NS:(c + 1) * NS, 0])
        m["adaA"] = f(ada_w[:, c * 512:(c + 1) * 512])
        m["adaB"] = f(ada_w[:, 4096 + c * 1792:4096 + (c + 1) * 1792])
        s1 = np.zeros((128, 17), np.float32)
        for s in range(NS):
            s1[c * NS + s, 1 + s] = 1.0
        s2 = np.zeros((4, 17), np.float32)
        s2[b, 0] = 1.0
        m["sel1"] = s1
        m["sel2"] = s2
        fl = np.zeros((128, 2), np.float32)
        fl[:, 0] = float(h)
        fl[:, 1] = 1.0 - float(h)
        m["flags"] = fl
        m["spool"] = f(inp["state_pool"][0, c * NS:(c + 1) * NS])
        sre = np.moveaxis(inp["state_ssm_re"][0, c * NS:(c + 1) * NS], 0, 2)
        sim = np.moveaxis(inp["state_ssm_im"][0, c * NS:(c + 1) * NS], 0, 2)
        m["s0A"] = f(np.stack([layA(sre), layA(sim)], axis=1))
        maps.append(m)
    return maps


def assemble(R):
    y_prompt = np.zeros((4, 2048, D), np.float32)
    y_sample = np.zeros((128, 1, D), np.float32)
    re_p = np.zeros((1, 4, 64, 64), np.float32)
    im_p = np.zeros((1, 4, 64, 64), np.float32)
    pool_p = np.zeros((1, 4, 15, 1024), np.float32)
    re_s = np.zeros((1, 128, 64, 64), np.float32)
    im_s = np.zeros((1, 128, 64, 64), np.float32)
    pool_s = np.zeros((1, 128, 15, 1024), np.float32)
    for c in range(NCORES):
        b, h = c // 2, c % 2
        y_prompt[b, h * NPT:(h + 1) * NPT] = R[c]["yp"]
        y_sample[c * NS:(c + 1) * NS, 0] = R[c]["ys"]
        def unA(x):
            sh = x.shape[2:]
            x = x.reshape((2, 64, 32) + sh)
            x = np.moveaxis(x, 2, 0)
            return x.reshape((64, 64) + sh)
        if h == 1:
            re_p[0, b] = unA(R[c]["nssm_p"][:, 0])
            im_p[0, b] = unA(R[c]["nssm_p"][:, 1])
            pool_p[0, b] = R[c]["npool_p"][1:16]
        re_s[0, c * NS:(c + 1) * NS] = np.moveaxis(unA(R[c]["nssm_s"][:, 0]), 2, 0)
        im_s[0, c * NS:(c + 1) * NS] = np.moveaxis(unA(R[c]["nssm_s"][:, 1]), 2, 0)
        pool_s[0, c * NS:(c + 1) * NS] = R[c]["npool_s"]
    return (y_prompt, y_sample, re_p, im_p, pool_p, re_s, im_s, pool_s)


def kernel(**inputs):
    inp = {k: np.asarray(v) for k, v in inputs.items()}
    nc = _get_program()
    maps = make_in_maps(inp)
    res = run_bass_kernel_spmd(nc, maps, core_ids=list(range(NCORES)))
    return assemble(res.results)
```

```python
import contextlib
import numpy as np
import concourse.bass as bass
import concourse.mybir as mybir
from concourse.bass_utils import run_bass_kernel_spmd

F32 = mybir.dt.float32
BF16 = mybir.dt.bfloat16
I32 = mybir.dt.int32
ACT = mybir.ActivationFunctionType
ALU = mybir.AluOpType

D = 2048
KC = 16
FF = 5632
FC = 44
NPT = 1024
NS = 16
NT = NPT + NS
TILES = [(0, 512), (512, 512), (1024, 16)]
NPART = 4
FPP = FC // NPART
EPS = 1e-6
NCORES = 8


class Sched:
    ENGS = ("pe", "act", "dve", "pool", "sp")

    def __init__(self, nc, es, n_dma_sems=12):
        self.nc = nc
        self.es = es
        self.ncc = 0
        self.eng = {"pe": nc.tensor, "act": nc.scalar, "dve": nc.vector,
                    "pool": nc.gpsimd, "sp": nc.sync}
        self.sem = {e: es.enter_context(nc.semaphore("s_" + e)) for e in self.ENGS}
        self.cnt = {e: 0 for e in self.ENGS}
        self.dsem = {}
        for q in ("sp", "pool", "act"):
            n = n_dma_sems if q != "act" else 4
            self.dsem[q] = [[es.enter_context(nc.semaphore("d_%s%d" % (q, i))), 0]
                            for i in range(n)]
        self.dnext = {q: 0 for q in self.dsem}
        self.waited = {e: {} for e in self.ENGS}
        self.last_w = {}
        self.readers = {}
        self.all_dma = []
        self.last_ev = {e: None for e in self.ENGS}

    def _deps(self, r, w):
        deps = []
        for k in r:
            ev = self.last_w.get(k)
            if ev is not None:
                deps.append(ev)
        for k in w:
            ev = self.last_w.get(k)
            if ev is not None:
                deps.append(ev)
            deps.extend(self.readers.get(k, ()))
        return deps

    def _emit_waits(self, e, deps):
        best = {}
        for ev in deps:
            if ev is None:
                continue
            sem, val, src, sid = ev
            if src == "pe" and e == "pe":
                continue
            if sid not in best or best[sid][1] < val:
                best[sid] = ev
        for sid, (sem, val, src, _) in best.items():
            if src is not None and val > self.cnt[src]:
                raise RuntimeError("wait on future event %s %d>%d" % (src, val, self.cnt[src]))
            if self.waited[e].get(sid, 0) >= val:
                continue
            self.eng[e].wait_ge(sem, val)
            self.waited[e][sid] = val

    def _record(self, ev, r, w):
        for k in r:
            self.readers.setdefault(k, []).append(ev)
        for k in w:
            self.last_w[k] = ev
            self.readers[k] = []

    def op(self, e, fn, r=(), w=(), sig=True, extra=(), skip_same=False):
        deps = self._deps(r, w) + list(extra)
        if skip_same:
            deps = [d for d in deps if d is not None and d[2] != e]
        self._emit_waits(e, deps)
        ins = fn()
        if sig:
            self.cnt[e] += 1
            ins.then_inc(self.sem[e], 1)
            ev = (self.sem[e], self.cnt[e], e, "c_" + e)
        else:
            ev = (self.sem[e], self.cnt[e] + 1, e, "c_" + e)
        self.last_ev[e] = ev
        self._record(ev, r, w)
        return ev

    def dma(self, q, fn, r=(), w=(), extra=(), inc=16):
        deps = self._deps(r, w) + list(extra)
        ring = self.dsem[q]
        i = self.dnext[q]
        self.dnext[q] = (i + 1) % len(ring)
        slot = ring[i]
        sid = "d_%s%d" % (q, i)
        if slot[1] > 0:
            deps.append((slot[0], slot[1], None, sid))
        self._emit_waits(q, deps)
        ins = fn()
        slot[1] += inc
        ins.then_inc(slot[0], inc)
        ev = (slot[0], slot[1], None, sid)
        self.all_dma.append(ev)
        self._record(ev, r, w)
        return ev

    def cc(self, fn, r=(), w=()):
        self.barrier_keep()
        deps = self._deps(r, w)
        self._emit_waits("pool", deps)
        sem = self.es.enter_context(self.nc.semaphore("cc%d" % self.ncc))
        sid = "cc%d" % self.ncc
        self.ncc += 1
        ins = fn()
        ins.then_inc(sem)
        ev = (sem, 1, None, sid)
        self.all_dma.append(ev)
        self._record(ev, r, w)
        self.barrier_keep()
        return ev

    def barrier_keep(self):
        evs = [self.last_ev[e] for e in self.ENGS if self.last_ev[e] is not None]
        evs += self.all_dma
        for e in self.ENGS:
            self._emit_waits(e, [ev for ev in evs if not (ev[2] == e == "pe")])

    def barrier(self):
        evs = [self.last_ev[e] for e in self.ENGS if self.last_ev[e] is not None]
        evs += self.all_dma
        for e in self.ENGS:
            self._emit_waits(e, [ev for ev in evs if not (ev[2] == e == "pe")])
        self.all_dma = []
        self.last_w = {}
        self.readers = {}

    def finish(self):
        self._emit_waits("sp", self.all_dma)


WINS = (2, 4, 8, 16)
TWO_PI = 6.283185307179586


def build_program(debug=None, with_s5=True, s5main=True):
    nc = bass.Bass("TRN2", target_bir_lowering=False)

    def din(name, shape, dt=F32):
        return nc.dram_tensor(name, list(shape), dt, kind="ExternalInput").ap()

    def dout(name, shape, dt=F32):
        return nc.dram_tensor(name, list(shape), dt, kind="ExternalOutput").ap()

    def dint(name, shape, dt=F32):
        return nc.dram_tensor(name, list(shape), dt).ap()

    xp = din("xp", [NPT, D])
    xs = din("xs", [NS, D])
    call = din("call", [132, D])
    adaA = din("adaA", [D, 512])
    adaB = din("adaB", [D, 1792])
    ada_b = din("ada_b", [9 * D // 128, 128])
    sel1 = din("sel1", [128, 17])
    sel2 = din("sel2", [4, 17])
    flags = din("flags", [128, 2])
    norms = din("norms", [4 * KC, 128])
    w_g = [din("w_g%d" % i, [D, FF]) for i in range(2)]
    w_u = [din("w_u%d" % i, [D, FF]) for i in range(2)]
    w_d = [din("w_d%d" % i, [FF, D]) for i in range(2)]
    w_in = din("w_in", [D, D])
    w_out = din("w_out", [D, D])
    pool_w = din("pool_w", [4, 256, 256])
    pvec = din("pvec", [16, 128])
    spool = din("spool", [NS, 15, 1024])
    lamA_d = din("lamA", [128, 3, 32])
    BA_d = din("BA", [128, 2, 32, 16])
    CA_d = din("CA", [128, 2, 32, 16])
    Dcol_d = din("Dcol", [128, 64])
    gluw = din("gluw", [64, 16, 16])
    glub = din("glub", [8, 128])
    s0A_d = din("s0A", [128, 2, 32, NS])

    yp = dout("yp", [NPT, D])
    ys = dout("ys", [NS, D])
    nssm_p = dout("nssm_p", [128, 2, 32])
    npool_p = dout("npool_p", [16, 1024])
    nssm_s = dout("nssm_s", [128, 2, 32, NS])
    npool_s = dout("npool_s", [NS, 15, 1024])

    ag1_in = dint("ag1_in", [132, 512]); ag1_out = dint("ag1_out", [8 * 132, 512])
    ag2_in = dint("ag2_in", [132, 1792]); ag2_out = dint("ag2_out", [8 * 132, 1792])
    agv_in = dint("agv_in", [16, 1024]); agv_out = dint("agv_out", [32, 1024])
    ags_in = dint("ags_in", [128, 64]); ags_out = dint("ags_out", [256, 64])
    st_w1 = dint("st_w1", [2, 128, 64 * 64], BF16)
    st_toep = dint("st_toep", [128, 64 * 128], BF16)
    st_w2 = dint("st_w2", [2, 128, 32 * 128], BF16)

    PAIRS = [[0, 1], [2, 3], [4, 5], [6, 7]]
    ALL8 = [list(range(8))]

    es = contextlib.ExitStack()
    with es:
        S = Sched(nc, es)
        uid = [0]

        def sb(st, name, shape, dt=F32):
            uid[0] += 1
            return st.enter_context(nc.sbuf_tensor("%s_%d" % (name, uid[0]), list(shape), dt))

        ps = [es.enter_context(nc.psum_tensor("ps%d" % i, [128, 512], F32)) for i in range(8)]

        xT = sb(es, "xT", [128, KC, NT])
        ident = sb(es, "ident", [128, 128])
        identb = sb(es, "identb", [128, 128], BF16)
        onesb = sb(es, "onesb", [128, 128], BF16)
        iot = sb(es, "iot", [128, 128], I32)
        adab = sb(es, "adab", [128, 9 * KC])
        normw = sb(es, "normw", [128, 4 * KC])
        flg = sb(es, "flg", [128, 2])
        gsP = sb(es, "gsP", [128, 3, KC])
        shP = sb(es, "shP", [128, 3, KC])
        gtP = sb(es, "gtP", [128, 3, KC])
        gsS = sb(es, "gsS", [128, 3, KC, NS])
        shS = sb(es, "shS", [128, 3, KC, NS])
        gtS = sb(es, "gtS", [128, 3, KC, NS])
        epsb = sb(es, "epsb", [128, 1])
        pvT = sb(es, "pvT", [128, 16])
        Fw = sb(es, "Fw", [128, 4, 8, 2])
        a_re = sb(es, "a_re", [128, 32]); a_im = sb(es, "a_im", [128, 32])
        al_re = sb(es, "al_re", [128, 32]); al_im = sb(es, "al_im", [128, 32])
        glubT = sb(es, "glubT", [128, 8])

        S.op("pool", lambda: nc.gpsimd.iota(iot[:], pattern=[[1, 128]], base=0, channel_multiplier=-1), w=["iot"])
        S.op("dve", lambda: nc.vector.tensor_single_scalar(out=ident[:], in_=iot[:], scalar=0, op=ALU.is_equal),
             r=["iot"], w=["ident"])
        S.op("dve", lambda: nc.vector.tensor_copy(out=identb[:], in_=ident[:]), r=["ident"], w=["identb"])
        S.op("dve", lambda: nc.vector.memset(onesb[:], 1.0), w=["onesb"])
        S.op("dve", lambda: nc.vector.memset(epsb[:], EPS), w=["epsb"])
        S.dma("sp", lambda: nc.sync.dma_start(out=flg[:, :], in_=flags), w=["flg"])

        psrr = [0]

        def misc_bank():
            b = 6 + (psrr[0] % 2)
            psrr[0] += 1
            return b

        def evac(i, dst, src, r, w):
            if i % 2 == 0:
                S.op("dve", lambda: nc.vector.tensor_copy(out=dst, in_=src), r=r, w=w)
            else:
                S.op("act", lambda: nc.scalar.copy(out=dst, in_=src), r=r, w=w)

        def rows_to_featmajor(st, src_rows, nrows, dst, dkey, tag):
            stg = sb(st, "stg" + tag, [128, 128])
            S.dma("sp", lambda: nc.sync.dma_start(out=stg[:nrows, :], in_=src_rows), w=["stg" + tag])
            b = misc_bank()
            S.op("pe", lambda: nc.tensor.transpose(out=ps[b][:, 0:nrows], in_=stg[:nrows, :],
                                                   identity=ident[:nrows, :nrows]),
                 r=["stg" + tag, "ident"], w=[("ps", b)])
            S.op("dve", lambda: nc.vector.tensor_copy(out=dst, in_=ps[b][:, 0:nrows]), r=[("ps", b)], w=[dkey])

        modscope = contextlib.ExitStack()
        es.enter_context(modscope)
        modT = sb(modscope, "modT", [128, 9 * KC, 17])

        def derive(sub, kinds):
            base = sub * 3 * KC
            sc = modT[:, base + KC:base + 2 * KC, :]
            sh = modT[:, base:base + KC, :]
            gt = modT[:, base + 2 * KC:base + 3 * KC, :]
            nw = normw[:, sub * KC:(sub + 1) * KC]
            if "ss" in kinds:
                S.op("dve", lambda: nc.vector.scalar_tensor_tensor(
                    out=gsS[:, sub, :, :], in0=sc[:, :, 1:17], scalar=1.0,
                    in1=nw.unsqueeze(2).to_broadcast([128, KC, NS]), op0=ALU.add, op1=ALU.mult),
                    r=["modT", "normw"], w=["gsS"])
                S.op("dve", lambda: nc.vector.scalar_tensor_tensor(
                    out=gsP[:, sub, :], in0=sc[:, :, 0], scalar=1.0, in1=nw, op0=ALU.add, op1=ALU.mult),
                    r=["modT", "normw"], w=["gsP"])
                S.op("dve", lambda: nc.vector.tensor_copy(out=shS[:, sub, :, :], in_=sh[:, :, 1:17]),
                     r=["modT"], w=["shS"])
                S.op("dve", lambda: nc.vector.tensor_copy(out=shP[:, sub, :], in_=sh[:, :, 0]),
                     r=["modT"], w=["shP"])
            if "g" in kinds:
                gm = 1.0 if sub == 1 else 0.5
                S.op("dve", lambda: nc.vector.tensor_scalar(
                    out=gtS[:, sub, :, :], in0=gt[:, :, 1:17], scalar1=gm, scalar2=None, op0=ALU.mult),
                    r=["modT"], w=["gtS"])
                S.op("dve", lambda: nc.vector.tensor_scalar(
                    out=gtP[:, sub, :], in0=gt[:, :, 0], scalar1=gm, scalar2=None, op0=ALU.mult),
                    r=["modT"], w=["gtP"])

        ph0 = contextlib.ExitStack()
        es.enter_context(ph0)
        cT = sb(ph0, "cT", [128, KC, 132], BF16)
        sel1s = sb(ph0, "sel1s", [128, 17]); sel2s = sb(ph0, "sel2s", [4, 17])
        S.dma("sp", lambda: nc.sync.dma_start(out=sel1s[:, :], in_=sel1), w=["sel"])
        S.dma("sp", lambda: nc.sync.dma_start(out=sel2s[:, :], in_=sel2), w=["sel"])

        def ada_issue(st, wsrc, W, tag):
            nblk = (W + 511) // 512
            wv = wsrc.rearrange("(kc p) n -> p kc n", p=128)
            tiles = []
            for bi in range(nblk):
                n = min(512, W - bi * 512)
                t = sb(st, "adw" + tag, [128, KC, n], BF16)
                S.dma("pool", lambda: nc.gpsimd.dma_start(
                    out=t[:, :, :], in_=wv[:, :, bi * 512:bi * 512 + n]), w=[("adw" + tag, bi)])
                tiles.append((t, n, bi))
            return tiles

        def ada_compute(st, tiles, W, agin, agout, mc0, tag):
            o1 = sb(st, "ado1" + tag, [128, W])
            o2 = sb(st, "ado2" + tag, [4, W])
            for (t, n, bi) in tiles:
                b = misc_bank()
                for kc in range(KC):
                    S.op("pe", lambda: nc.tensor.matmul(
                        ps[b][:, 0:n], lhsT=cT[:, kc, 0:128], rhs=t[:, kc, :], start=(kc == 0), stop=(kc == KC - 1)),
                        r=[("adw" + tag, bi), "cT"], w=[("ps", b)], sig=(kc == KC - 1))
                evac(bi, o1[:, bi * 512:bi * 512 + n], ps[b][:, 0:n], [("ps", b)], ["ado1" + tag])
                b = misc_bank()
                for kc in range(KC):
                    S.op("pe", lambda: nc.tensor.matmul(
                        ps[b][0:4, 0:n], lhsT=cT[:, kc, 128:132], rhs=t[:, kc, :], start=(kc == 0), stop=(kc == KC - 1)),
                        r=[("adw" + tag, bi), "cT"], w=[("ps", b)], sig=(kc == KC - 1))
                evac(bi + 1, o2[:, bi * 512:bi * 512 + n], ps[b][0:4, 0:n], [("ps", b)], ["ado2" + tag])
            S.dma("sp", lambda: nc.sync.dma_start(out=agin[0:128, :], in_=o1[:, :]), r=["ado1" + tag], w=["agin" + tag])
            S.dma("sp", lambda: nc.sync.dma_start(out=agin[128:132, :], in_=o2[:, :]), r=["ado2" + tag], w=["agin" + tag])
            S.cc(lambda: nc.gpsimd.collective_compute("AllGather", ALU.bypass, replica_groups=ALL8,
                                                      ins=[agin.opt()], outs=[agout.opt()]),
                 r=["agin" + tag], w=["agout" + tag])
            nch = W // 128
            nbuf = 2 if W <= 512 else 1
            g1 = [sb(st, "adg1" + tag, [128, W]) for _ in range(nbuf)]
            g2 = [sb(st, "adg2" + tag, [4, W]) for _ in range(nbuf)]
            for r_ in range(8):
                G1 = g1[r_ % nbuf]; G2 = g2[r_ % nbuf]
                k1 = ("adg1" + tag, r_ % nbuf); k2 = ("adg2" + tag, r_ % nbuf)
                S.dma("sp", lambda: nc.sync.dma_start(out=G1[:, :], in_=agout[r_ * 132:r_ * 132 + 128, :]),
                      r=["agout" + tag], w=[k1])
                S.dma("sp", lambda: nc.sync.dma_start(out=G2[:, :], in_=agout[r_ * 132 + 128:r_ * 132 + 132, :]),
                      r=["agout" + tag], w=[k2])
                for j0 in range(0, nch, 16):
                    nj = min(16, nch - j0)
                    b = misc_bank()
                    for j in range(j0, j0 + nj):
                        o = ps[b][:, (j - j0) * 32:(j - j0) * 32 + 17]
                        S.op("pe", lambda: nc.tensor.matmul(
                            o, lhsT=G1[:, j * 128:(j + 1) * 128], rhs=sel1s[:, :], start=True, stop=False),
                            r=[k1, "sel"], w=[("ps", b)], sig=False)
                        S.op("pe", lambda: nc.tensor.matmul(
                            o, lhsT=G2[:, j * 128:(j + 1) * 128], rhs=sel2s[:, :], start=False, stop=True),
                            r=[k2, "sel"], w=[("ps", b)], sig=(j == j0 + nj - 1))
                    mc = mc0 + r_ * nch + j0
                    src = ps[b][:, 0:nj * 32].rearrange("p (s n) -> p s n", n=32)[:, :, 0:17]
                    S.op("dve", lambda: nc.vector.tensor_tensor(
                        out=modT[:, mc:mc + nj, :], in0=src,
                        in1=adab[:, mc:mc + nj].unsqueeze(2).to_broadcast([128, nj, 17]), op=ALU.add),
                        r=[("ps", b), "adab"], w=["modT"])

        rows_to_featmajor(ph0, ada_b[0:128, :], 128, adab[:, 0:128], "adab", "ab0")
        rows_to_featmajor(ph0, ada_b[128:144, :], 16, adab[:, 128:144], "adab", "ab1")
        rows_to_featmajor(ph0, norms, 64, normw[:, :], "normw", "nw")
        rows_to_featmajor(ph0, pvec, 16, pvT[:, :], "pvT", "pv")
        rows_to_featmajor(ph0, glub, 8, glubT[:, :], "glubT", "gb")

        with contextlib.ExitStack() as scA:
            tilesA = ada_issue(scA, adaA, 512, "A")
            cst = sb(scA, "cst", [128, D]); cst2 = sb(scA, "cst2", [4, D])
            S.dma("sp", lambda: nc.sync.dma_start(out=cst[:, :], in_=call[0:128, :]), w=["cst"])
            S.dma("sp", lambda: nc.sync.dma_start(out=cst2[:, :], in_=call[128:132, :]), w=["cst2"])
            S.op("act", lambda: nc.scalar.activation(out=cst[:, :], in_=cst[:, :], func=ACT.Silu), r=["cst"], w=["cst"])
            S.op("act", lambda: nc.scalar.activation(out=cst2[:, :], in_=cst2[:, :], func=ACT.Silu), r=["cst2"], w=["cst2"])
            for j0 in range(0, KC, 4):
                b = misc_bank()
                for j in range(j0, j0 + 4):
                    S.op("pe", lambda: nc.tensor.transpose(
                        out=ps[b][:, (j - j0) * 128:(j - j0 + 1) * 128], in_=cst[:, j * 128:(j + 1) * 128], identity=ident[:, :]),
                        r=["cst", "ident"], w=[("ps", b)], sig=(j == j0 + 3))
                evac(j0 // 4, cT[:, j0:j0 + 4, 0:128], ps[b][:, :].rearrange("p (j n) -> p j n", j=4), [("ps", b)], ["cT"])
                b = misc_bank()
                for j in range(j0, j0 + 4):
                    S.op("pe", lambda: nc.tensor.transpose(
                        out=ps[b][:, (j - j0) * 128:(j - j0) * 128 + 4], in_=cst2[:4, j * 128:(j + 1) * 128],
                        identity=ident[:4, :4]),
                        r=["cst2", "ident"], w=[("ps", b)], sig=(j == j0 + 3))
                evac(j0 // 4 + 1, cT[:, j0:j0 + 4, 128:132], ps[b][:, :].rearrange("p (j n) -> p j n", j=4)[:, :, 0:4],
                     [("ps", b)], ["cT"])
            ada_compute(scA, tilesA, 512, ag1_in, ag1_out, 0, "A")
            derive(0, ("ss",))
            S.barrier()

        with contextlib.ExitStack() as scB:
            tilesB = ada_issue(scB, adaB, 1792, "B")
            xpk = xp.rearrange("(c k) d -> k c d", k=8)
            scX = contextlib.ExitStack()
            scB.enter_context(scX)
            stgx = [sb(scX, "stgx", [128, D]) for _ in range(2)]
            for k in range(9):
                sx = stgx[k % 2]
                key = ("stgx", k % 2)
                nrows = 128 if k < 8 else NS
                src = xpk[k] if k < 8 else xs
                col0 = k * 128
                S.dma("sp", lambda: nc.sync.dma_start(out=sx[:nrows, :], in_=src), w=[key])
                for j0 in range(0, KC, 4):
                    b = misc_bank()
                    for j in range(j0, j0 + 4):
                        S.op("pe", lambda: nc.tensor.transpose(
                            out=ps[b][:, (j - j0) * 128:(j - j0) * 128 + nrows],
                            in_=sx[:nrows, j * 128:(j + 1) * 128], identity=ident[:nrows, :nrows]),
                            r=[key, "ident"], w=[("ps", b)], sig=(j == j0 + 3))
                    evac(j0 // 4, xT[:, j0:j0 + 4, col0:col0 + nrows],
                         ps[b][:, :].rearrange("p (j n) -> p j n", j=4)[:, :, 0:nrows],
                         [("ps", b)], [("xT", j) for j in range(j0, j0 + 4)])
            S.barrier()
            scX.close()
            tp1i = sb(scB, "tp1i", [128, 16], I32); tp1 = sb(scB, "tp1", [128, 16])
            S.op("pool", lambda: nc.gpsimd.iota(tp1i[:], pattern=[[1, 8], [8, 2]], base=1, channel_multiplier=0), w=["tp1i"])
            S.op("dve", lambda: nc.vector.tensor_copy(out=tp1[:], in_=tp1i[:]), r=["tp1i"], w=["tp1"])
            for wi, w_ in enumerate(WINS):
                fw = Fw[:, wi, :, :].rearrange("p k c -> p (k c)")
                S.op("dve", lambda: nc.vector.tensor_scalar(out=fw, in0=tp1[:], scalar1=float(w_), scalar2=None,
                                                            op0=ALU.min), r=["tp1"], w=[("Fw", wi)])
                S.op("dve", lambda: nc.vector.reciprocal(out=fw, in_=fw), r=[("Fw", wi)], w=[("Fw", wi)])
                S.op("dve", lambda: nc.vector.tensor_scalar(out=fw, in0=fw, scalar1=float(w_), scalar2=-1.0,
                                                            op0=ALU.mult, op1=ALU.add), r=[("Fw", wi)], w=[("Fw", wi)])
                S.op("dve", lambda: nc.vector.tensor_scalar(out=fw, in0=fw, scalar1=flg[:, 1:2], scalar2=1.0,
                                                            op0=ALU.mult, op1=ALU.add), r=[("Fw", wi), "flg"], w=[("Fw", wi)])
            ada_compute(scB, tilesB, 1792, ag2_in, ag2_out, 32, "B")
            derive(0, ("g",))
            derive(1, ("ss", "g"))
            derive(2, ("ss", "g"))
            S.barrier()

        if with_s5:
            with contextlib.ExitStack() as scS:
                S5Setup(nc, S, sb, ps, misc_bank, evac, locals()).run(scS)
                S.barrier()
        ph0.close()
        modscope.close()

        def rms_stats(scr, rstd):
            for kc in range(KC):
                S.op("act", lambda kc=kc: nc.scalar.activation(out=scr[:, kc, :], in_=xT[:, kc, :], func=ACT.Square),
                     r=[("xT", kc)], w=[("hT", kc)])
            for ti, (c0, n) in enumerate(TILES):
                b = misc_bank()
                for kc in range(KC):
                    S.op("pe", lambda kc=kc, b=b, c0=c0, n=n: nc.tensor.matmul(
                        ps[b][:, 0:n], lhsT=onesb[:, :], rhs=scr[:, kc, c0:c0 + n],
                        start=(kc == 0), stop=(kc == KC - 1)),
                        r=[("hT", kc), "onesb"], w=[("ps", b)], sig=(kc == KC - 1))
                S.op("act", lambda b=b, c0=c0, n=n: nc.scalar.activation(
                    out=rstd[:, c0:c0 + n], in_=ps[b][:, 0:n], func=ACT.Sqrt, bias=epsb[:, 0:1], scale=1.0 / D),
                    r=[("ps", b), "epsb"], w=[("rstd", ti)])
                S.op("dve", lambda c0=c0, n=n: nc.vector.reciprocal(out=rstd[:, c0:c0 + n], in_=rstd[:, c0:c0 + n]),
                     r=[("rstd", ti)], w=[("rstd", ti)])

        def norm_mod(sub, hT):
            with contextlib.ExitStack() as st:
                rstd = sb(st, "rstd", [128, NT])
                tsc = sb(st, "tsc", [128, 2, NT])
                rms_stats(hT, rstd)
                for kc in range(KC):
                    t = tsc[:, kc % 2, :]
                    tk = ("tsc", kc % 2)
                    S.op("dve", lambda kc=kc, t=t: nc.vector.tensor_tensor(out=t, in0=xT[:, kc, :], in1=rstd[:, :], op=ALU.mult),
                         r=[("xT", kc)] + [("rstd", i) for i in range(3)], w=[tk])
                    S.op("act", lambda kc=kc, t=t: nc.scalar.activation(
                        out=hT[:, kc, 0:NPT], in_=t[:, 0:NPT], func=ACT.Identity,
                        bias=shP[:, sub, kc:kc + 1], scale=gsP[:, sub, kc:kc + 1]),
                        r=[tk, "gsP", "shP"], w=[("hT", kc)])
                    S.op("dve", lambda kc=kc, t=t: nc.vector.tensor_tensor(
                        out=t[:, NPT:NT], in0=t[:, NPT:NT], in1=gsS[:, sub, kc, :], op=ALU.mult),
                        r=[tk, "gsS"], w=[tk])
                    S.op("dve", lambda kc=kc, t=t: nc.vector.tensor_tensor(
                        out=hT[:, kc, NPT:NT], in0=t[:, NPT:NT], in1=shS[:, sub, kc, :], op=ALU.add),
                        r=[tk, "shS"], w=[("hT", kc)])
                S.barrier()

        def resid_evac(bo, m, ti, c0, n, sub, tmpS):
            if ti < 2:
                S.op("dve", lambda: nc.vector.scalar_tensor_tensor(
                    out=xT[:, m, c0:c0 + n], in0=ps[bo][:, 0:n], scalar=gtP[:, sub, m:m + 1],
                    in1=xT[:, m, c0:c0 + n], op0=ALU.mult, op1=ALU.add),
                    r=[("ps", bo), "gtP", ("xT", m)], w=[("xT", m)])
            else:
                S.op("dve", lambda: nc.vector.tensor_tensor(
                    out=tmpS[:, :], in0=ps[bo][:, 0:n], in1=gtS[:, sub, m, :], op=ALU.mult),
                    r=[("ps", bo), "gtS"], w=["tmpS"])
                S.op("dve", lambda: nc.vector.tensor_tensor(
                    out=xT[:, m, c0:c0 + n], in0=xT[:, m, c0:c0 + n], in1=tmpS[:, :], op=ALU.add),
                    r=["tmpS", ("xT", m)], w=[("xT", m)])

        def ffn(fi, sub, hook=None):
            with contextlib.ExitStack() as ph:
                hT = sb(ph, "hT", [128, KC, NT], BF16)
                aT = sb(ph, "aT", [128, FPP, NT], BF16)
                NR = 3
                wgr = [sb(ph, "wgr", [128, KC, 128], BF16) for i in range(NR)]
                wur = [sb(ph, "wur", [128, KC, 128], BF16) for i in range(NR)]
                wdr = [sb(ph, "wdr", [128, FPP, 128], BF16) for i in range(NR)]
                sgb = [sb(ph, "sgb", [128, 512]) for i in range(2)]
                tmpS = sb(ph, "tmpS", [128, NS])
                wgv = w_g[fi].rearrange("(kc p) n -> p kc n", p=128)
                wuv = w_u[fi].rearrange("(kc p) n -> p kc n", p=128)
                wdv = w_d[fi].rearrange("(fc p) n -> p fc n", p=128)
                loads = []
                for q in range(NPART):
                    for fl in range(FPP):
                        loads.append(("gu", q * FPP + fl))
                    for m in range(KC):
                        loads.append(("d", q, m))
                state = {"next": 0}

                def issue_load():
                    i = state["next"]
                    if i >= len(loads):
                        return
                    state["next"] += 1
                    L = loads[i]
                    if L[0] == "gu":
                        f = L[1]
                        sl = f % NR
                        S.dma("pool", lambda: nc.gpsimd.dma_start(
                            out=wgr[sl][:, :, :], in_=wgv[:, :, f * 128:(f + 1) * 128]), w=[("wg", sl)])
                        S.dma("pool", lambda: nc.gpsimd.dma_start(
                            out=wur[sl][:, :, :], in_=wuv[:, :, f * 128:(f + 1) * 128]), w=[("wu", sl)])
                    else:
                        _, q, m = L
                        sl = (q * KC + m) % NR
                        S.dma("pool", lambda: nc.gpsimd.dma_start(
                            out=wdr[sl][:, :, :], in_=wdv[:, q * FPP:(q + 1) * FPP, m * 128:(m + 1) * 128]),
                            w=[("wd", sl)])

                issue_load()
                issue_load()
                norm_mod(sub, hT)
                gi = 0
                ei = 0
                for q in range(NPART):
                    for fl in range(FPP):
                        f = q * FPP + fl
                        sl = f % NR
                        issue_load()
                        if hook is not None and f == 3:
                            hook(ph)
                        for ti, (c0, n) in enumerate(TILES):
                            bg = gi % 2
                            bu = 2 + gi % 2
                            gi += 1
                            for kc in range(KC):
                                S.op("pe", lambda kc=kc: nc.tensor.matmul(
                                    ps[bg][:, 0:n], lhsT=wgr[sl][:, kc, :], rhs=hT[:, kc, c0:c0 + n],
                                    start=(kc == 0), stop=(kc == KC - 1)),
                                    r=[("wg", sl), ("hT", kc)], w=[("ps", bg)], sig=(kc == KC - 1))
                            for kc in range(KC):
                                S.op("pe", lambda kc=kc: nc.tensor.matmul(
                                    ps[bu][:, 0:n], lhsT=wur[sl][:, kc, :], rhs=hT[:, kc, c0:c0 + n],
                                    start=(kc == 0), stop=(kc == KC - 1)),
                                    r=[("wu", sl), ("hT", kc)], w=[("ps", bu)], sig=(kc == KC - 1))
                            sg = sgb[ei % 2]
                            sk = ("sgb", ei % 2)
                            ei += 1
                            S.op("act", lambda: nc.scalar.activation(out=sg[:, 0:n], in_=ps[bg][:, 0:n], func=ACT.Silu),
                                 r=[("ps", bg)], w=[sk])
                            S.op("dve", lambda: nc.vector.tensor_tensor(
                                out=aT[:, fl, c0:c0 + n], in0=sg[:, 0:n], in1=ps[bu][:, 0:n], op=ALU.mult),
                                r=[sk, ("ps", bu)], w=[("aT", fl, ti)])
                    for m in range(KC):
                        sl = (q * KC + m) % NR
                        issue_load()
                        for ti, (c0, n) in enumerate(TILES):
                            bo = 4 + gi % 4
                            gi += 1
                            for fl in range(FPP):
                                S.op("pe", lambda fl=fl: nc.tensor.matmul(
                                    ps[bo][:, 0:n], lhsT=wdr[sl][:, fl, :], rhs=aT[:, fl, c0:c0 + n],
                                    start=(fl == 0), stop=(fl == FPP - 1)),
                                    r=[("wd", sl), ("aT", fl, ti)], w=[("ps", bo)], sig=(fl == FPP - 1))
                            resid_evac(bo, m, ti, c0, n, sub, tmpS)
            S.barrier()

        ffn(0, 0)

        mixer = Mixer(nc, S, sb, ps, misc_bank, evac, locals())
        mixer.run(with_s5 and s5main)

        ffn(1, 2)

        with contextlib.ExitStack() as ph:
            scr = sb(ph, "scrF", [128, KC, NT], BF16)
            rstd = sb(ph, "rstdF", [128, NT])
            ost = [sb(ph, "ost", [128, D]) for i in range(2)]
            rms_stats(scr, rstd)
            for kc in range(KC):
                S.op("dve", lambda kc=kc: nc.vector.scalar_tensor_tensor(
                    out=xT[:, kc, :], in0=xT[:, kc, :], scalar=normw[:, 3 * KC + kc:3 * KC + kc + 1],
                    in1=rstd[:, :], op0=ALU.mult, op1=ALU.mult),
                    r=[("xT", kc), "normw"] + [("rstd", i) for i in range(3)], w=[("xT", kc)])
            ypk = yp.rearrange("(c k) d -> k c d", k=8)
            for k in range(9):
                o = ost[k % 2]
                ok = ("ost", k % 2)
                nrows = 128 if k < 8 else NS
                col0 = k * 128
                for j0 in range(0, KC, 4):
                    b = misc_bank()
                    for j in range(j0, j0 + 4):
                        S.op("pe", lambda j=j, b=b, j0=j0: nc.tensor.transpose(
                            out=ps[b][:nrows, (j - j0) * 128:(j - j0 + 1) * 128],
                            in_=xT[:, j, col0:col0 + nrows], identity=ident[:, :]),
                            r=[("xT", j), "ident"], w=[("ps", b)], sig=(j == j0 + 3))
                    evac(j0 // 4, o[:nrows, j0 * 128:(j0 + 4) * 128], ps[b][:nrows, :], [("ps", b)], [ok])
                dstd = ypk[k] if k < 8 else ys
                S.dma("sp", lambda o=o, dstd=dstd: nc.sync.dma_start(out=dstd, in_=o[:nrows, :]), r=[ok])
            S.finish()
    return nc


class Mixer:
    def __init__(self, nc, S, sb, ps, misc_bank, evac, env):
        self.nc, self.S, self.sb, self.ps, self.misc_bank, self.evac, self.e = nc, S, sb, ps, misc_bank, evac, env

    def run(self, with_s5):
        nc, S, sb, ps, misc_bank, evac, e = self.nc, self.S, self.sb, self.ps, self.misc_bank, self.evac, self.e
        xT, ident, identb, flg, pvT, Fw = e["xT"], e["ident"], e["identb"], e["flg"], e["pvT"], e["Fw"]
        w_in, w_out, pool_w, spool = e["w_in"], e["w_out"], e["pool_w"], e["spool"]
        agv_in, agv_out, npool_p, npool_s = e["agv_in"], e["agv_out"], e["npool_p"], e["npool_s"]
        gtP, gtS = e["gtP"], e["gtS"]
        with contextlib.ExitStack() as mp:
            mixedP = sb(mp, "mixedP", [128, 8, NT], BF16)
            us = sb(mp, "us", [NS, 1024])
            X = sb(mp, "X", [128, 64, 128], BF16)
            with contextlib.ExitStack() as m1:
                hT = sb(m1, "hTm", [128, KC, NT], BF16)
                e["norm_mod"](1, hT)
                wr = [sb(m1, "wr", [128, KC, 128], BF16) for _ in range(2)]
                hrT = sb(m1, "hrT", [128, 8, 16])
                pw = sb(m1, "pw", [128, 4, 2, 256], BF16)
                zsT = sb(m1, "zsT", [128, 8, NS], BF16)
                hH = sb(m1, "hH", [128, KC, 16], BF16)
                sp_ = contextlib.ExitStack()
                m1.enter_context(sp_)
                vs = sb(sp_, "vs", [NS, 1024])
                vh = sb(sp_, "vh", [16, 1024])
                hrecv = sb(sp_, "hrecv", [16, 1024])
                wiv = w_in.rearrange("(kc p) n -> p kc n", p=128)
                S.dma("pool", lambda: nc.gpsimd.dma_start(
                    out=pw[:, :, :, :], in_=pool_w.rearrange("g (k p) n -> p g k n", p=128)), w=["pw"])
                lcount = [0]

                def wload(blk):
                    sl = lcount[0] % 2
                    lcount[0] += 1
                    S.dma("pool", lambda: nc.gpsimd.dma_start(
                        out=wr[sl][:, :, :], in_=wiv[:, :, blk * 128:(blk + 1) * 128]), w=[("wr", sl)])
                    return sl

                for k8 in range(0, KC, 8):
                    S.op("dve", lambda: nc.vector.tensor_copy(
                        out=hH[:, k8:k8 + 8, :].rearrange("p a (k c) -> p a k c", c=2),
                        in_=hT[:, k8:k8 + 8, 0:NPT].rearrange("p a (k c) -> p a k c", k=8)[:, :, :, 126:128]),
                        r=[("hT", kc) for kc in range(k8, k8 + 8)], w=["hH"])
                sl = wload(8)
                for j in range(8):
                    nsl = wload(8 + j + 1) if j < 7 else None
                    for kc in range(KC):
                        S.op("pe", lambda: nc.tensor.matmul(
                            ps[j // 4][0:NS, (j % 4) * 128:(j % 4 + 1) * 128], lhsT=hT[:, kc, NPT:NT], rhs=wr[sl][:, kc, :],
                            start=(kc == 0), stop=(kc == KC - 1)),
                            r=[("wr", sl), ("hT", kc)], w=[("ps", j // 4)], sig=(kc == KC - 1))
                    for kc in range(KC):
                        S.op("pe", lambda: nc.tensor.matmul(
                            ps[2 + j // 4][0:16, (j % 4) * 128:(j % 4 + 1) * 128], lhsT=hH[:, kc, :],
                            rhs=wr[sl][:, kc, :], start=(kc == 0), stop=(kc == KC - 1)),
                            r=[("wr", sl), "hH"], w=[("ps", 2 + j // 4)], sig=(kc == KC - 1))
                    sl = nsl
                for hb in range(2):
                    evac(hb, vs[:, hb * 512:(hb + 1) * 512], ps[hb][0:NS, :], [("ps", hb)], ["vs"])
                    evac(hb + 1, vh[:, hb * 512:(hb + 1) * 512], ps[2 + hb][0:16, :], [("ps", 2 + hb)], ["vh"])
                S.dma("sp", lambda: nc.sync.dma_start(out=agv_in, in_=vh[:, :]), r=["vh"], w=["agv_in"])
                S.cc(lambda: nc.gpsimd.collective_compute("AllGather", ALU.bypass, replica_groups=e["PAIRS"],
                                                          ins=[agv_in.opt()], outs=[agv_out.opt()]),
                     r=["agv_in"], w=["agv_out"])
                S.dma("sp", lambda: nc.sync.dma_start(
                    out=npool_p.rearrange("(cp k) d -> k cp d", k=8), in_=agv_in.rearrange("(k cp) d -> k cp d", cp=2)),
                    r=["agv_in"])
                S.dma("sp", lambda: nc.sync.dma_start(out=hrecv[:, :], in_=agv_out[0:16, :]), r=["agv_out"], w=["hrecv"])
                S.dma("sp", lambda: nc.sync.dma_start(out=npool_s[:, 0:14, :], in_=spool[:, 1:15, :]))
                S.dma("sp", lambda: nc.sync.dma_start(out=npool_s[:, 14, :], in_=vs[:, :]), r=["vs"])
                if True:
                    sbuf_ = [sb(sp_, "spb", [NS, 16, 128]) for _ in range(2)]
                    zs = sb(sp_, "zs", [NS, 1024])
                    for ch in range(8):
                        bf = sbuf_[ch % 2]
                        bk = ("spb", ch % 2)
                        w_ = WINS[ch // 2]
                        S.dma("sp", lambda: nc.sync.dma_start(out=bf[:, 0:15, :], in_=spool[:, :, ch * 128:(ch + 1) * 128]), w=[bk])
                        S.op("dve", lambda: nc.vector.tensor_copy(out=bf[:, 15, :], in_=vs[:, ch * 128:(ch + 1) * 128]),
                             r=["vs"], w=[bk])
                        S.op("dve", lambda: nc.vector.tensor_reduce(
                            out=zs[:, ch * 128:(ch + 1) * 128], in_=bf[:, 16 - w_:16, :].rearrange("p r c -> p c r"),
                            axis=mybir.AxisListType.X, op=ALU.add), r=[bk], w=[("zs", ch)])
                        S.op("dve", lambda: nc.vector.scalar_tensor_tensor(
                            out=zs[:, ch * 128:(ch + 1) * 128], in0=zs[:, ch * 128:(ch + 1) * 128], scalar=1.0 / w_,
                            in1=vs[:, ch * 128:(ch + 1) * 128], op0=ALU.mult, op1=ALU.subtract),
                            r=[("zs", ch), "vs"], w=[("zs", ch)])
                    for j0 in range(0, 8, 4):
                        b = misc_bank()
                        for j in range(j0, j0 + 4):
                            S.op("pe", lambda: nc.tensor.transpose(
                                out=ps[b][:, (j - j0) * 128:(j - j0) * 128 + NS], in_=zs[:NS, j * 128:(j + 1) * 128],
                                identity=ident[:NS, :NS]),
                                r=[("zs", j), "ident"], w=[("ps", b)], sig=(j == j0 + 3))
                        evac(0, zsT[:, j0:j0 + 4, :], ps[b][:, :].rearrange("p (j n) -> p j n", j=4)[:, :, 0:NS],
                             [("ps", b)], ["zsT"])
                    for j0 in range(0, 8, 4):
                        b = misc_bank()
                        for j in range(j0, j0 + 4):
                            S.op("pe", lambda: nc.tensor.transpose(
                                out=ps[b][:, (j - j0) * 128:(j - j0) * 128 + 16], in_=hrecv[:16, j * 128:(j + 1) * 128],
                                identity=ident[:16, :16]),
                                r=["hrecv", "ident"], w=[("ps", b)], sig=(j == j0 + 3))
                        S.op("dve", lambda: nc.vector.tensor_scalar(
                            out=hrT[:, j0:j0 + 4, :], in0=ps[b][:, :].rearrange("p (j n) -> p j n", j=4)[:, :, 0:16],
                            scalar1=flg[:, 0:1], scalar2=None, op0=ALU.mult), r=[("ps", b), "flg"], w=["hrT"])
                    S.barrier()
                    sp_.close()
                with contextlib.ExitStack() as m2:
                    U_tok = sb(m2, "U_tok", [128, 64, 8, 16], BF16)
                    sl = wload(0)
                    for blk in range(8):
                        nsl = wload(blk + 1) if blk < 7 else wload(8)
                        for half in range(2):
                            b = half
                            for i in range(4):
                                k = 7 - (half * 4 + i)
                                for kc in range(KC):
                                    S.op("pe", lambda: nc.tensor.matmul(
                                        ps[b][:, i * 128:(i + 1) * 128], lhsT=hT[:, kc, k * 128:(k + 1) * 128],
                                        rhs=wr[sl][:, kc, :], start=(kc == 0), stop=(kc == KC - 1)),
                                        r=[("wr", sl), ("hT", kc)], w=[("ps", b)], sig=(kc == KC - 1 and i == 3))
                            evac(half, U_tok[:, blk * 8:(blk + 1) * 8, half * 4:half * 4 + 4, :].rearrange("p g k h -> p k g h"),
                                 ps[b][:, :].rearrange("p (i g h) -> p i g h", i=4, g=8), [("ps", b)], [("U_tok", blk)])
                        b = 2 + blk % 2
                        for kc in range(KC):
                            S.op("pe", lambda: nc.tensor.matmul(
                                ps[b][0:NS, 0:128], lhsT=hT[:, kc, NPT:NT], rhs=wr[sl][:, kc, :],
                                start=(kc == 0), stop=(kc == KC - 1)),
                                r=[("wr", sl), ("hT", kc)], w=[("ps", b)], sig=(kc == KC - 1))
                        evac(blk, us[:, blk * 128:(blk + 1) * 128], ps[b][0:NS, 0:128], [("ps", b)], ["us"])
                        for g0 in range(0, 8, 4):
                            b = misc_bank()
                            for gg in range(g0, g0 + 4):
                                g = blk * 8 + gg
                                S.op("pe", lambda: nc.tensor.matmul(
                                    ps[b][:, (gg - g0) * 128:(gg - g0 + 1) * 128], lhsT=U_tok[:, g, :, :].rearrange("p k h -> p (k h)"),
                                    rhs=identb[:, :], start=True, stop=True),
                                    r=[("U_tok", blk), "identb"], w=[("ps", b)], sig=(gg == g0 + 3))
                            evac(g0 // 4, X[:, blk * 8 + g0:blk * 8 + g0 + 4, :],
                                 ps[b][:, :].rearrange("p (g n) -> p g n", g=4), [("ps", b)], [("X", blk)])
                        sl = nsl
                    vT = [sb(m2, "vT", [128, 8, 130]) for _ in range(2)]
                    Ab = [sb(m2, "Ab", [128, 8, 130]) for _ in range(2)]
                    Bb = [sb(m2, "Bb", [128, 8, 130]) for _ in range(2)]
                    zTg = sb(m2, "zTg", [128, 2, NT], BF16)
                    for t_ in Ab + Bb + vT:
                        S.op("pool", lambda: nc.gpsimd.memset(t_[:, :, :], 0.0), w=["poolinit"])
                    S.barrier()

                    def shift_add(dst, src, s, kk):
                        S.op("dve", lambda: nc.vector.tensor_tensor(out=dst[:, s:8, :], in0=src[:, s:8, :], in1=src[:, 0:8 - s, :],
                                                                   op=ALU.add), r=[kk], w=[kk])
                        S.op("dve", lambda: nc.vector.tensor_tensor(out=dst[:, 0:s, 1:130], in0=src[:, 0:s, 1:130],
                                                                   in1=src[:, 8 - s:8, 0:129], op=ALU.add), r=[kk], w=[kk])

                    for g in range(4):
                        w_ = WINS[g]
                        for c2 in range(2):
                            ch = 2 * g + c2
                            nsl = wload(8 + ch + 1) if ch < 7 else None
                            V, A_, B_ = vT[c2], Ab[c2], Bb[c2]
                            kk = ("pool", c2)
                            for ti in range(2):
                                b = ti
                                c0 = ti * 512
                                for kc in range(KC):
                                    S.op("pe", lambda: nc.tensor.matmul(
                                        ps[b][:, :], lhsT=wr[sl][:, kc, :], rhs=hT[:, kc, c0:c0 + 512],
                                        start=(kc == 0), stop=(kc == KC - 1)),
                                        r=[("wr", sl), ("hT", kc)], w=[("ps", b)], sig=(kc == KC - 1))
                                evac(ti, V[:, ti * 4:ti * 4 + 4, 2:130], ps[b][:, :].rearrange("p (k c) -> p k c", k=4),
                                     [("ps", b)], [kk])
                            S.op("dve", lambda: nc.vector.tensor_copy(out=V[:, :, 0:2], in_=hrT[:, ch, :].rearrange("p (k c) -> p k c", c=2)),
                                 r=["hrT", kk], w=[kk])
                            shift_add(A_, V, 1, kk)
                            P_ = A_
                            if w_ >= 4:
                                shift_add(B_, A_, 2, kk)
                                P_ = B_
                            if w_ >= 8:
                                shift_add(A_, B_, 4, kk)
                                P_ = A_
                            if w_ >= 16:
                                S.op("dve", lambda: nc.vector.tensor_tensor(out=B_[:, :, 1:130], in0=A_[:, :, 1:130],
                                                                           in1=A_[:, :, 0:129], op=ALU.add), r=[kk], w=[kk])
                                P_ = B_
                            S.op("dve", lambda: nc.vector.tensor_tensor(out=P_[:, :, 2:4], in0=P_[:, :, 2:4], in1=Fw[:, g, :, :],
                                                                       op=ALU.mult), r=[kk, ("Fw", g)], w=[kk])
                            S.op("dve", lambda: nc.vector.scalar_tensor_tensor(
                                out=zTg[:, c2, 0:NPT].rearrange("p (k c) -> p k c", k=8), in0=P_[:, :, 2:130], scalar=1.0 / w_,
                                in1=V[:, :, 2:130], op0=ALU.mult, op1=ALU.subtract), r=[kk], w=[("zTg", c2)])
                            S.op("dve", lambda: nc.vector.tensor_copy(out=zTg[:, c2, NPT:NT], in_=zsT[:, ch, :]),
                                 r=["zsT"], w=[("zTg", c2)])
                            sl = nsl
                        for m2_ in range(2):
                            chn = 2 * g + m2_
                            for ti, (c0, n) in enumerate(TILES):
                                b = 4 + (ti % 2)
                                for k2 in range(2):
                                    S.op("pe", lambda: nc.tensor.matmul(
                                        ps[b][:, 0:n], lhsT=pw[:, g, k2, m2_ * 128:(m2_ + 1) * 128], rhs=zTg[:, k2, c0:c0 + n],
                                        start=(k2 == 0), stop=(k2 == 1)),
                                        r=["pw", ("zTg", k2)], w=[("ps", b)], sig=(k2 == 1))
                                S.op("dve", lambda: nc.vector.tensor_scalar(
                                    out=mixedP[:, chn, c0:c0 + n], in0=ps[b][:, 0:n], scalar1=pvT[:, chn:chn + 1],
                                    scalar2=pvT[:, 8 + chn:8 + chn + 1], op0=ALU.add, op1=ALU.mult),
                                    r=[("ps", b), "pvT"], w=[("mixedP", chn)])
                    S.barrier()
                S.barrier()
            mixedS = sb(mp, "mixedS", [128, 8, NT], BF16)
            if with_s5:
                S5Main(nc, S, sb, ps, misc_bank, evac, e, X, us, mixedS).run(mp)
            else:
                S.op("dve", lambda: nc.vector.memset(mixedS[:, :, :], 0.0), w=["mixedS"])
            S.barrier()
            with contextlib.ExitStack() as m3:
                wo = [sb(m3, "wo", [128, KC, 128], BF16) for _ in range(3)]
                tmpS = sb(m3, "tmpSo", [128, NS])
                wov = w_out.rearrange("(kc p) n -> p kc n", p=128)

                def oload(m):
                    S.dma("pool", lambda: nc.gpsimd.dma_start(out=wo[m % 3][:, :, :], in_=wov[:, :, m * 128:(m + 1) * 128]),
                          w=[("wo", m % 3)])
                oload(0)
                oload(1)
                gi = 0
                for m in range(KC):
                    if m + 2 < KC:
                        oload(m + 2)
                    for ti, (c0, n) in enumerate(TILES):
                        bo = 4 + gi % 4
                        gi += 1
                        for kc in range(KC):
                            src = mixedS[:, kc, c0:c0 + n] if kc < 8 else mixedP[:, kc - 8, c0:c0 + n]
                            S.op("pe", lambda: nc.tensor.matmul(
                                ps[bo][:, 0:n], lhsT=wo[m % 3][:, kc, :], rhs=src, start=(kc == 0), stop=(kc == KC - 1)),
                                r=[("wo", m % 3), "mixedS", ("mixedP", kc - 8)], w=[("ps", bo)], sig=(kc == KC - 1))
                        e["resid_evac"](bo, m, ti, c0, n, 1, tmpS)
                S.barrier()

S5STAGE = [4]


class S5Setup:
    def __init__(self, nc, S, sb, ps, misc_bank, evac, env):
        self.nc, self.S, self.sb, self.ps, self.misc_bank, self.evac, self.e = nc, S, sb, ps, misc_bank, evac, env

    def run(self, st):
        nc, S, sb, ps, e = self.nc, self.S, self.sb, self.ps, self.e
        V = nc.vector
        ident, identb = e["ident"], e["identb"]
        a_re, a_im, al_re, al_im = e["a_re"], e["a_im"], e["al_re"], e["al_im"]
        st_w1, st_toep, st_w2 = e["st_w1"], e["st_toep"], e["st_w2"]
        K = "s5s"

        def dv(fn, r=(), w=()):
            S.op("dve", fn, r=[K] + list(r), w=[K] + list(w))

        def ac(fn, r=(), w=()):
            S.op("act", fn, r=[K] + list(r), w=[K] + list(w))

        lamA = sb(st, "lamA", [128, 3, 32])
        BA = sb(st, "BA", [128, 2, 32, 16])
        CA = sb(st, "CA", [128, 2, 32, 16])
        Dcol = sb(st, "Dcol", [128, 64])
        S.dma("sp", lambda: nc.sync.dma_start(out=lamA[:, :, :], in_=e["lamA_d"]), w=[K])
        S.dma("sp", lambda: nc.sync.dma_start(out=BA[:, :, :, :], in_=e["BA_d"]), w=[K])
        S.dma("sp", lambda: nc.sync.dma_start(out=CA[:, :, :, :], in_=e["CA_d"]), w=[K])
        S.dma("sp", lambda: nc.sync.dma_start(out=Dcol[:, :], in_=e["Dcol_d"]), w=[K])
        SM = sb(st, "SM", [128, 20, 32])
        QI = sb(st, "QI", [128, 32], I32)
        (lr, dt, lrd, ang, mag, rr, qf, fr, m1, sn, cs, den, nr, f_re, f_im, t1, t2, fc) = [SM[:, i, :] for i in range(18)]
        li = lamA[:, 1, :]
        dv(lambda: V.tensor_scalar(out=lr, in0=lamA[:, 0, :], scalar1=-1e-4, scalar2=None, op0=ALU.min))
        ac(lambda: nc.scalar.activation(out=dt, in_=lamA[:, 2, :], func=ACT.Exp))
        dv(lambda: V.tensor_tensor(out=lrd, in0=lr, in1=dt, op=ALU.mult))
        dv(lambda: V.tensor_tensor(out=ang, in0=li, in1=dt, op=ALU.mult))
        ac(lambda: nc.scalar.activation(out=mag, in_=lrd, func=ACT.Exp))
        dv(lambda: V.tensor_scalar(out=rr, in0=ang, scalar1=1.0 / TWO_PI, scalar2=None, op0=ALU.mult))
        dv(lambda: V.tensor_copy(out=QI[:, :], in_=rr))
        dv(lambda: V.tensor_copy(out=qf, in_=QI[:, :]))
        dv(lambda: V.tensor_tensor(out=fr, in0=rr, in1=qf, op=ALU.subtract))
        dv(lambda: V.tensor_single_scalar(out=m1, in_=fr, scalar=0.5, op=ALU.is_gt))
        dv(lambda: V.tensor_tensor(out=fr, in0=fr, in1=m1, op=ALU.subtract))
        dv(lambda: V.tensor_single_scalar(out=m1, in_=fr, scalar=-0.5, op=ALU.is_lt))
        dv(lambda: V.tensor_tensor(out=fr, in0=fr, in1=m1, op=ALU.add))
        ac(lambda: nc.scalar.activation(out=sn, in_=fr, func=ACT.Sin, scale=TWO_PI))
        dv(lambda: V.tensor_scalar(out=fc, in0=fr, scalar1=0.25, scalar2=None, op0=ALU.add))
        dv(lambda: V.tensor_single_scalar(out=m1, in_=fc, scalar=0.5, op=ALU.is_gt))
        dv(lambda: V.tensor_tensor(out=fc, in0=fc, in1=m1, op=ALU.subtract))
        ac(lambda: nc.scalar.activation(out=cs, in_=fc, func=ACT.Sin, scale=TWO_PI))
        dv(lambda: V.tensor_tensor(out=a_re[:, :], in0=mag, in1=cs, op=ALU.mult))
        dv(lambda: V.tensor_tensor(out=a_im[:, :], in0=mag, in1=sn, op=ALU.mult))
        dv(lambda: V.tensor_tensor(out=t1, in0=lr, in1=lr, op=ALU.mult))
        dv(lambda: V.tensor_tensor(out=t2, in0=li, in1=li, op=ALU.mult))
        dv(lambda: V.tensor_tensor(out=den, in0=t1, in1=t2, op=ALU.add))
        dv(lambda: V.reciprocal(out=den, in_=den))
        dv(lambda: V.tensor_scalar(out=nr, in0=a_re[:, :], scalar1=-1.0, scalar2=None, op0=ALU.add))
        dv(lambda: V.tensor_tensor(out=t1, in0=nr, in1=lr, op=ALU.mult))
        dv(lambda: V.tensor_tensor(out=t2, in0=a_im[:, :], in1=li, op=ALU.mult))
        dv(lambda: V.tensor_tensor(out=t1, in0=t1, in1=t2, op=ALU.add))
        dv(lambda: V.tensor_tensor(out=f_re, in0=t1, in1=den, op=ALU.mult))
        dv(lambda: V.tensor_tensor(out=t1, in0=a_im[:, :], in1=lr, op=ALU.mult))
        dv(lambda: V.tensor_tensor(out=t2, in0=nr, in1=li, op=ALU.mult))
        dv(lambda: V.tensor_tensor(out=t1, in0=t1, in1=t2, op=ALU.subtract))
        dv(lambda: V.tensor_tensor(out=f_im, in0=t1, in1=den, op=ALU.mult))
        Pw = sb(st, "Pw", [128, 2, 9, 32])
        dv(lambda: V.memset(Pw[:, 0, 0, :], 1.0))
        dv(lambda: V.memset(Pw[:, 1, 0, :], 0.0))
        for m in range(1, 9):
            dv(lambda: V.tensor_tensor(out=t1, in0=Pw[:, 0, m - 1, :], in1=a_re[:, :], op=ALU.mult))
            dv(lambda: V.tensor_tensor(out=t2, in0=Pw[:, 1, m - 1, :], in1=a_im[:, :], op=ALU.mult))
            dv(lambda: V.tensor_tensor(out=Pw[:, 0, m, :], in0=t1, in1=t2, op=ALU.subtract))
            dv(lambda: V.tensor_tensor(out=t1, in0=Pw[:, 0, m - 1, :], in1=a_im[:, :], op=ALU.mult))
            dv(lambda: V.tensor_tensor(out=t2, in0=Pw[:, 1, m - 1, :], in1=a_re[:, :], op=ALU.mult))
            dv(lambda: V.tensor_tensor(out=Pw[:, 1, m, :], in0=t1, in1=t2, op=ALU.add))
        dv(lambda: V.tensor_copy(out=al_re[:, :], in_=Pw[:, 0, 8, :]))
        dv(lambda: V.tensor_copy(out=al_im[:, :], in_=Pw[:, 1, 8, :]))

        def bc(ap32):
            return ap32.unsqueeze(2).to_broadcast([128, 32, 16])

        T1 = sb(st, "T1", [128, 32, 16]); T2 = sb(st, "T2", [128, 32, 16])
        Bb = sb(st, "Bb", [128, 2, 32, 16])
        dv(lambda: V.tensor_tensor(out=T1[:], in0=BA[:, 0], in1=bc(f_re), op=ALU.mult))
        dv(lambda: V.tensor_tensor(out=T2[:], in0=BA[:, 1], in1=bc(f_im), op=ALU.mult))
        dv(lambda: V.tensor_tensor(out=Bb[:, 0], in0=T1[:], in1=T2[:], op=ALU.subtract))
        dv(lambda: V.tensor_tensor(out=T1[:], in0=BA[:, 1], in1=bc(f_re), op=ALU.mult))
        dv(lambda: V.tensor_tensor(out=T2[:], in0=BA[:, 0], in1=bc(f_im), op=ALU.mult))
        dv(lambda: V.tensor_tensor(out=Bb[:, 1], in0=T1[:], in1=T2[:], op=ALU.add))
        CPr = sb(st, "CPr", [128, 2, 32, 16, 16], BF16)
        Bz = sb(st, "Bz", [128, 2, 32, 15, 16], BF16)
        W1A = sb(st, "W1A", [128, 2, 32, 8, 16], BF16)
        S.op("pool", lambda: nc.gpsimd.memset(CPr[:].rearrange("p a b c d -> p (a b c d)"), 0.0), w=[K])
        S.op("pool", lambda: nc.gpsimd.memset(Bz[:].rearrange("p a b c d -> p (a b c d)"), 0.0), w=[K])
        for m in range(9):
            s_ = 8 - m
            dv(lambda: V.tensor_tensor(out=T1[:], in0=CA[:, 0], in1=bc(Pw[:, 0, m, :]), op=ALU.mult))
            dv(lambda: V.tensor_tensor(out=T2[:], in0=CA[:, 1], in1=bc(Pw[:, 1, m, :]), op=ALU.mult))
            dv(lambda: V.tensor_tensor(out=CPr[:, 0, :, s_, :], in0=T1[:], in1=T2[:], op=ALU.subtract))
            dv(lambda: V.tensor_tensor(out=T1[:], in0=CA[:, 0], in1=bc(Pw[:, 1, m, :]), op=ALU.mult))
            dv(lambda: V.tensor_tensor(out=T2[:], in0=CA[:, 1], in1=bc(Pw[:, 0, m, :]), op=ALU.mult))
            dv(lambda: V.scalar_tensor_tensor(out=CPr[:, 1, :, s_, :], in0=T1[:], scalar=-1.0, in1=T2[:],
                                              op0=ALU.mult, op1=ALU.subtract))
        for ri in range(2):
            dv(lambda: V.tensor_copy(out=Bz[:, ri, :, 7, :], in_=Bb[:, ri]))
        for k_ in range(8):
            dv(lambda: V.tensor_tensor(out=T1[:], in0=Bb[:, 0], in1=bc(Pw[:, 0, k_, :]), op=ALU.mult))
            dv(lambda: V.tensor_tensor(out=T2[:], in0=Bb[:, 1], in1=bc(Pw[:, 1, k_, :]), op=ALU.mult))
            dv(lambda: V.tensor_tensor(out=W1A[:, 0, :, k_, :], in0=T1[:], in1=T2[:], op=ALU.subtract))
            dv(lambda: V.tensor_tensor(out=T1[:], in0=Bb[:, 0], in1=bc(Pw[:, 1, k_, :]), op=ALU.mult))
            dv(lambda: V.tensor_tensor(out=T2[:], in0=Bb[:, 1], in1=bc(Pw[:, 0, k_, :]), op=ALU.mult))
            dv(lambda: V.tensor_tensor(out=W1A[:, 1, :, k_, :], in0=T1[:], in1=T2[:], op=ALU.add))
        for ri in range(2):
            S.dma("sp", lambda: nc.sync.dma_start(
                out=st_w2[ri].rearrange("p (r s h) -> p r s h", r=32, s=8), in_=CPr[:, ri, :, 0:8, :]), r=[K], w=["st_w2"])
        TS = [sb(st, "TS", [128, 4, 128], BF16) for _ in range(2)]
        toepv = st_toep.rearrange("p (r two c) -> p r two c", two=2, c=128)
        for gq in range(16):
            gl, q4 = gq // 8, gq % 8
            b = gq % 4
            ts_ = TS[gq % 2]
            tk = ("TS", gq % 2)
            R = slice(gl * 64, gl * 64 + 64)
            for gg in range(4):
                pr = q4 * 4 + gg
                n_ = 0
                for v in range(8):
                    for ri in range(2):
                        S.op("pe", lambda: nc.tensor.matmul(
                            ps[b][:, gg * 128:(gg + 1) * 128],
                            lhsT=Bz[R, ri, pr, 7 - v:15 - v, :].rearrange("p b h -> p (b h)"),
                            rhs=CPr[R, ri, pr, 8 - v:16 - v, :].rearrange("p s h -> p (s h)"),
                            start=(n_ == 0), stop=(n_ == 15)),
                            r=[K], w=[("ps", b)], sig=(n_ == 15 and gg == 3))
                        n_ += 1
            for gg in range(4):
                g = 2 * (q4 * 4 + gg) + gl
                S.op("dve", lambda: V.scalar_tensor_tensor(
                    out=ts_[:, gg, :], in0=ident[:, :], scalar=Dcol[:, g:g + 1], in1=ps[b][:, gg * 128:(gg + 1) * 128],
                    op0=ALU.mult, op1=ALU.add), r=[("ps", b), K, "ident"], w=[tk])
            S.dma("sp", lambda: nc.sync.dma_start(out=toepv[:, q4 * 4:(q4 + 1) * 4, gl, :], in_=ts_[:, :, :]),
                  r=[tk], w=["st_toep"])
        WS = [[sb(st, "WS", [128, 512], BF16) for _ in range(2)] for _ in range(2)]
        for gq in range(8):
            for ri in range(2):
                b = 4 + ri
                ws = WS[gq % 2][ri]
                wk = ("WS", gq % 2, ri)
                for gg in range(8):
                    g = gq * 8 + gg
                    pr, gl = g // 2, g % 2
                    S.op("pe", lambda: nc.tensor.matmul(
                        ps[b][:, gg * 64:(gg + 1) * 64], lhsT=W1A[:, ri, pr, :, :].rearrange("p k h -> p (k h)"),
                        rhs=identb[:, gl * 64:gl * 64 + 64], start=True, stop=True),
                        r=[K, "identb"], w=[("ps", b)], sig=(gg == 7))
                self.evac(ri, ws[:, :], ps[b][:, :], [("ps", b)], [wk])
                S.dma("sp", lambda: nc.sync.dma_start(out=st_w1[ri][:, gq * 512:(gq + 1) * 512], in_=ws[:, :]),
                      r=[wk], w=["st_w1"])


class S5Main:
    def __init__(self, nc, S, sb, ps, misc_bank, evac, env, X, us, mixedS):
        self.nc, self.S, self.sb, self.ps, self.misc_bank, self.evac, self.e = nc, S, sb, ps, misc_bank, evac, env
        self.X, self.us, self.mixedS = X, us, mixedS

    def run(self, mp):
        nc, S, sb, ps, e = self.nc, self.S, self.sb, self.ps, self.e
        misc_bank, evac = self.misc_bank, self.evac
        X, us, mixedS = self.X, self.us, self.mixedS
        V = nc.vector
        ident, identb, flg, glubT = e["ident"], e["identb"], e["flg"], e["glubT"]
        a_re, a_im, al_re, al_im = e["a_re"], e["a_im"], e["al_re"], e["al_im"]
        st_w1, st_toep, st_w2 = e["st_w1"], e["st_toep"], e["st_w2"]
        ags_in, ags_out = e["ags_in"], e["ags_out"]
        with contextlib.ExitStack() as s1:
            ZbB = sb(s1, "ZbB", [128, 2, 32, 128], BF16)
            GWb = sb(s1, "GWb", [128, 8, 128], BF16)
            with contextlib.ExitStack() as sg:
                GWf = sb(sg, "GWf", [128, 8, 128])
                S.op("pool", lambda: nc.gpsimd.memset(GWf[:].rearrange("p a b -> p (a b)"), 0.0), w=["GWf"])
                gwv = e["gluw"].rearrange("(c g) h k -> g h c k", g=8)
                for g8 in range(8):
                    S.dma("sp", lambda: nc.sync.dma_start(out=GWf[g8 * 16:(g8 + 1) * 16, :, g8 * 16:(g8 + 1) * 16], in_=gwv[g8]),
                          r=["GWf"], w=["GWf"])
                S.op("dve", lambda: V.tensor_copy(out=GWb[:], in_=GWf[:]), r=["GWf"], w=["GWb"])
                S.barrier()
            with contextlib.ExitStack() as sc:
                ZZ = sb(sc, "ZZ", [128, 129, 2, 32])
                w1r = [sb(sc, "w1r", [128, 2, 8, 64], BF16) for _ in range(2)]
                TB = sb(sc, "TB", [128, 32, 2, 32])
                tq = sb(sc, "tq", [128, 32, 32])
                al2 = sb(sc, "al2", [128, 7, 2, 32])
                AA = sb(sc, "AA", [128, 2, 32]); AB = sb(sc, "AB", [128, 2, 32])
                Tt = sb(sc, "Tt", [128, 2, 32]); Ut = sb(sc, "Ut", [128, 2, 32])
                Rin = sb(sc, "Rin", [128, 2, 32])
                TA = ZbB[:].rearrange("p a b c -> p (a b c)").bitcast(F32).rearrange("p (i r q) -> p i r q", i=64, r=2)

                def w1load(gb):
                    for ri in range(2):
                        S.dma("sp", lambda: nc.sync.dma_start(
                            out=w1r[gb % 2][:, ri, :, :],
                            in_=st_w1[ri][:, gb * 512:(gb + 1) * 512].rearrange("p (g c) -> p g c", c=64)),
                            r=["st_w1"], w=[("w1r", gb % 2)])
                w1load(0)
                for gb in range(8):
                    if gb + 1 < 8:
                        w1load(gb + 1)
                    sl = gb % 2
                    for ri in range(2):
                        b = ri * 2 + gb % 2
                        for gg in range(8):
                            g = gb * 8 + gg
                            pr, gl = g // 2, g % 2
                            pl = pr % 4
                            S.op("pe", lambda: nc.tensor.matmul(
                                ps[b][gl * 64:(gl + 1) * 64, pl * 128:(pl + 1) * 128], lhsT=w1r[sl][:, ri, gg, :],
                                rhs=X[:, g, :], start=True, stop=True),
                                r=[("w1r", sl), "X"], w=[("ps", b)], sig=(gg == 7))
                        evac(ri, ZZ[:, 1:129, ri, gb * 4:(gb + 1) * 4], ps[b][:, :].rearrange("p (a c) -> p c a", a=4),
                             [("ps", b)], ["ZZ"])
                KZ = "ZZ"

                def dv(fn, r=(), w=()):
                    S.op("dve", fn, r=[KZ] + list(r), w=[KZ] + list(w))
                dv(lambda: V.tensor_copy(out=al2[:, 0, 0, :], in_=al_re[:, :]))
                dv(lambda: V.tensor_copy(out=al2[:, 0, 1, :], in_=al_im[:, :]))
                for l in range(1, 7):
                    pr_, pi_ = al2[:, l - 1, 0, :], al2[:, l - 1, 1, :]
                    dv(lambda: V.tensor_tensor(out=Tt[:, 0, :], in0=pr_, in1=pr_, op=ALU.mult))
                    dv(lambda: V.tensor_tensor(out=Tt[:, 1, :], in0=pi_, in1=pi_, op=ALU.mult))
                    dv(lambda: V.tensor_tensor(out=al2[:, l, 0, :], in0=Tt[:, 0, :], in1=Tt[:, 1, :], op=ALU.subtract))
                    dv(lambda: V.tensor_tensor(out=Tt[:, 0, :], in0=pr_, in1=pi_, op=ALU.mult))
                    dv(lambda: V.tensor_scalar(out=al2[:, l, 1, :], in0=Tt[:, 0, :], scalar1=2.0, scalar2=None, op0=ALU.mult))
                dv(lambda: V.tensor_copy(out=AA[:, 0, :], in_=al_re[:, :]))
                dv(lambda: V.tensor_copy(out=AA[:, 1, :], in_=al_re[:, :]))
                dv(lambda: V.tensor_copy(out=AB[:, 0, :], in_=al_im[:, :]))
                dv(lambda: V.tensor_copy(out=AB[:, 1, :], in_=al_im[:, :]))

                def level(src, n_in, dst, l, d0=0):
                    n = n_in // 2
                    sv = src.rearrange("p (i two) r q -> p i two r q", two=2)
                    Er, Ei, Or, Oi = sv[:, :, 0, 0, :], sv[:, :, 0, 1, :], sv[:, :, 1, 0, :], sv[:, :, 1, 1, :]
                    Dr, Di = dst[:, d0:d0 + n, 0, :], dst[:, d0:d0 + n, 1, :]
                    ar = al2[:, l, 0, :].unsqueeze(1).to_broadcast([128, n, 32])
                    ai = al2[:, l, 1, :].unsqueeze(1).to_broadcast([128, n, 32])
                    tt = tq[:, 0:n, :]
                    dv(lambda: V.tensor_tensor(out=Dr, in0=Er, in1=ar, op=ALU.mult), r=["ZbB"], w=["ZbB"])
                    dv(lambda: V.tensor_tensor(out=tt, in0=Ei, in1=ai, op=ALU.mult), r=["ZbB"], w=["ZbB"])
                    dv(lambda: V.tensor_tensor(out=Dr, in0=Dr, in1=tt, op=ALU.subtract), r=["ZbB"], w=["ZbB"])
                    dv(lambda: V.tensor_tensor(out=Dr, in0=Dr, in1=Or, op=ALU.add), r=["ZbB"], w=["ZbB"])
                    dv(lambda: V.tensor_tensor(out=Di, in0=Er, in1=ai, op=ALU.mult), r=["ZbB"], w=["ZbB"])
                    dv(lambda: V.tensor_tensor(out=tt, in0=Ei, in1=ar, op=ALU.mult), r=["ZbB"], w=["ZbB"])
                    dv(lambda: V.tensor_tensor(out=Di, in0=Di, in1=tt, op=ALU.add), r=["ZbB"], w=["ZbB"])
                    dv(lambda: V.tensor_tensor(out=Di, in0=Di, in1=Oi, op=ALU.add), r=["ZbB"], w=["ZbB"])
                level(ZZ[:, 1:65, :, :], 64, TA, 0, 0)
                level(ZZ[:, 65:129, :, :], 64, TA, 0, 32)
                level(TA[:, 0:64], 64, TB, 1)
                level(TB[:, 0:32], 32, TA, 2)
                level(TA[:, 0:16], 16, TB, 3)
                level(TB[:, 0:8], 8, TA, 4)
                level(TA[:, 0:4], 4, TB, 5)
                level(TB[:, 0:2], 2, TA, 6)
                S.dma("sp", lambda: nc.sync.dma_start(out=ags_in, in_=TA[:, 0, :, :].rearrange("p r q -> p (r q)")),
                      r=[KZ, "ZbB"], w=["ags_in"])
                S.cc(lambda: nc.gpsimd.collective_compute("AllGather", ALU.bypass, replica_groups=e["PAIRS"],
                                                          ins=[ags_in.opt()], outs=[ags_out.opt()]),
                     r=["ags_in"], w=["ags_out"])
                S.dma("sp", lambda: nc.sync.dma_start(out=Rin[:, :, :].rearrange("p r q -> p (r q)"), in_=ags_out[0:128, :]),
                      r=["ags_out"], w=["Rin"])
                dv(lambda: V.tensor_scalar(out=ZZ[:, 0, :, :], in0=Rin[:, :, :], scalar1=flg[:, 0:1], scalar2=None,
                                           op0=ALU.mult), r=["Rin", "flg"])
                q4 = sb(sc, "q4", [128, 4, 32])

                def dvn(fn, first=False):
                    S.op("dve", fn, r=[KZ], w=[KZ], skip_same=not first)
                ar_, ai_ = AA[:, 0, :], AB[:, 0, :]
                for c in range(1, 129):
                    zr, zi = ZZ[:, c - 1, 0, :], ZZ[:, c - 1, 1, :]
                    dvn(lambda: V.tensor_tensor(out=q4[:, 0, :], in0=zr, in1=ar_, op=ALU.mult), first=(c == 1))
                    dvn(lambda: V.tensor_tensor(out=q4[:, 1, :], in0=zi, in1=ai_, op=ALU.mult))
                    dvn(lambda: V.tensor_tensor(out=q4[:, 2, :], in0=zr, in1=ai_, op=ALU.mult))
                    dvn(lambda: V.tensor_tensor(out=q4[:, 3, :], in0=zi, in1=ar_, op=ALU.mult))
                    dvn(lambda: V.tensor_tensor(out=ZZ[:, c, 0, :], in0=ZZ[:, c, 0, :], in1=q4[:, 0, :], op=ALU.add))
                    dvn(lambda: V.tensor_tensor(out=ZZ[:, c, 1, :], in0=ZZ[:, c, 1, :], in1=q4[:, 2, :], op=ALU.add))
                    dvn(lambda: V.tensor_tensor(out=ZZ[:, c, 0, :], in0=ZZ[:, c, 0, :], in1=q4[:, 1, :], op=ALU.subtract))
                    dvn(lambda: V.tensor_tensor(out=ZZ[:, c, 1, :], in0=ZZ[:, c, 1, :], in1=q4[:, 3, :], op=ALU.add))
                dv(lambda: V.tensor_copy(out=Tt[:], in_=ZZ[:, 128, :, :]))
                S.dma("sp", lambda: nc.sync.dma_start(out=e["nssm_p"], in_=ZZ[:, 128, :, :]), r=[KZ])
                S.op("dve", lambda: V.tensor_copy(out=ZbB[:, 0, :, :], in_=ZZ[:, 0:128, 0, :].rearrange("p c q -> p q c")),
                     r=[KZ, "ZbB"], w=["ZbB"])
                S.op("act", lambda: nc.scalar.copy(out=ZbB[:, 1, :, :], in_=ZZ[:, 0:128, 1, :].rearrange("p c q -> p q c")),
                     r=[KZ, "ZbB"], w=["ZbB"])
                S.barrier()
            if S5STAGE[0] < 2:
                S.op("dve", lambda: V.memset(mixedS[:, :, :], 0.0), w=["mixedS"])
                S.barrier()
                return
            with contextlib.ExitStack() as ss:
                w1s = sb(ss, "w1s", [16, 2, 64, 64], BF16)
                toeps = sb(ss, "toeps", [128, 64, 16], BF16)
                w2s = sb(ss, "w2s", [128, 2, 64, 16], BF16)
                SN = sb(ss, "SN", [128, 2, 32, NS])
                q1 = sb(ss, "q1", [128, 32, NS]); q2 = sb(ss, "q2", [128, 32, NS])
                gys = sb(ss, "gys", [16, 1024], BF16)
                S0A = sb(ss, "S0A", [128, 2, 32, NS])
                S0b = sb(ss, "S0b", [128, 2, 32, NS], BF16)
                usTb = sb(ss, "usTb", [128, 64, NS], BF16)
                S.dma("sp", lambda: nc.sync.dma_start(out=S0A[:, :, :, :], in_=e["s0A_d"]), w=["S0A"])
                S.op("dve", lambda: V.tensor_copy(out=S0b[:], in_=S0A[:]), r=["S0A"], w=["S0b"])
                S.op("pool", lambda: nc.gpsimd.memset(usTb[:].rearrange("p a b -> p (a b)"), 0.0), w=["usTb"])
                for hb in range(2):
                    b = misc_bank()
                    for gg in range(32):
                        g = hb * 32 + gg
                        S.op("pe", lambda: nc.tensor.transpose(
                            out=ps[b][0:16, gg * 16:(gg + 1) * 16], in_=us[:NS, g * 16:(g + 1) * 16], identity=ident[:NS, :NS]),
                            r=["us", "ident"], w=[("ps", b)], sig=(gg == 31))
                    S.op("dve", lambda: V.tensor_copy(out=usTb[0:16, hb * 32:(hb + 1) * 32, :],
                                                      in_=ps[b][0:16, :].rearrange("p (g t) -> p g t", t=NS)),
                         r=[("ps", b), "usTb"], w=["usTb"])
                S.op("pool", lambda: nc.gpsimd.memset(toeps[:].rearrange("p a b -> p (a b)"), 0.0), w=["toeps"])
                S.op("pool", lambda: nc.gpsimd.memset(w2s[:].rearrange("p a b c -> p (a b c)"), 0.0), w=["w2s"])
                for ri in range(2):
                    S.dma("sp", lambda: nc.sync.dma_start(
                        out=w1s[:, ri, :, :], in_=st_w1[ri][0:16, :].rearrange("p (g c) -> p g c", c=64)), r=["st_w1"], w=["w1s"])
                    for gl in range(2):
                        S.dma("sp", lambda: nc.sync.dma_start(
                            out=w2s[gl * 64:(gl + 1) * 64, ri, :, :].rearrange("p (r two) h -> p r two h", two=2)[:, :, gl, :],
                            in_=st_w2[ri][gl * 64:(gl + 1) * 64, :].rearrange("p (r c) -> p r c", c=128)[:, :, 112:128]),
                            r=["st_w2", "w2s"], w=["w2s"])
                S.dma("sp", lambda: nc.sync.dma_start(
                    out=toeps[0:16, :, :], in_=st_toep[0:16, :].rearrange("p (g c) -> p g c", c=128)[:, :, 0:16]),
                    r=["st_toep", "toeps"], w=["toeps"])

                def bt(ap32):
                    return ap32.unsqueeze(2).to_broadcast([128, 32, NS])
                for ri in range(2):
                    for g in range(64):
                        pr, gl = g // 2, g % 2
                        S.op("pe", lambda: nc.tensor.matmul(
                            ps[ri][gl * 64:(gl + 1) * 64, pr * 16:(pr + 1) * 16], lhsT=w1s[0:16, ri, g, :], rhs=usTb[0:16, g, :],
                            start=True, stop=True), r=["w1s", "usTb"], w=[("ps", ri)], sig=(g == 63))
                KS = "SN"

                def dv2(fn, r=(), w=()):
                    S.op("dve", fn, r=[KS] + list(r), w=[KS] + list(w))
                dv2(lambda: V.tensor_tensor(out=q1[:], in0=S0A[:, 0], in1=bt(a_re[:, :]), op=ALU.mult), r=["S0A"])
                dv2(lambda: V.tensor_tensor(out=q2[:], in0=S0A[:, 1], in1=bt(a_im[:, :]), op=ALU.mult))
                dv2(lambda: V.tensor_tensor(out=SN[:, 0], in0=q1[:], in1=q2[:], op=ALU.subtract))
                dv2(lambda: V.tensor_tensor(out=SN[:, 0], in0=SN[:, 0], in1=ps[0][:, :].rearrange("p (r t) -> p r t", t=NS),
                                            op=ALU.add), r=[("ps", 0)])
                dv2(lambda: V.tensor_tensor(out=q1[:], in0=S0A[:, 0], in1=bt(a_im[:, :]), op=ALU.mult))
                dv2(lambda: V.tensor_tensor(out=q2[:], in0=S0A[:, 1], in1=bt(a_re[:, :]), op=ALU.mult))
                dv2(lambda: V.tensor_tensor(out=SN[:, 1], in0=q1[:], in1=q2[:], op=ALU.add))
                dv2(lambda: V.tensor_tensor(out=SN[:, 1], in0=SN[:, 1], in1=ps[1][:, :].rearrange("p (r t) -> p r t", t=NS),
                                            op=ALU.add), r=[("ps", 1)])
                S.dma("sp", lambda: nc.sync.dma_start(out=e["nssm_s"], in_=SN[:, :, :, :]), r=[KS])
                for g in range(64):
                    pr, gl = g // 2, g % 2
                    b = 2 + g // 32
                    o = ps[b][0:16, (g % 32) * 16:(g % 32 + 1) * 16]
                    S.op("pe", lambda: nc.tensor.matmul(o, lhsT=usTb[:, g, :], rhs=toeps[:, g, :], start=True, stop=False),
                         r=["usTb", "toeps"], w=[("ps", b)], sig=False)
                    S.op("pe", lambda: nc.tensor.matmul(o, lhsT=S0b[:, 0, pr, :], rhs=w2s[:, 0, g, :], start=False, stop=False),
                         r=["S0b", "w2s"], w=[("ps", b)], sig=False)
                    S.op("pe", lambda: nc.tensor.matmul(o, lhsT=S0b[:, 1, pr, :], rhs=w2s[:, 1, g, :], start=False, stop=True),
                         r=["S0b", "w2s"], w=[("ps", b)], sig=(g % 32 == 31))
                for hb in range(2):
                    S.op("act", lambda: nc.scalar.activation(out=gys[:, hb * 512:(hb + 1) * 512], in_=ps[2 + hb][0:16, :],
                                                             func=ACT.Gelu_apprx_tanh), r=[("ps", 2 + hb)], w=["gys"])
                b = misc_bank()
                for ch in range(8):
                    S.op("pe", lambda: nc.tensor.matmul(ps[b][:, ch * 16:(ch + 1) * 16], lhsT=gys[0:16, ch * 128:(ch + 1) * 128],
                                                        rhs=identb[0:16, 0:16], start=True, stop=True),
                         r=["gys", "identb"], w=[("ps", b)], sig=(ch == 7))
                S.op("dve", lambda: V.tensor_copy(out=mixedS[:, :, NPT:NT], in_=ps[b][:, 0:128].rearrange("p (c t) -> p c t", t=NS)),
                     r=[("ps", b)], w=["mixedS"])
                S.barrier()
            if S5STAGE[0] < 3:
                S.op("dve", lambda: V.memset(mixedS[:, :, 0:NPT], 0.0), w=["mixedS"])
                S.barrier()
                return
            with contextlib.ExitStack() as so:
                toepr = [sb(so, "toepr", [128, 8, 128], BF16) for _ in range(2)]
                w2r = [sb(so, "w2r", [128, 2, 8, 128], BF16) for _ in range(2)]
                gy8 = [sb(so, "gy8", [128, 8, 128], BF16) for _ in range(2)]
                sgt = [sb(so, "sgt", [128, 512], BF16) for _ in range(2)]
                for t_ in w2r:
                    S.op("dve", lambda: V.memset(t_[:].rearrange("p a b c -> p (a b c)"), 0.0), w=["w2init"])
                S.barrier()

                def oload(ch8):
                    sl = ch8 % 2
                    S.dma("sp", lambda: nc.sync.dma_start(
                        out=toepr[sl][:, :, :], in_=st_toep[:, ch8 * 1024:(ch8 + 1) * 1024].rearrange("p (g c) -> p g c", c=128)),
                        r=["st_toep"], w=[("toepr", sl)])
                    for ri in range(2):
                        for gl in range(2):
                            S.dma("sp", lambda: nc.sync.dma_start(
                                out=w2r[sl][gl * 64:(gl + 1) * 64, ri, :, :].rearrange("p (r two) c -> p r two c", two=2)[:, :, gl, :],
                                in_=st_w2[ri][gl * 64:(gl + 1) * 64, ch8 * 512:(ch8 + 1) * 512].rearrange("p (r c) -> p r c", c=128)),
                                r=["st_w2"], w=[("w2r", sl)])
                oload(0)
                si = 0
                for ch8 in range(8):
                    if ch8 + 1 < 8:
                        oload(ch8 + 1)
                    sl = ch8 % 2
                    gy = gy8[ch8 % 2]
                    gk = ("gy8", ch8 % 2)
                    for half in range(2):
                        b = half
                        for gg in range(4):
                            g8 = half * 4 + gg
                            g = ch8 * 8 + g8
                            pr = g // 2
                            o = ps[b][:, gg * 128:(gg + 1) * 128]
                            S.op("pe", lambda: nc.tensor.matmul(o, lhsT=X[:, g, :], rhs=toepr[sl][:, g8, :], start=True, stop=False),
                                 r=["X", ("toepr", sl)], w=[("ps", b)], sig=False)
                            S.op("pe", lambda: nc.tensor.matmul(o, lhsT=ZbB[:, 0, pr, :], rhs=w2r[sl][:, 0, g8, :], start=False, stop=False),
                                 r=["ZbB", ("w2r", sl)], w=[("ps", b)], sig=False)
                            S.op("pe", lambda: nc.tensor.matmul(o, lhsT=ZbB[:, 1, pr, :], rhs=w2r[sl][:, 1, g8, :], start=False, stop=True),
                                 r=["ZbB", ("w2r", sl)], w=[("ps", b)], sig=(gg == 3))
                        S.op("act", lambda: nc.scalar.activation(
                            out=gy[:, :, half * 64:(half + 1) * 64].rearrange("p j (g h) -> p g j h", g=4),
                            in_=ps[b][:, :].rearrange("p (g j h) -> p g j h", g=4, j=8), func=ACT.Gelu_apprx_tanh),
                            r=[("ps", b)], w=[gk])
                    for hf in range(2):
                        b = 2 + hf
                        for i in range(4):
                            jj = (3 - i) if hf == 1 else (7 - i)
                            S.op("pe", lambda: nc.tensor.matmul(ps[b][:, i * 128:(i + 1) * 128], lhsT=gy[:, jj, :], rhs=identb[:, :],
                                                                start=True, stop=True),
                                 r=[gk, "identb"], w=[("ps", b)], sig=(i == 3))
                        c0 = 512 if hf == 1 else 0
                        evac(hf, mixedS[:, ch8, c0:c0 + 512], ps[b][:, :], [("ps", b)], ["mixedS"])
                    for ti, (c0, n) in enumerate(TILES):
                        b = 4 + si % 2
                        sg = sgt[si % 2]
                        sk = ("sgt", si % 2)
                        si += 1
                        S.op("pe", lambda: nc.tensor.matmul(ps[b][:, 0:n], lhsT=GWb[:, ch8, :], rhs=mixedS[:, ch8, c0:c0 + n],
                                                            start=True, stop=True), r=["GWb", "mixedS"], w=[("ps", b)])
                        S.op("act", lambda: nc.scalar.activation(out=sg[:, 0:n], in_=ps[b][:, 0:n], func=ACT.Sigmoid,
                                                                 bias=glubT[:, ch8:ch8 + 1], scale=1.0),
                             r=[("ps", b), "glubT"], w=[sk])
                        S.op("dve", lambda: V.tensor_tensor(out=mixedS[:, ch8, c0:c0 + n], in0=mixedS[:, ch8, c0:c0 + n],
                                                            in1=sg[:, 0:n], op=ALU.mult), r=[sk, "mixedS"], w=["mixedS"])
                    if S5STAGE[0] == 4:
                        S.barrier()
                S.barrier()


_CACHE = {}


def _get_program(with_s5=True):
    if with_s5 not in _CACHE:
        _CACHE[with_s5] = build_program(with_s5=with_s5)
    return _CACHE[with_s5]


def make_in_maps(inp):
    f = lambda a: np.ascontiguousarray(a, dtype=np.float32)
    norms = np.concatenate([inp["ffn1_norm"][0], inp["mix_norm"][0], inp["ffn2_norm"][0],
                            inp["final_norm"]]).reshape(4 * KC, 128)
    ada_w = inp["ada_w"][0]
    def layA(x):
        sh = x.shape[2:]
        x = x.reshape((32, 2, 64) + sh)
        x = np.moveaxis(x, 0, 2)
        return x.reshape((128, 32) + sh)
    lamA = np.stack([layA(inp["ssm_lambda_re"][0]), layA(inp["ssm_lambda_im"][0]),
                     layA(np.repeat(inp["ssm_log_dt"][0][:, None], 64, axis=1))], axis=1)
    BA = np.stack([layA(inp["ssm_b_re"][0]), layA(inp["ssm_b_im"][0])], axis=1)
    CA = np.stack([layA(np.swapaxes(inp["ssm_c_re"][0], 1, 2)), layA(np.swapaxes(inp["ssm_c_im"][0], 1, 2))], axis=1)
    Dcol = np.tile(inp["ssm_d"][0].T, (8, 1))
    shared = {
        "call": f(np.concatenate([inp["c_sample"], inp["c_prompt"]], axis=0)),
        "ada_b": f(inp["ada_b"][0].reshape(9 * D // 128, 128)),
        "norms": f(norms),
        "w_g0": f(inp["ffn1_w_gate"][0]), "w_u0": f(inp["ffn1_w_up"][0]), "w_d0": f(inp["ffn1_w_down"][0]),
        "w_g1": f(inp["ffn2_w_gate"][0]), "w_u1": f(inp["ffn2_w_up"][0]), "w_d1": f(inp["ffn2_w_down"][0]),
        "w_in": f(inp["w_in"][0]), "w_out": f(inp["w_out"][0]),
        "pool_w": f(inp["pool_w"][0]),
        "pvec": f(np.concatenate([inp["pool_b"][0].reshape(8, 128), inp["pool_scale"][0].reshape(8, 128)], axis=0)),
        "lamA": f(lamA), "BA": f(BA), "CA": f(CA), "Dcol": f(Dcol),
        "gluw": f(inp["ssm_glu_w"][0]),
        "glub": f(inp["ssm_glu_b"][0].reshape(8, 128)),
    }
    maps = []
    for c in range(NCORES):
        b, h = c // 2, c % 2
        m = dict(shared)
        m["xp"] = f(inp["x_prompt"][b, h * NPT:(h + 1) * NPT])
        m["xs"] = f(inp["x_sample"][c * NS:(c + 1) * NS, 0])
        m["adaA"] = f(ada_w[:, c * 512:(c + 1) * 512])
        m["adaB"] = f(ada_w[:, 4096 + c * 1792:4096 + (c + 1) * 1792])
        s1 = np.zeros((128, 17), np.float32)
        for s in range(NS):
            s1[c * NS + s, 1 + s] = 1.0
        s2 = np.zeros((4, 17), np.float32)
        s2[b, 0] = 1.0
        m["sel1"] = s1
        m["sel2"] = s2
        fl = np.zeros((128, 2), np.float32)
        fl[:, 0] = float(h)
        fl[:, 1] = 1.0 - float(h)
        m["flags"] = fl
        m["spool"] = f(inp["state_pool"][0, c * NS:(c + 1) * NS])
        sre = np.moveaxis(inp["state_ssm_re"][0, c * NS:(c + 1) * NS], 0, 2)
        sim = np.moveaxis(inp["state_ssm_im"][0, c * NS:(c + 1) * NS], 0, 2)
        m["s0A"] = f(np.stack([layA(sre), layA(sim)], axis=1))
        maps.append(m)
    return maps


def assemble(R):
    y_prompt = np.zeros((4, 2048, D), np.float32)
    y_sample = np.zeros((128, 1, D), np.float32)
    re_p = np.zeros((1, 4, 64, 64), np.float32)
    im_p = np.zeros((1, 4, 64, 64), np.float32)
    pool_p = np.zeros((1, 4, 15, 1024), np.float32)
    re_s = np.zeros((1, 128, 64, 64), np.float32)
    im_s = np.zeros((1, 128, 64, 64), np.float32)
    pool_s = np.zeros((1, 128, 15, 1024), np.float32)
    for c in range(NCORES):
        b, h = c // 2, c % 2
        y_prompt[b, h * NPT:(h + 1) * NPT] = R[c]["yp"]
        y_sample[c * NS:(c + 1) * NS, 0] = R[c]["ys"]
        def unA(x):
            sh = x.shape[2:]
            x = x.reshape((2, 64, 32) + sh)
            x = np.moveaxis(x, 2, 0)
            return x.reshape((64, 64) + sh)
        if h == 1:
            re_p[0, b] = unA(R[c]["nssm_p"][:, 0])
            im_p[0, b] = unA(R[c]["nssm_p"][:, 1])
            pool_p[0, b] = R[c]["npool_p"][1:16]
        re_s[0, c * NS:(c + 1) * NS] = np.moveaxis(unA(R[c]["nssm_s"][:, 0]), 2, 0)
        im_s[0, c * NS:(c + 1) * NS] = np.moveaxis(unA(R[c]["nssm_s"][:, 1]), 2, 0)
        pool_s[0, c * NS:(c + 1) * NS] = R[c]["npool_s"]
    return (y_prompt, y_sample, re_p, im_p, pool_p, re_s, im_s, pool_s)


def kernel(**inputs):
    inp = {k: np.asarray(v) for k, v in inputs.items()}
    nc = _get_program()
    maps = make_in_maps(inp)
    res = run_bass_kernel_spmd(nc, maps, core_ids=list(range(NCORES)))
    return assemble(res.results)
```
